# Optimizing a Trainium2 kernel written in Bass

```python
import jax
import jax.numpy as jnp
from jax import lax
import numpy as np

D_MODEL = 1024
BATCH = 16
SEQ = 256
DEPTH = 2
DEC_BATCH = 8
DEC_SEQ = 1024
PAST_LEN = 256

GRID_W = 64
HEAD_DIM = 64
N_Q_HEADS = 8
N_KV_HEADS = 4
ATTN_WIDTH = N_Q_HEADS * HEAD_DIM
KV_WIDTH = N_KV_HEADS * HEAD_DIM
Q_BLOCK = 128
ROPE_THETA = 10000.0
M_HEADS = 4
M_DK = 64
M_DV = 64
M_WIDTH = M_HEADS * M_DV
M_CHUNK = 64
M_CONV = 3
F_GROUPS = 4
F_GROUP_CH = 64
F_WIDTH = F_GROUPS * F_GROUP_CH
FF_HIDDEN = -(-8 * D_MODEL // (3 * 256)) * 256
EPS = 1e-6
IN_SIZES = (ATTN_WIDTH, KV_WIDTH, KV_WIDTH, M_HEADS * M_DK, M_HEADS * M_DK, M_WIDTH, M_WIDTH, 4 * M_HEADS, F_WIDTH, 3 * D_MODEL)
IN_COLS = sum(IN_SIZES)

kernel_name = 'hybrid_mlstm_gqa_fourier_diffusion_step'


def _rmsnorm(x, g):
    xf = x.astype(jnp.float32)
    y = xf * lax.rsqrt(jnp.mean(xf * xf, axis=-1, keepdims=True) + EPS)
    return (y * g.astype(jnp.float32)).astype(x.dtype)


def _modulation(cvec, w_ada, b_ada):
    mod = jax.nn.silu(cvec) @ w_ada + b_ada
    return jnp.split(mod, 6, axis=-1)


def _rope_1d(x, pos):
    n = x.shape[-1] // 2
    inv = 1.0 / (ROPE_THETA ** (jnp.arange(n, dtype=jnp.float32) / n))
    ang = pos.astype(jnp.float32)[:, None] * inv[None, :]
    cos = jnp.cos(ang)[:, None, :]
    sin = jnp.sin(ang)[:, None, :]
    xf = x.astype(jnp.float32)
    x1, x2 = xf[..., :n], xf[..., n:]
    return jnp.concatenate([x1 * cos - x2 * sin, x2 * cos + x1 * sin], axis=-1).astype(x.dtype)


def _rope_2d(x, row, col):
    half = x.shape[-1] // 2
    return jnp.concatenate([_rope_1d(x[..., :half], row), _rope_1d(x[..., half:], col)], axis=-1)


def _attend(q, k, v):
    B, Tq, Hq, hd = q.shape
    G = Hq // N_KV_HEADS
    nb = Tq // Q_BLOCK
    qb = q.reshape(B, nb, Q_BLOCK, N_KV_HEADS, G, hd).transpose(1, 0, 2, 3, 4, 5)
    scale = hd ** -0.5

    def block(qi):
        s = jnp.einsum('bqhgd,bkhd->bhgqk', qi, k, preferred_element_type=jnp.float32) * scale
        p = jax.nn.softmax(s, axis=-1).astype(v.dtype)
        return jnp.einsum('bhgqk,bkhd->bqhgd', p, v)

    o = lax.map(block, qb)
    return o.transpose(1, 0, 2, 3, 4, 5).reshape(B, Tq, Hq * hd)


def _short_conv(x, w):
    T = x.shape[1]
    pad = M_CONV // 2
    xp = jnp.pad(x, ((0, 0), (pad, pad), (0, 0)))
    return sum(xp[:, j:j + T] * w[j] for j in range(M_CONV))


def _fourier(u):
    B, T, _ = u.shape
    ug = u.reshape(B, T, F_GROUPS, F_GROUP_CH).astype(jnp.float32)
    y = jnp.fft.fft2(ug, axes=(1, 3), norm='ortho').real
    return y.reshape(B, T, F_WIDTH).astype(u.dtype)


def _mlstm_scan(q, k, v, li, lf, C0, n0, m0):
    B, T, H, _ = q.shape
    nc = T // M_CHUNK
    tril = jnp.tril(jnp.ones((M_CHUNK, M_CHUNK), dtype=bool))

    def chunks(a):
        return a.reshape((B, nc, M_CHUNK) + a.shape[2:]).swapaxes(0, 1)

    def step(carry, inp):
        C, n, m = carry
        qc, kc, vc, lic, lfc = inp
        b = jnp.cumsum(lfc, axis=1).swapaxes(1, 2)
        ig = lic.swapaxes(1, 2)
        inter = b + m[..., None]
        dmat = b[..., :, None] - b[..., None, :] + ig[..., None, :]
        dmat = jnp.where(tril, dmat, -jnp.inf)
        mj = jnp.maximum(inter, dmat.max(axis=-1))
        w = jnp.exp(dmat - mj[..., None])
        wi = jnp.exp(inter - mj)
        s = jnp.einsum('blhd,bshd->bhls', qc, kc) * w
        num = jnp.einsum('bhls,bshe->bhle', s, vc) + wi[..., None] * jnp.einsum('blhd,bhde->bhle', qc, C)
        den = s.sum(axis=-1) + wi * jnp.einsum('blhd,bhd->bhl', qc, n)
        h = num / jnp.maximum(jnp.abs(den), jnp.exp(-mj))[..., None]
        bl = b[..., -1]
        g = bl[..., None] - b + ig
        m_new = jnp.maximum(bl + m, g.max(axis=-1))
        wk = jnp.exp(g - m_new[..., None])
        decay = jnp.exp(bl + m - m_new)
        C_new = decay[..., None, None] * C + jnp.einsum('bhs,bshd,bshe->bhde', wk, kc, vc)
        n_new = decay[..., None] * n + jnp.einsum('bhs,bshd->bhd', wk, kc)
        return (C_new, n_new, m_new), h.swapaxes(1, 2)

    (C, n, m), h = lax.scan(step, (C0, n0, m0), (chunks(q), chunks(k), chunks(v), chunks(li), chunks(lf)))
    h = h.swapaxes(0, 1).reshape(B, T, H, v.shape[-1])
    return h, (C, n, m)


def _mlstm_bidir(q, k, v, gates, st_fw, st_bw):
    li_f, lf_f, li_b, lf_b = jnp.split(gates, 4, axis=-1)
    lf_f = jax.nn.log_sigmoid(lf_f)
    lf_b = jax.nn.log_sigmoid(lf_b)
    h_f, s_f = _mlstm_scan(q, k, v, li_f, lf_f, *st_fw)
    fl = lambda a: jnp.flip(a, axis=1)
    h_b, s_b = _mlstm_scan(fl(q), fl(k), fl(v), fl(li_b), fl(lf_b), *st_bw)
    return h_f + fl(h_b), s_f, s_b


def _layer(x, mod, pos, ctx, norm1_g, w_in, q_norm_g, k_norm_g, conv_w, gate_b, m_norm_g,
           w_pa, w_pm, w_pf, w_out, norm2_g, w_ffn_in, w_ffn_out):
    shift1, scale1, gate1, shift2, scale2, gate2 = mod
    B, T, _ = x.shape
    f32 = jnp.float32
    h = _rmsnorm(x, norm1_g) * (1 + scale1) + shift1
    z = h @ w_in
    qa, ka, va, qm, km, vm, om, gm, fu, bg = jnp.split(z, np.cumsum(IN_SIZES)[:-1].tolist(), axis=-1)
    qa = _rmsnorm(qa.reshape(B, T, N_Q_HEADS, HEAD_DIM), q_norm_g)
    ka = _rmsnorm(ka.reshape(B, T, N_KV_HEADS, HEAD_DIM), k_norm_g)
    va = va.reshape(B, T, N_KV_HEADS, HEAD_DIM)
    if ctx is None:
        keys, vals = ka, va
        zero_state = (jnp.zeros((B, M_HEADS, M_DK, M_DV), f32), jnp.zeros((B, M_HEADS, M_DK), f32),
                      jnp.zeros((B, M_HEADS), f32))
        st_fw, st_bw = zero_state, zero_state
    else:
        row, col = pos
        qa = _rope_2d(qa, row, col)
        keys = jnp.concatenate([ctx[0].astype(ka.dtype), _rope_2d(ka, row, col)], axis=1)
        vals = jnp.concatenate([ctx[1].astype(va.dtype), va], axis=1)
        st_fw, st_bw = ctx[2], ctx[3]
    attn = _attend(qa, keys, vals)
    qk = jax.nn.silu(_short_conv(jnp.concatenate([qm, km], axis=-1), conv_w))
    qm, km = jnp.split(qk, 2, axis=-1)
    qm = qm.reshape(B, T, M_HEADS, M_DK).astype(f32)
    km = km.reshape(B, T, M_HEADS, M_DK).astype(f32) * (M_DK ** -0.5)
    vm = vm.reshape(B, T, M_HEADS, M_DV).astype(f32)
    gm = gm.astype(f32) + gate_b.astype(f32)
    hm, st_f, st_b = _mlstm_bidir(qm, km, vm, gm, st_fw, st_bw)
    hm = _rmsnorm(hm, m_norm_g.reshape(M_HEADS, M_DV)).reshape(B, T, M_WIDTH).astype(x.dtype)
    hm = jax.nn.sigmoid(om) * hm
    fo = _fourier(fu)
    g_a, g_m, g_f = jnp.split(jax.nn.sigmoid(bg), 3, axis=-1)
    merged = g_a * (attn @ w_pa) + g_m * (hm @ w_pm) + g_f * (fo @ w_pf)
    x = x + gate1 * (merged @ w_out)
    h2 = _rmsnorm(x, norm2_g) * (1 + scale2) + shift2
    fg, fv = jnp.split(h2 @ w_ffn_in, 2, axis=-1)
    x = x + gate2 * ((jax.nn.silu(fg) * fv) @ w_ffn_out)
    return x, ka, va, st_f, st_b


def setup_inputs(seed: int = 0) -> dict:
    key = jax.random.key(seed)
    ks = jax.random.split(key, 32)
    f32 = jnp.float32

    def nrm(k, shape, s):
        return s * jax.random.normal(k, shape, f32)

    D = D_MODEL
    f_bias = jnp.linspace(3.0, 6.0, M_HEADS, dtype=f32)
    zh = jnp.zeros((M_HEADS,), f32)
    gate_base = jnp.concatenate([zh, f_bias, zh, f_bias])
    return {
        'x_prompt': nrm(ks[0], (BATCH, SEQ, D), 1.0),
        'x_sample': nrm(ks[1], (DEC_BATCH, DEC_SEQ, D), 1.0),
        'cache_k': nrm(ks[2], (DEC_BATCH, DEPTH, PAST_LEN, N_KV_HEADS, HEAD_DIM), 1.0),
        'cache_v': nrm(ks[3], (DEC_BATCH, DEPTH, PAST_LEN, N_KV_HEADS, HEAD_DIM), 1.0),
        'state_C': nrm(ks[4], (DEC_BATCH, DEPTH, 2, M_HEADS, M_DK, M_DV), 0.1),
        'state_n': nrm(ks[5], (DEC_BATCH, DEPTH, 2, M_HEADS, M_DK), 0.5),
        'state_m': 1.0 + nrm(ks[6], (DEC_BATCH, DEPTH, 2, M_HEADS), 0.5),
        'c': nrm(ks[7], (DEC_BATCH, D), 1.0),
        'c_ctx': nrm(ks[8], (D,), 1.0),
        'w_ada': nrm(ks[9], (DEPTH, D, 6 * D), D ** -0.5),
        'b_ada': nrm(ks[10], (DEPTH, 6 * D), 0.01),
        'norm1_g': 1.0 + nrm(ks[11], (DEPTH, D), 0.02),
        'w_in': nrm(ks[12], (DEPTH, D, IN_COLS), D ** -0.5),
        'q_norm_g': 1.0 + nrm(ks[13], (DEPTH, HEAD_DIM), 0.02),
        'k_norm_g': 1.0 + nrm(ks[14], (DEPTH, HEAD_DIM), 0.02),
        'm_conv_w': nrm(ks[15], (DEPTH, M_CONV, 2 * M_HEADS * M_DK), M_CONV ** -0.5),
        'm_gate_b': gate_base + nrm(ks[16], (DEPTH, 4 * M_HEADS), 0.1),
        'm_norm_g': 1.0 + nrm(ks[17], (DEPTH, M_WIDTH), 0.02),
        'w_proj_attn': nrm(ks[18], (DEPTH, ATTN_WIDTH, D), ATTN_WIDTH ** -0.5),
        'w_proj_mlstm': nrm(ks[19], (DEPTH, M_WIDTH, D), M_WIDTH ** -0.5),
        'w_proj_fourier': nrm(ks[20], (DEPTH, F_WIDTH, D), F_WIDTH ** -0.5),
        'w_out': nrm(ks[21], (DEPTH, D, D), D ** -0.5),
        'norm2_g': 1.0 + nrm(ks[22], (DEPTH, D), 0.02),
        'w_ffn_in': nrm(ks[23], (DEPTH, D, 2 * FF_HIDDEN), D ** -0.5),
        'w_ffn_out': nrm(ks[24], (DEPTH, FF_HIDDEN, D), FF_HIDDEN ** -0.5),
    }


def reference(x_prompt, x_sample, cache_k, cache_v, state_C, state_n, state_m, c, c_ctx,
              w_ada, b_ada, norm1_g, w_in, q_norm_g, k_norm_g, m_conv_w, m_gate_b, m_norm_g,
              w_proj_attn, w_proj_mlstm, w_proj_fourier, w_out, norm2_g, w_ffn_in, w_ffn_out):
    f32 = jnp.float32
    t_lat = x_sample.shape[1]
    rows = t_lat // GRID_W
    row = jnp.repeat(jnp.arange(rows, dtype=jnp.int32), GRID_W)
    col = jnp.tile(jnp.arange(GRID_W, dtype=jnp.int32), rows)
    xp = x_prompt
    xs = x_sample
    ks_, vs_, Cs, ns, ms = [], [], [], [], []
    for l in range(DEPTH):
        wts = (norm1_g[l], w_in[l], q_norm_g[l], k_norm_g[l], m_conv_w[l], m_gate_b[l], m_norm_g[l],
               w_proj_attn[l], w_proj_mlstm[l], w_proj_fourier[l], w_out[l], norm2_g[l],
               w_ffn_in[l], w_ffn_out[l])
        mod_ctx = _modulation(c_ctx, w_ada[l], b_ada[l])
        xp, k_l, v_l, st_f, st_b = _layer(xp, mod_ctx, None, None, *wts)
        ks_.append(k_l)
        vs_.append(v_l)
        Cs.append(jnp.stack([st_f[0], st_b[0]], axis=1))
        ns.append(jnp.stack([st_f[1], st_b[1]], axis=1))
        ms.append(jnp.stack([st_f[2], st_b[2]], axis=1))
        mod_lat = [m[:, None, :] for m in _modulation(c, w_ada[l], b_ada[l])]
        ctx_fw = (state_C[:, l, 0].astype(f32), state_n[:, l, 0].astype(f32), state_m[:, l, 0].astype(f32))
        ctx_bw = (state_C[:, l, 1].astype(f32), state_n[:, l, 1].astype(f32), state_m[:, l, 1].astype(f32))
        ctx = (cache_k[:, l], cache_v[:, l], ctx_fw, ctx_bw)
        xs = _layer(xs, mod_lat, (row, col), ctx, *wts)[0]
    new_k = jnp.stack(ks_, axis=1)
    new_v = jnp.stack(vs_, axis=1)
    new_C = jnp.stack(Cs, axis=1)
    new_n = jnp.stack(ns, axis=1)
    new_m = jnp.stack(ms, axis=1)
    return (xp, xs, new_k, new_v, new_C, new_n, new_m)
```

```python
import contextlib
import numpy as np
import concourse.bass as bass
import concourse.mybir as mybir
from concourse.bass_utils import run_bass_kernel_spmd

F32 = mybir.dt.float32
BF16 = mybir.dt.bfloat16
AF = mybir.ActivationFunctionType
ALU = mybir.AluOpType
AX = mybir.AxisListType

NCORES = 8
D = 1024
T = 1536
NT = 3
TILES = 12
DEPTH = 2
EPS = 1e-6
IN_COLS = 5392
FFH = 2816
SEQS = [(0, 256, False), (256, 256, False), (512, 1024, True)]

_off = 0


def _alloc(n):
    global _off
    o = _off
    _off += n
    return o


PV_C = _alloc(16)
PV_BADA = [_alloc(48) for _ in range(DEPTH)]
PV_N1G = [_alloc(8) for _ in range(DEPTH)]
PV_N2G = [_alloc(8) for _ in range(DEPTH)]
PV_CONV = [_alloc(12) for _ in range(DEPTH)]
PV_GB = [_alloc(2) for _ in range(DEPTH)]
PV_M0 = [_alloc(1) for _ in range(DEPTH)]
PV_M0R = [_alloc(8) for _ in range(DEPTH)]
PV_GQK = [_alloc(128) for _ in range(DEPTH)]
PV_GM = [_alloc(256) for _ in range(DEPTH)]
PV_COS = _alloc(8 * 32)
PV_SIN = _alloc(8 * 32)
PV_ONE = _alloc(1)
PV_EPS = _alloc(1)
NP = _off


class Prog:
    ENG = ["pe", "act", "dve", "pool", "sp"]
    PERSIST = ("ps", "wb", "xT", "hT", "modT", "modd", "pv", "ident", "onesb", "zerob", "maskn", "sel", "bd", "dft256", "scT")
    BLK = {"pe": "tensor", "act": "scalar", "dve": "vector", "pool": "gpsimd", "sp": "sync"}

    def __init__(self, nc, stack, same_eng_sync=True):
        self.nc = nc
        self.stack = stack
        self.same = same_eng_sync
        self.prog = {e: [] for e in self.ENG}
        self.semh = {}
        self.cnt = {}
        self.waited = {e: {} for e in self.ENG}
        self.lastw = {}
        self.readers = {}
        self.outtoks = []
        self.lazy = None
        self.fresh_done = set()
        self.pe_pending = False
        for e in self.ENG:
            self._sem("e_" + e)

    def _sem(self, name):
        if name not in self.semh:
            self.semh[name] = self.stack.enter_context(self.nc.semaphore(name))
            self.cnt[name] = 0
        return self.semh[name]

    def _deps(self, eng, reads, writes):
        need = {}
        own = "e_" + eng

        def add(tok):
            if tok is None:
                return
            s, v = tok
            if need.get(s, 0) < v:
                need[s] = v

        for r in reads:
            add(self.lastw.get(r))
            if r.startswith("ps"):
                for s, v in self.readers.get(r, {}).items():
                    if s != own:
                        add((s, v))
        for w in writes:
            add(self.lastw.get(w))
            for s, v in self.readers.get(w, {}).items():
                add((s, v))
            if self.lazy is not None and w not in self.fresh_done and not w.startswith(self.PERSIST):
                self.fresh_done.add(w)
                for s, v in self.lazy.items():
                    add((s, v))
        own = "e_" + eng
        waits = []
        for s, v in need.items():
            if s == own and (eng == "pe" or not self.same):
                continue
            if self.waited[eng].get(s, 0) >= v:
                continue
            self.waited[eng][s] = v
            waits.append((s, v))
        return waits

    def _mark(self, tok, reads, writes):
        for r in reads:
            d = self.readers.setdefault(r, {})
            if d.get(tok[0], 0) < tok[1]:
                d[tok[0]] = tok[1]
        for w in writes:
            self.lastw[w] = tok
            self.readers[w] = {}

    def op(self, eng, fn, reads=(), writes=(), inc=True):
        waits = self._deps(eng, reads, writes)
        own = "e_" + eng
        if inc:
            self.cnt[own] += 1
            tok = (own, self.cnt[own])
        else:
            tok = (own, self.cnt[own] + 1)
        if eng == "pe":
            self.pe_pending = not inc
        self.prog[eng].append((waits, fn, (own, 1) if inc else None))
        self._mark(tok, reads, writes)

    def dma(self, q, out, in_, reads=(), writes=(), sem=None, is_output=False):
        waits = self._deps(q, reads, writes)
        self._sem(sem)
        self.cnt[sem] += 16
        tok = (sem, self.cnt[sem])
        self.prog[q].append((waits, lambda e, o=out, i=in_: e.dma_start(out=o, in_=i), (sem, 16)))
        self._mark(tok, reads, writes)
        if is_output:
            self.outtoks.append(tok)

    def dma_multi(self, q, pieces, reads=(), writes=(), sem=None):
        waits = self._deps(q, reads, writes)
        self._sem(sem)
        for i, (out, in_) in enumerate(pieces):
            self.cnt[sem] += 16
            self.prog[q].append((waits if i == 0 else [], lambda e, o=out, i_=in_: e.dma_start(out=o, in_=i_), (sem, 16)))
        tok = (sem, self.cnt[sem])
        self._mark(tok, reads, writes)

    def barrier(self, hard=False):
        assert not self.pe_pending, "open PE group at a phase boundary"
        if not hard:
            self.lazy = {s: v for s, v in self.cnt.items() if v > 0 and not s.startswith("d_wb")}
            self.fresh_done = set()
            return
        for e in self.ENG:
            if e == "pool":
                continue
            waits = []
            for s, v in self.cnt.items():
                if v == 0:
                    continue
                if s.startswith("d_wb"):
                    continue
                if s == "e_pe":
                    pass
                if s == "e_" + e and e == "pe":
                    continue
                if self.waited[e].get(s, 0) >= v:
                    continue
                self.waited[e][s] = v
                waits.append((s, v))
            if waits:
                self.prog[e].append((waits, None, None))

    def finish(self):
        need = {}
        for s, v in self.outtoks:
            need[s] = max(need.get(s, 0), v)
        waits = [(s, v) for s, v in need.items()]
        self.prog["sp"].append((waits, None, None))

    def emit(self):
        nc = self.nc
        with nc.Block() as block:
            for e in self.ENG:
                def body(engh, e=e):
                    for waits, fn, inc in self.prog[e]:
                        for s, v in waits:
                            engh.wait_ge(self.semh[s], v)
                        if fn is not None:
                            ins = fn(engh)
                            if inc is not None:
                                ins.then_inc(self.semh[inc[0]], inc[1])
                getattr(block, self.BLK[e])(body)


class Builder:
    def __init__(self, stop_after=None, taps=()):
        self.stop_after = stop_after
        self.taps = set(taps)
        self.nc = bass.Bass("TRN2", target_bir_lowering=False)
        self.tapnames = []
        self.bank_i = 0
        self.wslot_i = 0

    def din(self, name, shape):
        return self.nc.dram_tensor(name, list(shape), F32, kind="ExternalInput").ap()

    def dout(self, name, shape):
        return self.nc.dram_tensor(name, list(shape), F32, kind="ExternalOutput").ap()

    def sb(self, st, name, shape, dt):
        self.uid = getattr(self, "uid", 0) + 1
        return st.enter_context(self.nc.sbuf_tensor("%s_u%d" % (name, self.uid), list(shape), dt))

    def bank(self):
        b = self.banks[self.bank_i % 8]
        r = "ps%d" % (self.bank_i % 8)
        self.bank_i += 1
        return b, r

    def tap(self, name, ap, region, dt=F32):
        if name not in self.taps:
            return
        d = self.nc.dram_tensor("tap_" + name, list(ap.shape), dt, kind="ExternalOutput").ap()
        self.tapnames.append("tap_" + name)
        self.P.dma("sp", d, ap, reads=[region], sem="tap", is_output=True)

    def build(self):
        nc = self.nc
        with contextlib.ExitStack() as st:
            self.st = st
            self.P = P = Prog(nc, st)
            self.declare_io()
            self.alloc_persistent(st)
            self.phase0()
            done = self.stop_after == "phase0"
            for l in range(DEPTH):
                if done:
                    break
                done = self.layer(l)
            if not done:
                self.final_out()
            P.finish()
            P.emit()
        return nc

    def declare_io(self):
        self.xin = self.din("xin", [T, D])
        self.pvec = self.din("pvec", [128, NP])
        self.ck = self.din("ck", [DEPTH, 256, 256])
        self.cv = self.din("cv", [DEPTH, 256, 256])
        self.stC = self.din("stC", [DEPTH, 2, 4, 64, 65])
        self.w_ada = self.din("w_ada", [DEPTH, D, 6 * D])
        self.w_in = self.din("w_in", [DEPTH, D, IN_COLS])
        self.w_pa = self.din("w_pa", [DEPTH, 512, D])
        self.w_pm = self.din("w_pm", [DEPTH, 256, D])
        self.w_pf = self.din("w_pf", [DEPTH, 256, D])
        self.w_out = self.din("w_out", [DEPTH, D, D])
        self.w_f1 = self.din("w_f1", [DEPTH, D, 2 * FFH])
        self.w_f2 = self.din("w_f2", [DEPTH, FFH, D])
        self.c_ident = self.din("c_ident", [128, 128])
        self.c_mask = self.din("c_mask", [128, 2, 128])
        self.c_sel = self.din("c_sel", [64, 8, 128])
        self.c_bd = self.din("c_bd", [128, 256])
        self.c_dft1k = self.din("c_dft1k", [2, 1024, 1024])
        self.c_dft256 = self.din("c_dft256", [2, 256, 256])
        self.y = self.dout("y", [T, D])
        self.newk = self.dout("newk", [2, DEPTH, 256, 256])
        self.newv = self.dout("newv", [2, DEPTH, 256, 256])
        self.newC = self.dout("newC", [2, DEPTH, 2, 4, 64, 65])
        self.newm = self.dout("newm", [2, DEPTH, 2, 4])

    def alloc_persistent(self, st):
        sb = self.sb
        self.xT = sb(st, "xT", [128, 8, T], F32)
        self.hT = sb(st, "hT", [128, 8, T], BF16)
        self.pv = sb(st, "pv", [128, NP], F32)
        self.identf = sb(st, "identf", [128, 128], F32)
        self.identb = sb(st, "identb", [128, 128], BF16)
        self.onesb = sb(st, "onesb", [128, 128], BF16)
        self.zerob = sb(st, "zerob", [128, 128], BF16)
        self.maskn = sb(st, "maskn", [128, 2, 128], F32)
        self.sel = sb(st, "sel", [64, 8, 128], F32)
        self.bd = sb(st, "bd", [128, 256], BF16)
        self.dft256 = sb(st, "dft256", [128, 2, 2, 256], BF16)
        self.modT = sb(st, "modT", [128, DEPTH, 48, 2], F32)
        self.modd = sb(st, "modd", [128, DEPTH, 2, 2, 8], F32)
        self.scT = sb(st, "scT", [128, 8, 2], BF16)
        self.wb = [sb(st, "wb%d" % i, [128, 4096], BF16) for i in range(4)]
        self.pinned = set()
        self.bigbanks = [st.enter_context(self.nc.psum_tensor("bankpair%d" % i, [128, 1024], F32)) for i in range(4)]
        self.banks = [self.bigbanks[i // 2][:, (i % 2) * 512:(i % 2 + 1) * 512] for i in range(8)]

    def wslot(self, pin=False):
        while True:
            i = self.wslot_i % 4
            self.wslot_i += 1
            if i not in self.pinned:
                break
        if pin:
            self.pinned.add(i)
        return self.wb[i], "wb%d" % i

    def unpin(self, r):
        self.pinned.discard(int(r[2:]))

    def phase0(self):
        P = self.P
        P.dma("sp", self.pv[:], self.pvec, writes=["pv"], sem="d_pv")
        P.dma("sp", self.identf[:], self.c_ident, writes=["identf"], sem="d_c0")
        P.dma("pool", self.identb[:], self.c_ident, writes=["identb"], sem="d_c1")
        P.dma("sp", self.maskn[:], self.c_mask, writes=["maskn"], sem="d_c2")
        P.dma("sp", self.sel[:], self.c_sel, writes=["sel"], sem="d_c3")
        P.dma("pool", self.bd[:], self.c_bd, writes=["bd"], sem="d_c4")
        P.dma("pool", self.dft256[:], self.c_dft256.rearrange("a (c p) t -> p a c t", p=128),
              writes=["dft256"], sem="d_c5")
        P.op("dve", lambda e: e.memset(self.onesb[:], 1.0), writes=["onesb"])
        P.op("dve", lambda e: e.memset(self.zerob[:], 0.0), writes=["zerob"])
        P.op("act", lambda e: e.activation(out=self.scT[:].rearrange("p a b -> p (a b)"),
                                           in_=self.pv[:, PV_C:PV_C + 16], func=AF.Silu),
             reads=["pv"], writes=["scT"])
        with contextlib.ExitStack() as ph:
            xtmp = [self.sb(ph, "xtmp%d" % i, [128, D], F32) for i in range(2)]
            for t in range(TILES):
                xt = xtmp[t % 2]
                rx = "xtmp%d" % (t % 2)
                P.dma("sp", xt[:], self.xin[t * 128:(t + 1) * 128, :], writes=[rx], sem="d_" + rx)
                for half in range(2):
                    bk, rb = self.bank()
                    for kk in range(4):
                        c = half * 4 + kk
                        P.op("pe", lambda e, bk=bk, kk=kk, c=c, xt=xt: e.transpose(
                            out=bk[:, kk * 128:(kk + 1) * 128], in_=xt[:, c * 128:(c + 1) * 128],
                            identity=self.identf[:]),
                            reads=[rx, "identf"], writes=[rb], inc=(kk == 3))
                    eng = "act" if half == 0 else "dve"
                    dst = self.xT[:, half * 4:(half + 1) * 4, t * 128:(t + 1) * 128]
                    src = bk[:].rearrange("p (a b) -> p a b", b=128)
                    if eng == "act":
                        P.op("act", lambda e, dst=dst, src=src: e.copy(out=dst, in_=src),
                             reads=[rb], writes=["xT%d" % (t // 4)])
                    else:
                        P.op("dve", lambda e, dst=dst, src=src: e.tensor_copy(out=dst, in_=src),
                             reads=[rb], writes=["xT%d" % (t // 4)])
            self.ada(0, [0, 1, 2, 3])
            self.ada_derive1(0)
            self.P.barrier()
        self.tap("xT", self.xT[:, :, 0:512], "xT0")
        self.tap("modT", self.modT[:], "modT0")

    def ada(self, l, slabs, fixed_bank=None):
        P = self.P
        for s in slabs:
            w, rw = self.wslot()
            wv = w[:, 0:4096].rearrange("p (k c) -> p k c", c=512)
            P.dma("pool", wv, self.w_ada[l, :, s * 512:(s + 1) * 512].rearrange("(k p) c -> p k c", p=128),
                  writes=[rw], sem="d_" + rw)
            bk, rb = self.bank() if fixed_bank is None else self.fbank(fixed_bank)
            for m in range(4):
                for k in range(8):
                    P.op("pe", lambda e, bk=bk, m=m, k=k, wv=wv: e.matmul(
                        bk[:, m * 2:m * 2 + 2], lhsT=wv[:, k, m * 128:(m + 1) * 128], rhs=self.scT[:, k, :],
                        start=(k == 0), stop=(k == 7)),
                        reads=[rw, "scT"], writes=[rb], inc=(m == 3 and k == 7))
            dst = self.modT[:, l, 4 * s:4 * s + 4, :]
            src = bk[:, 0:8].rearrange("p (a b) -> p a b", b=2)
            bia = self.pv[:, PV_BADA[l] + 4 * s:PV_BADA[l] + 4 * s + 4].unsqueeze(2).to_broadcast([128, 4, 2])
            P.op("dve", lambda e, dst=dst, src=src, bia=bia: e.tensor_tensor(out=dst, in0=src, in1=bia, op=ALU.add),
                 reads=[rb, "pv"], writes=["modT%d" % l])

    def ada_derive1(self, l):
        self.ada_derive(l, which=(0,))

    def ada_derive2(self, l):
        self.ada_derive(l, which=(1,))

    def ada_derive(self, l, which=(0, 1)):
        P = self.P
        for j in range(2):
            for i, (sc0, g) in enumerate([(8, PV_N1G[l]), (32, PV_N2G[l])]):
                if i not in which:
                    continue
                dst = self.modd[:, l, j, i, :]
                src = self.modT[:, l, sc0:sc0 + 8, j]
                gg = self.pv[:, g:g + 8]
                P.op("dve", lambda e, dst=dst, src=src, gg=gg: e.scalar_tensor_tensor(
                    out=dst, in0=src, scalar=1.0, in1=gg, op0=ALU.add, op1=ALU.mult),
                    reads=["modT%d" % l, "pv"], writes=["modd%d_%d" % (l, i)])

    def mod(self, l, j, what, k):
        if what == "s1":
            return self.modd[:, l, j, 0, k:k + 1]
        if what == "s2":
            return self.modd[:, l, j, 1, k:k + 1]
        base = {"shift1": 0, "gate1": 16, "shift2": 24, "gate2": 40}[what]
        return self.modT[:, l, base + k, j:j + 1]

    def layer(self, l):
        with contextlib.ExitStack() as lay:
            self.attnT = self.sb(lay, "attnT", [128, 4, T], BF16)
            self.attention_phase(l)
            self.tap("attnT%d" % l, self.attnT[:], "attnT0", BF16)
            if self.stop_after == "attn_%d" % l:
                return True
            self.hmT = self.sb(lay, "hmT", [128, 2, T], BF16)
            self.mlstm_phase(l)
            self.tap("hmT%d" % l, self.hmT[:], "hmT0", BF16)
            if self.stop_after == "mlstm_%d" % l:
                return True
            self.foT = self.sb(lay, "foT", [128, 2, T], BF16)
            self.fourier_phase(l)
            self.tap("foT%d" % l, self.foT[:], "foT0", BF16)
            if self.stop_after == "fourier_%d" % l:
                return True
            self.mergedT = self.sb(lay, "mergedT", [128, 8, T], BF16)
            self.merge_phase(l)
            self.tap("mergedT%d" % l, self.mergedT[:], "mergedT0", BF16)
            self.tap("xmid%d" % l, self.xT[:, :, 0:512], "xT0")
            if self.stop_after == "merge_%d" % l:
                return True
        self.ffn_phase(l)
        self.tap("xend%d" % l, self.xT[:, :, 0:512], "xT0")
        if self.stop_after == "ffn_%d" % l:
            return True
        return False

    def fourier_phase(self, l):
        P = self.P
        sb = self.sb
        foT = self.foT
        with contextlib.ExitStack() as ph:
            uT = sb(ph, "uT", [128, 2, T], BF16)
            ABt = sb(ph, "ABt", [128, 12, 512], BF16)
            w, rw = self.wslot()
            wv = w[:, 0:2048].rearrange("p (k c) -> p k c", c=256)
            P.dma("pool", wv, self.w_in[l, :, 2064:2320].rearrange("(k p) c -> p k c", p=128), writes=[rw], sem="d_" + rw)
            for c in range(2):
                for nt in range(NT):
                    tok = slice(nt * 512, (nt + 1) * 512)
                    bk, rb = self.bank()
                    for k in range(8):
                        P.op("pe", lambda e, bk=bk, k=k, c=c, tok=tok: e.matmul(
                            bk[:], lhsT=wv[:, k, c * 128:(c + 1) * 128], rhs=self.hT[:, k, tok], start=(k == 0), stop=(k == 7)),
                            reads=[rw, "hT%d" % nt], writes=[rb], inc=(k == 7))
                    P.op("act", lambda e, bk=bk, c=c, tok=tok: e.copy(out=uT[:, c, tok], in_=bk[:]),
                         reads=[rb], writes=["uT%d" % nt])
            for t in range(TILES):
                tok = slice(t * 128, (t + 1) * 128)
                bk, rb = self.bank()
                for c in range(2):
                    P.op("pe", lambda e, bk=bk, c=c, tok=tok: e.matmul(
                        bk[:, c * 256:(c + 1) * 256], lhsT=uT[:, c, tok], rhs=self.bd[:, 0:256], start=True, stop=True),
                        reads=["uT%d" % (t // 4), "bd"], writes=[rb], inc=(c == 1))
                if t % 2 == 0:
                    P.op("dve", lambda e, bk=bk, t=t: e.tensor_copy(out=ABt[:, t, :], in_=bk[:]), reads=[rb], writes=["ABt%d" % t])
                else:
                    P.op("act", lambda e, bk=bk, t=t: e.copy(out=ABt[:, t, :], in_=bk[:]), reads=[rb], writes=["ABt%d" % t])
            for seq in range(2):
                s0 = seq * 256
                for c in range(2):
                    bk, rb = self.bank()
                    n = 0
                    for cs in range(2):
                        for tc in range(2):
                            P.op("pe", lambda e, bk=bk, c=c, cs=cs, tc=tc, seq=seq, n=n: e.matmul(
                                bk[:, 0:256], lhsT=ABt[:, 2 * seq + tc, c * 256 + cs * 128:c * 256 + cs * 128 + 128],
                                rhs=self.dft256[:, cs, tc, :], start=(n == 0), stop=(n == 3)),
                                reads=["ABt%d" % (2 * seq + tc), "dft256"], writes=[rb], inc=(n == 3))
                            n += 1
                    P.op("act", lambda e, bk=bk, c=c, s0=s0: e.copy(out=foT[:, c, s0:s0 + 256], in_=bk[:, 0:256]),
                         reads=[rb], writes=["foT%d" % (2 * seq), "foT%d" % (2 * seq + 1)])
            for pc in range(2):
                Wm = []
                rWm = []
                for cs in range(2):
                    wsl, rws_ = self.wslot()
                    wview = wsl[:, :].rearrange("p (c t) -> p c t", t=512)
                    P.dma("pool", wview, self.c_dft1k[cs][:, pc * 512:(pc + 1) * 512].rearrange("(c p) t -> p c t", p=128),
                          writes=[rws_], sem="d_" + rws_)
                    Wm.append(wview)
                    rWm.append(rws_)
                for c in range(2):
                    bk, rb = self.bank()
                    n = 0
                    for cs in range(2):
                        for tc in range(8):
                            P.op("pe", lambda e, bk=bk, c=c, cs=cs, tc=tc, n=n, Wm=Wm: e.matmul(
                                bk[:], lhsT=ABt[:, 4 + tc, c * 256 + cs * 128:c * 256 + cs * 128 + 128],
                                rhs=Wm[cs][:, tc, :], start=(n == 0), stop=(n == 15)),
                                reads=["ABt%d" % (4 + tc), rWm[cs]], writes=[rb], inc=(n == 15))
                            n += 1
                    P.op("act", lambda e, bk=bk, c=c, pc=pc: e.copy(
                        out=foT[:, c, 512 + pc * 512:512 + (pc + 1) * 512], in_=bk[:]),
                        reads=[rb], writes=["foT%d" % (4 + 4 * pc + i) for i in range(4)])
            P.barrier()

    def merge_phase(self, l):
        P = self.P
        sb = self.sb
        mergedT = self.mergedT
        with contextlib.ExitStack() as ph:
            sg = [sb(ph, "sg%d" % i, [128, 512], F32) for i in range(3)]
            m1 = sb(ph, "mg1", [128, 512], F32)
            m2 = sb(ph, "mg2", [128, 512], F32)
            wp1, rwp1 = self.wslot(pin=True)
            wp2, rwp2 = self.wslot(pin=True)
            wpa = wp1[:, 0:4096].rearrange("p (k c) -> p k c", c=1024)
            wpm = wp2[:, 0:2048].rearrange("p (k c) -> p k c", c=1024)
            wpf = wp2[:, 2048:4096].rearrange("p (k c) -> p k c", c=1024)
            P.dma("pool", wpa, self.w_pa[l].rearrange("(k p) c -> p k c", p=128), writes=[rwp1], sem="d_" + rwp1)
            P.dma_multi("pool", [(wpm, self.w_pm[l].rearrange("(k p) c -> p k c", p=128)),
                                 (wpf, self.w_pf[l].rearrange("(k p) c -> p k c", p=128))],
                        writes=[rwp2], sem="d_" + rwp2)
            rwp = rwp1
            for j in range(8):
                gslot, rg = self.wslot()
                gv = gslot[:, 0:3072].rearrange("p (k c) -> p k c", c=384)
                P.dma_multi("pool", [(gv[:, :, gi * 128:(gi + 1) * 128],
                                      self.w_in[l, :, 2320 + gi * 1024 + j * 128:2320 + gi * 1024 + (j + 1) * 128].rearrange(
                                          "(k p) c -> p k c", p=128)) for gi in range(3)],
                            writes=[rg], sem="d_" + rg)
                if True:
                    jj = 0
                    for nt in range(NT):
                        tok = slice(nt * 512, (nt + 1) * 512)
                        pb = [self.bank() for _ in range(3)]
                        gb = [self.bank() for _ in range(3)]
                        srcs = [(wpa, 4, self.attnT, "attnT"), (wpm, 2, self.hmT, "hmT"), (wpf, 2, self.foT, "foT")]
                        for bi, (wmat, nk, act, rname) in enumerate(srcs):
                            bk, rb = pb[bi]
                            for k in range(nk):
                                P.op("pe", lambda e, bk=bk, k=k, wmat=wmat, act=act, j=j, tok=tok, nk=nk: e.matmul(
                                    bk[:], lhsT=wmat[:, k, j * 128:(j + 1) * 128], rhs=act[:, k, tok],
                                    start=(k == 0), stop=(k == nk - 1)),
                                    reads=[rwp1, rwp2] + ["%s%d" % (rname, 4 * nt + i) for i in range(4)], writes=[rb],
                                    inc=(k == nk - 1))
                        for gi in range(3):
                            bk, rb = gb[gi]
                            for k in range(8):
                                P.op("pe", lambda e, bk=bk, k=k, gi=gi, tok=tok, gv=gv: e.matmul(
                                    bk[:], lhsT=gv[:, k, gi * 128:(gi + 1) * 128], rhs=self.hT[:, k, tok],
                                    start=(k == 0), stop=(k == 7)),
                                    reads=[rg, "hT%d" % nt], writes=[rb], inc=(k == 7))
                            P.op("act", lambda e, bk=bk, gi=gi: e.activation(out=sg[gi][:], in_=bk[:], func=AF.Sigmoid),
                                 reads=[rb], writes=["sg%d" % gi])
                        P.op("dve", lambda e, b0=pb[0][0]: e.tensor_tensor(out=m1[:], in0=b0[:], in1=sg[0][:], op=ALU.mult),
                             reads=[pb[0][1], "sg0"], writes=["mg1"])
                        P.op("dve", lambda e, b1=pb[1][0]: e.tensor_tensor(out=m2[:], in0=b1[:], in1=sg[1][:], op=ALU.mult),
                             reads=[pb[1][1], "sg1"], writes=["mg2"])
                        P.op("dve", lambda e: e.tensor_tensor(out=m1[:], in0=m1[:], in1=m2[:], op=ALU.add),
                             reads=["mg1", "mg2"], writes=["mg1"])
                        P.op("dve", lambda e, b2=pb[2][0]: e.tensor_tensor(out=m2[:], in0=b2[:], in1=sg[2][:], op=ALU.mult),
                             reads=[pb[2][1], "sg2"], writes=["mg2"])
                        P.op("dve", lambda e, j=j, tok=tok: e.tensor_tensor(out=mergedT[:, j, tok], in0=m1[:], in1=m2[:], op=ALU.add),
                             reads=["mg1", "mg2"], writes=["mergedT%d" % nt])
            self.unpin(rwp1)
            self.unpin(rwp2)
            for j in range(8):
                if j % 4 == 0:
                    wo, rwo = self.wslot()
                    wov = wo[:, :].rearrange("p (k c) -> p k c", c=512)
                    P.dma("pool", wov, self.w_out[l][:, (j // 4) * 512:(j // 4 + 1) * 512].rearrange("(k p) c -> p k c", p=128),
                          writes=[rwo], sem="d_" + rwo)
                for nt in range(NT):
                    tok = slice(nt * 512, (nt + 1) * 512)
                    js = 0 if nt == 0 else 1
                    bk, rb = self.bank()
                    for k in range(8):
                        P.op("pe", lambda e, bk=bk, k=k, j=j, tok=tok, wov=wov: e.matmul(
                            bk[:], lhsT=wov[:, k, (j % 4) * 128:(j % 4 + 1) * 128], rhs=mergedT[:, k, tok], start=(k == 0), stop=(k == 7)),
                            reads=[rwo, "mergedT%d" % nt], writes=[rb], inc=(k == 7))
                    g1 = self.mod(l, js, "gate1", j)
                    P.op("dve", lambda e, bk=bk, j=j, tok=tok, g1=g1: e.scalar_tensor_tensor(
                        out=self.xT[:, j, tok], in0=bk[:], scalar=g1, in1=self.xT[:, j, tok], op0=ALU.mult, op1=ALU.add),
                        reads=[rb, "modT%d" % l, "xT%d" % nt], writes=["xT%d" % nt])
            P.barrier()

    def ffn_phase(self, l):
        P = self.P
        sb = self.sb
        with contextlib.ExitStack() as ph:
            hid = sb(ph, "hid", [128, 22, T], BF16)
            sl = [sb(ph, "fsl%d" % i, [128, 512], BF16) for i in range(2)]
            slabs = {}

            def load_slab(si):
                j0 = 2 * si
                w, rw = self.wslot()
                wv = w[:, :].rearrange("p (k c) -> p k c", c=512)
                P.dma_multi("pool", [
                    (wv[:, :, 0:256], self.w_f1[l, :, j0 * 128:(j0 + 2) * 128].rearrange("(k p) c -> p k c", p=128)),
                    (wv[:, :, 256:512],
                     self.w_f1[l, :, FFH + j0 * 128:FFH + (j0 + 2) * 128].rearrange("(k p) c -> p k c", p=128))],
                    writes=[rw], sem="d_" + rw)
                slabs[si] = (wv, rw)

            load_slab(0)
            load_slab(1)
            self.norm(l, 2, barrier=False, use_pool=False)
            it = 0
            for si in range(11):
                j0 = 2 * si
                if si + 2 < 11:
                    load_slab(si + 2)
                wv, rw = slabs[si]
                for jj in range(2):
                    j = j0 + jj
                    for nt in range(NT):
                        tok = slice(nt * 512, (nt + 1) * 512)
                        bg, rbg = self.bank()
                        bv, rbv = self.bank()
                        for k in range(8):
                            P.op("pe", lambda e, bg=bg, k=k, jj=jj, tok=tok, wv=wv: e.matmul(
                                bg[:], lhsT=wv[:, k, jj * 128:(jj + 1) * 128], rhs=self.hT[:, k, tok], start=(k == 0), stop=(k == 7)),
                                reads=[rw, "hT%d" % nt], writes=[rbg], inc=False)
                        for k in range(8):
                            P.op("pe", lambda e, bv=bv, k=k, jj=jj, tok=tok, wv=wv: e.matmul(
                                bv[:], lhsT=wv[:, k, 256 + jj * 128:256 + (jj + 1) * 128], rhs=self.hT[:, k, tok],
                                start=(k == 0), stop=(k == 7)),
                                reads=[rw, "hT%d" % nt], writes=[rbv], inc=(k == 7))
                        s_ = sl[it % 2]
                        rs_ = "fsl%d" % (it % 2)
                        it += 1
                        P.op("act", lambda e, s_=s_, bg=bg: e.activation(out=s_[:], in_=bg[:], func=AF.Silu),
                             reads=[rbg], writes=[rs_])
                        P.op("dve", lambda e, s_=s_, bv=bv, j=j, tok=tok: e.tensor_tensor(
                            out=hid[:, j, tok], in0=bv[:], in1=s_[:], op=ALU.mult),
                            reads=[rbv, rs_], writes=["hid%d" % nt])
            for j in range(8):
                w, rw = self.wslot()
                wv = w[:, 0:22 * 128].rearrange("p (k c) -> p k c", c=128)
                P.dma("pool", wv, self.w_f2[l, :, j * 128:(j + 1) * 128].rearrange("(k p) c -> p k c", p=128),
                      writes=[rw], sem="d_" + rw)
                for nt in range(NT):
                    tok = slice(nt * 512, (nt + 1) * 512)
                    js = 0 if nt == 0 else 1
                    bk, rb = self.bank()
                    for k in range(22):
                        P.op("pe", lambda e, bk=bk, k=k, tok=tok, wv=wv: e.matmul(
                            bk[:], lhsT=wv[:, k, :], rhs=hid[:, k, tok], start=(k == 0), stop=(k == 21)),
                            reads=[rw, "hid%d" % nt], writes=[rb], inc=(k == 21))
                    g2 = self.mod(l, js, "gate2", j)
                    P.op("dve", lambda e, bk=bk, j=j, tok=tok, g2=g2: e.scalar_tensor_tensor(
                        out=self.xT[:, j, tok], in0=bk[:], scalar=g2, in1=self.xT[:, j, tok], op0=ALU.mult, op1=ALU.add),
                        reads=[rb, "modT%d" % l, "xT%d" % nt], writes=["xT%d" % nt])
            P.barrier()

    def fbank(self, i):
        return self.banks[i], "ps%d" % i

    def mlstm_phase(self, l):
        P = self.P
        sb = self.sb
        hmT = self.hmT
        eps = self.pv[:, PV_EPS:PV_EPS + 1]
        with contextlib.ExitStack() as ph:
            qkmT = sb(ph, "qkmT", [128, 4, T], BF16)
            Vm = sb(ph, "Vm", [128, 12, 4, 65], BF16)
            sigom = sb(ph, "sigom", [128, 12, 256], BF16)
            Agt = sb(ph, "Agt", [64, T], F32)
            gtok = sb(ph, "gtok", [128, 12, 3, 64], F32)
            C0 = sb(ph, "C0", [128, 2, 2, 65], BF16)
            mfin = sb(ph, "mfin", [64, 2], F32)
            P.op("dve", lambda e: e.memset(Vm[:, :, :, 64:65], 1.0), writes=["Vm%d" % i for i in range(12)])
            P.op("dve", lambda e: e.memset(Agt[:], 0.0), writes=["Agt"])
            wgs, rwgs = self.wslot(pin=True)
            wg_raw = wgs[:, 0:128].rearrange("p (k c) -> p k c", c=16)
            wgi = wgs[:, 128:640].rearrange("p (k c) -> p k c", c=64)
            wgf = wgs[:, 640:1152].rearrange("p (k c) -> p k c", c=64)
            P.dma("pool", wg_raw, self.w_in[l, :, 2048:2064].rearrange("(k p) c -> p k c", p=128),
                  writes=[rwgs], sem="d_" + rwgs)
            P.op("dve", lambda e: e.memset(wgs[:, 128:1152], 0.0), reads=[rwgs], writes=[rwgs])
            for (dst, c0_, src0) in [(wgi, 0, 0), (wgi, 32, 8), (wgf, 0, 4), (wgf, 32, 12)]:
                P.op("dve", lambda e, dst=dst, c0_=c0_, src0=src0: e.tensor_copy(
                    out=dst[:, :, c0_:c0_ + 4], in_=wg_raw[:, :, src0:src0 + 4]),
                    reads=[rwgs], writes=[rwgs])

            with contextlib.ExitStack() as sub:
                pre2 = [sb(sub, "pre%d" % i, [128, T], F32) for i in range(2)]
                cvb2 = [sb(sub, "cvb%d" % i, [128, T], F32) for i in range(2)]
                C0s = sb(sub, "C0s", [128, 2, 2, 65], F32)
                for d in range(2):
                    P.dma("sp", C0s[:, d, :, :], self.stC[l, d].rearrange("(a b) k e -> (b k) a e", b=2),
                          writes=["C0s%d" % d], sem="d_C0%d" % d)
                    P.op("dve", lambda e, d=d: e.tensor_copy(out=C0[:, d, :, :], in_=C0s[:, d, :, :]),
                         reads=["C0s%d" % d], writes=["C0_%d" % d])
                w, rw = self.wslot()
                wv = w[:, 0:4096].rearrange("p (k c) -> p k c", c=512)
                P.dma("pool", wv, self.w_in[l, :, 1024:1536].rearrange("(k p) c -> p k c", p=128),
                      writes=[rw], sem="d_" + rw)
                def projA(c):
                    pre = pre2[c % 2]
                    cvb = cvb2[c % 2]
                    rpre = "pre%d" % (c % 2)
                    rcvb = "cvb%d" % (c % 2)
                    for nt in range(NT):
                        tok = slice(nt * 512, (nt + 1) * 512)
                        bk, rb = self.bank()
                        for k in range(8):
                            P.op("pe", lambda e, bk=bk, k=k, c=c, tok=tok: e.matmul(
                                bk[:], lhsT=wv[:, k, c * 128:(c + 1) * 128], rhs=self.hT[:, k, tok],
                                start=(k == 0), stop=(k == 7)),
                                reads=[rw, "hT%d" % nt], writes=[rb], inc=(k == 7))
                        P.op("act", lambda e, bk=bk, tok=tok, pre=pre: e.copy(out=pre[:, tok], in_=bk[:]),
                             reads=[rb], writes=[rpre])

                def convB(c):
                    pre = pre2[c % 2]
                    cvb = cvb2[c % 2]
                    rpre = "pre%d" % (c % 2)
                    rcvb = "cvb%d" % (c % 2)
                    cw = [self.pv[:, PV_CONV[l] + c * 3 + j:PV_CONV[l] + c * 3 + j + 1] for j in range(3)]
                    P.op("dve", lambda e, cw=cw, pre=pre, cvb=cvb: e.tensor_scalar(out=cvb[:], in0=pre[:], scalar1=cw[1], scalar2=None,
                                                                op0=ALU.mult),
                         reads=[rpre, "pv"], writes=[rcvb])
                    for (s0, Ts, _) in SEQS:
                        P.op("dve", lambda e, cw=cw, s0=s0, Ts=Ts, pre=pre, cvb=cvb: e.scalar_tensor_tensor(
                            out=cvb[:, s0 + 1:s0 + Ts], in0=pre[:, s0:s0 + Ts - 1], scalar=cw[0],
                            in1=cvb[:, s0 + 1:s0 + Ts], op0=ALU.mult, op1=ALU.add),
                            reads=[rpre, "pv", rcvb], writes=[rcvb])
                        P.op("dve", lambda e, cw=cw, s0=s0, Ts=Ts, pre=pre, cvb=cvb: e.scalar_tensor_tensor(
                            out=cvb[:, s0:s0 + Ts - 1], in0=pre[:, s0 + 1:s0 + Ts], scalar=cw[2],
                            in1=cvb[:, s0:s0 + Ts - 1], op0=ALU.mult, op1=ALU.add),
                            reads=[rpre, "pv", rcvb], writes=[rcvb])
                    P.op("act", lambda e, c=c, cvb=cvb: e.activation(out=qkmT[:, c, :], in_=cvb[:], func=AF.Silu),
                         reads=[rcvb], writes=["qkmT%d" % c])
                    if c >= 2:
                        P.op("dve", lambda e, c=c: e.tensor_scalar(out=qkmT[:, c, :], in0=qkmT[:, c, :], scalar1=0.125,
                                                                  scalar2=None, op0=ALU.mult),
                             reads=["qkmT%d" % c], writes=["qkmT%d" % c])

                projA(0)
                for c in range(4):
                    if c + 1 < 4:
                        projA(c + 1)
                    convB(c)
                P.barrier()
            self.tap("qkmT%d" % l, qkmT[:], "qkmT0", BF16)
            with contextlib.ExitStack() as sub:
                gi = sb(sub, "gi", [64, T], F32)
                gf = sb(sub, "gf", [64, T], F32)
                t1 = sb(sub, "t1", [64, T], F32)
                t2 = sb(sub, "t2", [64, T], F32)
                for (wg, rwg, gt, rgt, bcol) in [(wgi, rwgs, gi, "gi", 0), (wgf, rwgs, gf, "gf", 1)]:
                    for nt in range(NT):
                        tok = slice(nt * 512, (nt + 1) * 512)
                        bk, rb = self.bank()
                        for k in range(8):
                            P.op("pe", lambda e, bk=bk, k=k, wg=wg, tok=tok: e.matmul(
                                bk[0:64, :], lhsT=wg[:, k, :], rhs=self.hT[:, k, tok], start=(k == 0), stop=(k == 7)),
                                reads=[rwg, "hT%d" % nt], writes=[rb], inc=(k == 7))
                        bia = self.pv[0:64, PV_GB[l] + bcol:PV_GB[l] + bcol + 1]
                        P.op("dve", lambda e, bk=bk, gt=gt, tok=tok, bia=bia: e.tensor_scalar(
                            out=gt[:, tok], in0=bk[0:64, :], scalar1=bia, scalar2=None, op0=ALU.add),
                            reads=[rb, "pv"], writes=[rgt])
                self.unpin(rwgs)
                w2_, rw2 = self.wslot()
                wv2 = w2_[:, 0:4096].rearrange("p (k c) -> p k c", c=512)
                P.dma("pool", wv2, self.w_in[l, :, 1536:2048].rearrange("(k p) c -> p k c", p=128),
                      writes=[rw2], sem="d_" + rw2)
                for t in range(TILES):
                    tok = slice(t * 128, (t + 1) * 128)
                    bk, rb = self.bank()
                    for k in range(8):
                        P.op("pe", lambda e, bk=bk, k=k, tok=tok: e.matmul(
                            bk[:], lhsT=self.hT[:, k, tok], rhs=wv2[:, k, :], start=(k == 0), stop=(k == 7)),
                            reads=[rw2, "hT%d" % (t // 4)], writes=[rb], inc=(k == 7))
                    P.op("act", lambda e, bk=bk, t=t: e.copy(
                        out=Vm[:, t, :, 0:64], in_=bk[:, 0:256].rearrange("p (h d) -> p h d", d=64)),
                        reads=[rb], writes=["Vm%d" % t])
                    P.op("act", lambda e, bk=bk, t=t: e.activation(out=sigom[:, t, :], in_=bk[:, 256:512], func=AF.Sigmoid),
                         reads=[rb], writes=["sigom%d" % t])
                P.op("act", lambda e: e.activation(out=t1[:], in_=gf[:], func=AF.Abs), reads=["gf"], writes=["t1"])
                P.op("act", lambda e: e.activation(out=t1[:], in_=t1[:], func=AF.Exp, scale=-1.0), reads=["t1"], writes=["t1"])
                P.op("act", lambda e: e.activation(out=t1[:], in_=t1[:], func=AF.Ln, bias=self.pv[0:64, PV_ONE:PV_ONE + 1]),
                     reads=["t1", "pv"], writes=["t1"])
                P.op("dve", lambda e: e.tensor_scalar_min(out=t2[:], in0=gf[:], scalar1=0.0), reads=["gf"], writes=["t2"])
                P.op("dve", lambda e: e.tensor_tensor(out=gf[:], in0=t2[:], in1=t1[:], op=ALU.subtract),
                     reads=["t1", "t2"], writes=["gf"])

                def rsl(s0, Ts):
                    return slice(s0 + Ts - 1, (s0 - 1) if s0 > 0 else None, -1)

                def ones(p0, n):
                    return self.pv[p0:p0 + 4, PV_ONE:PV_ONE + 1].to_broadcast([4, n])

                for (s0, Ts, is_s) in SEQS:
                    P.op("dve", lambda e, s0=s0, Ts=Ts: e.tensor_tensor_scan(
                        out=t1[0:4, s0:s0 + Ts], data0=ones(0, Ts), data1=gf[0:4, s0:s0 + Ts], initial=0.0,
                        op0=ALU.mult, op1=ALU.add), reads=["gf", "pv"], writes=["t1"])
                    P.op("dve", lambda e, s0=s0, Ts=Ts: e.tensor_tensor_scan(
                        out=t1[32:36, rsl(s0, Ts)], data0=ones(32, Ts), data1=gf[32:36, rsl(s0, Ts)], initial=0.0,
                        op0=ALU.mult, op1=ALU.add), reads=["gf", "pv"], writes=["t1"])
                P.op("dve", lambda e: e.tensor_tensor(out=gi[:], in0=gi[:], in1=t1[:], op=ALU.subtract),
                     reads=["gi", "t1"], writes=["gi"])
                for (s0, Ts, is_s) in SEQS:
                    i0 = self.pv[0:4, PV_M0[l]:PV_M0[l] + 1] if is_s else 0.0
                    i1 = self.pv[32:36, PV_M0[l]:PV_M0[l] + 1] if is_s else 0.0
                    P.op("dve", lambda e, s0=s0, Ts=Ts, i0=i0: e.tensor_tensor_scan(
                        out=Agt[0:4, s0:s0 + Ts], data0=ones(0, Ts), data1=gi[0:4, s0:s0 + Ts], initial=i0,
                        op0=ALU.mult, op1=ALU.max), reads=["gi", "pv"], writes=["Agt"])
                    P.op("dve", lambda e, s0=s0, Ts=Ts, i1=i1: e.tensor_tensor_scan(
                        out=Agt[32:36, rsl(s0, Ts)], data0=ones(32, Ts), data1=gi[32:36, rsl(s0, Ts)], initial=i1,
                        op0=ALU.mult, op1=ALU.max), reads=["gi", "pv"], writes=["Agt"])
                P.op("dve", lambda e: e.scalar_tensor_tensor(out=t2[:], in0=t1[:], scalar=-1.0, in1=Agt[:],
                                                            op0=ALU.mult, op1=ALU.subtract),
                     reads=["t1", "Agt"], writes=["t2"])
                for seq in range(2):
                    s0, Ts, _ = SEQS[seq]
                    P.op("dve", lambda e, seq=seq, s0=s0, Ts=Ts: e.tensor_scalar(
                        out=mfin[0:4, seq:seq + 1], in0=t2[0:4, s0 + Ts - 1:s0 + Ts], scalar1=-1.0, scalar2=None,
                        op0=ALU.mult), reads=["t2"], writes=["mfin"])
                    P.op("dve", lambda e, seq=seq, s0=s0: e.tensor_scalar(
                        out=mfin[32:36, seq:seq + 1], in0=t2[32:36, s0:s0 + 1], scalar1=-1.0, scalar2=None,
                        op0=ALU.mult), reads=["t2"], writes=["mfin"])
                for seq in range(2):
                    P.dma("sp", self.newm[seq, l, 0, :].unsqueeze(1), mfin[0:4, seq:seq + 1], reads=["mfin"],
                          sem="d_om", is_output=True)
                    P.dma("sp", self.newm[seq, l, 1, :].unsqueeze(1), mfin[32:36, seq:seq + 1], reads=["mfin"],
                          sem="d_om", is_output=True)
                for t in range(TILES):
                    tok = slice(t * 128, (t + 1) * 128)
                    bk, rb = self.bank()
                    P.op("pe", lambda e, bk=bk, tok=tok: e.transpose(out=bk[:, 0:64], in_=gi[0:64, tok],
                                                                    identity=self.identf[0:64, 0:64]),
                         reads=["gi", "identf"], writes=[rb], inc=False)
                    P.op("pe", lambda e, bk=bk, tok=tok: e.transpose(out=bk[:, 64:128], in_=t2[0:64, tok],
                                                                    identity=self.identf[0:64, 0:64]),
                         reads=["t2", "identf"], writes=[rb])
                    P.op("dve", lambda e, bk=bk, t=t: e.tensor_copy(out=gtok[:, t, 0, :], in_=bk[:, 0:64]),
                         reads=[rb], writes=["gtok"])
                    P.op("act", lambda e, bk=bk, t=t: e.activation(out=gtok[:, t, 1, :], in_=bk[:, 64:128], func=AF.Exp),
                         reads=[rb], writes=["gtok"])
                    P.op("dve", lambda e, t=t: e.tensor_scalar(out=gtok[:, t, 2, :], in0=gtok[:, t, 0, :], scalar1=-1.0,
                                                              scalar2=None, op0=ALU.mult),
                         reads=["gtok"], writes=["gtok"])
                self.tap("Agt%d" % l, Agt[:], "Agt")
                self.tap("negmj%d" % l, t2[:], "t2")
                self.tap("agate%d" % l, gi[:], "gi")
                P.barrier()
            hsum = sb(ph, "hsum", [128, 12, 256], F32)
            m4 = contextlib.ExitStack()
            wt = [sb(m4, "wt%d" % i, [128, 512], BF16) for i in range(2)]
            ptm = [sb(m4, "ptm%d" % i, [128, 512], BF16) for i in range(2)]
            dtmp = sb(m4, "dtmp", [128, 128], F32)
            mask01 = sb(m4, "mask01", [128, 2, 128], BF16)
            P.op("dve", lambda e: e.tensor_scalar(out=mask01[:], in0=self.maskn[:], scalar1=1.0e-30, scalar2=1.0,
                                                  op0=ALU.mult, op1=ALU.add), reads=["maskn"], writes=["mask01"])
            wib = sb(m4, "wib", [128, 512], F32)
            pin = [sb(m4, "pin%d" % i, [128, 512], BF16) for i in range(2)]
            dn = sb(m4, "dn", [128, 4], F32)
            htmp = sb(m4, "htmp", [128, 4, 64], F32)
            kmtok = sb(m4, "kmtok", [128, 4, 256], BF16)
            nA = sb(m4, "nA", [128, 1], F32)
            wk = sb(m4, "wk", [128, 2], F32)
            kw = [sb(m4, "kw%d" % i, [128, 64], BF16) for i in range(2)]
            Cst = [sb(m4, "Cst", [64, 8, 65], F32)] * 2
            for t in range(4):
                tb_, rtb_ = self.bank()
                tbv = tb_[:].bitcast(BF16).rearrange("p (i c) -> p i c", c=128)
                for pr in range(2):
                    P.op("pe", lambda e, tbv=tbv, pr=pr, t=t: e.transpose(
                        out=tbv[:, pr, :], in_=qkmT[:, 2 + pr, t * 128:(t + 1) * 128], identity=self.identb[:]),
                        reads=["qkmT%d" % (2 + pr), "identb"], writes=[rtb_], inc=(pr == 1))
                P.op("act", lambda e, tbv=tbv, t=t: e.copy(out=kmtok[:, t, :].rearrange("p (a b) -> p a b", b=128),
                                                          in_=tbv[:, 0:2, :]),
                     reads=[rtb_], writes=["kmtok"])
            jobs = []
            for si, (s0, Ts, is_s) in enumerate(SEQS):
                for d in range(2):
                    for h in range(4):
                        nch = Ts // 128
                        J = dict(idx=len(jobs), si=si, s0=s0, Ts=Ts, is_s=is_s, d=d, h=h, nch=nch, tile0=s0 // 128,
                                 npc=max(1, Ts // 512), pw=min(512, Ts), lpb=min(nch, 4), nacc=(nch + 3) // 4,
                                 r=d * 32 + h, ri=d * 4 + h, hp=(h % 2) * 64, pr=h // 2)
                        pieces = []
                        for sc in range(nch):
                            l_lo, l_hi = (sc * 128, Ts) if d == 0 else (0, (sc + 1) * 128)
                            for pc in range(l_lo // 512, (l_hi + 511) // 512):
                                c0, c1 = max(l_lo, pc * 512), min(l_hi, (pc + 1) * 512)
                                pieces.append((sc, pc, c0, c1))
                        lastpv = {}
                        for (sc, pc, c0, c1) in pieces:
                            for lt in range(c0 // 128, c1 // 128):
                                lastpv[lt // J["lpb"]] = (sc, lt)
                        J["pieces"] = pieces
                        J["lastpv"] = lastpv
                        base = 2 if J["idx"] % 2 == 0 else 4
                        if is_s:
                            J["abk"] = [self.fbank(pc) for pc in range(J["npc"])]
                        else:
                            J["abk"] = [self.fbank(J["idx"] % 2)]
                        J["acc"] = [self.fbank(base + i) for i in range(J["nacc"])]
                        J["cb"] = self.fbank(base + 1)
                        jobs.append(J)
            pin4 = pin + [sb(m4, "pinx%d" % i, [128, 512], BF16) for i in range(2)]
            nA2 = [nA, sb(m4, "nAx", [128, 1], F32)]
            wk2 = [wk, sb(m4, "wkx", [128, 2], F32)]
            cnt = dict(k=0, pin=0)

            def begin(J):
                s0, Ts, d, h, nch, tile0 = J["s0"], J["Ts"], J["d"], J["h"], J["nch"], J["tile0"]
                npc, pw, lpb, r, ri, hp, pr = J["npc"], J["pw"], J["lpb"], J["r"], J["ri"], J["hp"], J["pr"]
                for pc in range(npc):
                    bk, rb = J["abk"][pc]
                    P.op("pe", lambda e, bk=bk, pc=pc, ri=ri, s0=s0, pw=pw: e.matmul(
                        bk[:, 0:pw], lhsT=self.sel[0:64, ri, :], rhs=Agt[0:64, s0 + pc * pw:s0 + (pc + 1) * pw],
                        start=True, stop=True), reads=["sel", "Agt"], writes=[rb])
                for (bk, rb) in J["acc"]:
                    P.op("pe", lambda e, bk=bk: e.matmul(bk[:], lhsT=self.zerob[:, 0:128], rhs=self.hT[:, 0, 0:512],
                                                        start=True, stop=False, skip_group_check=True),
                         reads=["zerob", "hT0"], writes=[rb])
                if J["is_s"]:
                    for pc in range(npc):
                        ab_, rab = J["abk"][pc]
                        m0r = self.pv[hp:hp + 64, PV_M0R[l] + ri:PV_M0R[l] + ri + 1]
                        P.op("act", lambda e, ab_=ab_, m0r=m0r, hp=hp, pw=pw: e.activation(
                            out=wib[hp:hp + 64, 0:pw], in_=ab_[hp:hp + 64, 0:pw], func=AF.Exp, bias=m0r, scale=-1.0),
                            reads=[rab, "pv"], writes=["wib"])
                        pn = pin4[cnt["pin"] % 4]
                        rpn = "pin%d" % (cnt["pin"] % 4)
                        cnt["pin"] += 1
                        P.op("dve", lambda e, pn=pn, hp=hp, pr=pr, s0=s0, pc=pc, pw=pw: e.tensor_tensor(
                            out=pn[hp:hp + 64, 0:pw], in0=qkmT[hp:hp + 64, pr, s0 + pc * 512:s0 + pc * 512 + pw],
                            in1=wib[hp:hp + 64, 0:pw], op=ALU.mult),
                            reads=["wib", "qkmT%d" % pr], writes=[rpn])
                        def init_mm(pc=pc, pn=pn, rpn=rpn):
                            for lt in range(pc * 4, pc * 4 + 4):
                                ab2, rab2 = J["acc"][lt // lpb]
                                col = (lt % lpb) * 65
                                P.op("pe", lambda e, ab2=ab2, col=col, pn=pn, lt=lt, hp=hp, d=d, pr=pr: e.matmul(
                                    ab2[:, col:col + 65], lhsT=pn[hp:hp + 64, (lt % 4) * 128:(lt % 4) * 128 + 128],
                                    rhs=C0[hp:hp + 64, d, pr, :], start=False, stop=False, skip_group_check=True),
                                    reads=[rpn, "C0_%d" % d], writes=[rab2], inc=(lt == pc * 4 + 3))
                        J.setdefault("deferred", []).append(init_mm)
                else:
                    si = J["si"]
                    colA = Ts - 1 if d == 0 else 0
                    ab_, rab = J["abk"][0]
                    nA_ = nA2[J["idx"] % 2]
                    rnA = "nA%d" % (J["idx"] % 2)
                    wk_ = wk2[J["idx"] % 2]
                    rwk = "wk%d" % (J["idx"] % 2)
                    P.op("act", lambda e, ab_=ab_, colA=colA, nA_=nA_: e.mul(
                        out=nA_[:, 0:1], in_=ab_[:, colA:colA + 1], mul=-1.0),
                        reads=[rab], writes=[rnA])
                    P.op("act", lambda e, tile0=tile0, nch=nch, r=r, nA_=nA_, wk_=wk_: e.activation(
                        out=wk_[:, 0:nch], in_=gtok[:, tile0:tile0 + nch, 0, r], func=AF.Exp, bias=nA_[:, 0:1], scale=1.0),
                        reads=[rnA, "gtok"], writes=[rwk])
                    cb, rcb = J["cb"]
                    for sc in range(nch):
                        kw_ = kw[sc % 2]
                        P.op("dve", lambda e, kw_=kw_, sc=sc, tile0=tile0, h=h, wk_=wk_: e.tensor_scalar(
                            out=kw_[:], in0=kmtok[:, tile0 + sc, h * 64:(h + 1) * 64], scalar1=wk_[:, sc:sc + 1],
                            scalar2=None, op0=ALU.mult), reads=["kmtok", rwk], writes=["kw%d" % (sc % 2)])

                    def final_state_mm():
                        for sc in range(nch):
                            kw_ = kw[sc % 2]
                            P.op("pe", lambda e, cb=cb, kw_=kw_, sc=sc, tile0=tile0, h=h, nch=nch: e.matmul(
                                cb[0:64, 0:65], lhsT=kw_[:], rhs=Vm[:, tile0 + sc, h, :], start=(sc == 0),
                                stop=(sc == nch - 1)), reads=["kw%d" % (sc % 2), "Vm%d" % (tile0 + sc)],
                                writes=[rcb], inc=(sc == nch - 1))
                        P.op("act", lambda e, cb=cb, si=si, ri=ri: e.copy(out=Cst[si][:, ri, :], in_=cb[0:64, 0:65]),
                             reads=[rcb], writes=["Cst"])
                        if d == 1 and h == 3:
                            P.dma("sp", self.newC[si, l].rearrange("d h k e -> k (d h) e"), Cst[si][:], reads=["Cst"],
                                  sem="d_oC", is_output=True)
                    J.setdefault("deferred", []).append(final_state_mm)

            def stage_a(J, piece):
                sc, pc, c0, c1 = piece
                s0, d, hp, pr, r = J["s0"], J["d"], J["hp"], J["pr"], J["r"]
                k = cnt["k"]
                cnt["k"] += 1
                n = c1 - c0
                stile = J["tile0"] + sc
                sbk, rsb = self.fbank(6 + k % 2)
                P.op("pe", lambda e, sbk=sbk, n=n, hp=hp, pr=pr, s0=s0, sc=sc, c0=c0, c1=c1: e.matmul(
                    sbk[:, 0:n], lhsT=qkmT[hp:hp + 64, 2 + pr, s0 + sc * 128:s0 + (sc + 1) * 128],
                    rhs=qkmT[hp:hp + 64, pr, s0 + c0:s0 + c1], start=True, stop=True),
                    reads=["qkmT%d" % (2 + pr), "qkmT%d" % pr], writes=[rsb])
                ab_, rab = J["abk"][pc]
                W = wt[k % 2]
                rW = "wt%d" % (k % 2)
                a_s = gtok[:, stile, 0, r:r + 1]
                dlo = sc * 128
                rngs = []
                if c0 <= dlo < c1:
                    if dlo > c0:
                        rngs.append((c0, dlo))
                    if dlo + 128 < c1:
                        rngs.append((dlo + 128, c1))
                    na_s = gtok[:, stile, 2, r:r + 1]
                    P.op("act", lambda e, ab_=ab_, dlo=dlo, pc=pc, na_s=na_s: e.activation(
                        out=dtmp[:], in_=ab_[:, dlo - pc * 512:dlo - pc * 512 + 128], func=AF.Relu, bias=na_s, scale=1.0),
                        reads=[rab, "gtok"], writes=["dtmp"])
                    P.op("act", lambda e, W=W, dlo=dlo, c0=c0: e.activation(
                        out=W[:, dlo - c0:dlo - c0 + 128], in_=dtmp[:], func=AF.Exp, scale=-1.0),
                        reads=["dtmp"], writes=[rW])
                else:
                    rngs.append((c0, c1))
                for (x0, x1) in rngs:
                    P.op("act", lambda e, W=W, x0=x0, x1=x1, c0=c0, pc=pc, ab_=ab_, a_s=a_s: e.activation(
                        out=W[:, x0 - c0:x1 - c0], in_=ab_[:, x0 - pc * 512:x1 - pc * 512], func=AF.Exp,
                        bias=a_s, scale=-1.0), reads=[rab, "gtok"], writes=[rW])
                pt = ptm[k % 2]
                rpt = "ptm%d" % (k % 2)
                P.op("dve", lambda e, pt=pt, sbk=sbk, W=W, n=n: e.tensor_tensor(
                    out=pt[:, 0:n], in0=sbk[:, 0:n], in1=W[:, 0:n], op=ALU.mult),
                    reads=[rsb, rW], writes=[rpt])
                if c0 <= dlo < c1:
                    P.op("dve", lambda e, pt=pt, dlo=dlo, c0=c0, d=d: e.tensor_tensor(
                        out=pt[:, dlo - c0:dlo - c0 + 128], in0=pt[:, dlo - c0:dlo - c0 + 128], in1=mask01[:, d, :],
                        op=ALU.mult), reads=[rpt, "mask01"], writes=[rpt])
                return (pt, rpt)

            def stage_b(J, piece, ptinfo, is_last_piece):
                sc, pc, c0, c1 = piece
                pt, rpt = ptinfo
                lpb, h = J["lpb"], J["h"]
                stile = J["tile0"] + sc
                lts = list(range(c0 // 128, c1 // 128))
                for lt in lts:
                    ab2, rab2 = J["acc"][lt // lpb]
                    col = (lt % lpb) * 65
                    last = J["lastpv"][lt // lpb] == (sc, lt)
                    P.op("pe", lambda e, ab2=ab2, col=col, pt=pt, lt=lt, c0=c0, stile=stile, h=h, last=last: e.matmul(
                        ab2[:, col:col + 65], lhsT=pt[:, lt * 128 - c0:lt * 128 - c0 + 128],
                        rhs=Vm[:, stile, h, :], start=False, stop=last, skip_group_check=True),
                        reads=[rpt, "Vm%d" % stile], writes=[rab2], inc=(is_last_piece and lt == lts[-1]))

            def end(J):
                d, h, nch, lpb, tile0, r = J["d"], J["h"], J["nch"], J["lpb"], J["tile0"], J["r"]
                for bi, (ab2, rab2) in enumerate(J["acc"]):
                    nl = min(lpb, nch - bi * lpb)
                    av = ab2[:, 0:nl * 65].rearrange("p (a b) -> p a b", b=65)
                    t0_ = tile0 + bi * lpb
                    P.op("dve", lambda e, av=av, nl=nl, t0_=t0_, r=r: e.tensor_tensor(
                        out=dn[:, 0:nl], in0=av[:, :, 64], in1=gtok[:, t0_:t0_ + nl, 1, r], op=ALU.max),
                        reads=[rab2, "gtok"], writes=["dn"])
                    P.op("dve", lambda e, av=av, nl=nl: e.scalar_tensor_tensor(
                        out=dn[:, 0:nl], in0=av[:, :, 64], scalar=-1.0, in1=dn[:, 0:nl], op0=ALU.mult, op1=ALU.max),
                        reads=[rab2, "dn"], writes=["dn"])
                    P.op("dve", lambda e, nl=nl: e.reciprocal(out=dn[:, 0:nl], in_=dn[:, 0:nl]),
                         reads=["dn"], writes=["dn"])
                    hs_ = hsum[:, t0_:t0_ + nl, h * 64:(h + 1) * 64]
                    dnb = dn[:, 0:nl].unsqueeze(2).to_broadcast([128, nl, 64])
                    if d == 0:
                        P.op("dve", lambda e, hs_=hs_, av=av, dnb=dnb: e.tensor_tensor(
                            out=hs_, in0=av[:, :, 0:64], in1=dnb, op=ALU.mult),
                            reads=[rab2, "dn"], writes=["hsum"])
                    else:
                        P.op("dve", lambda e, av=av, dnb=dnb, nl=nl: e.tensor_tensor(
                            out=htmp[:, 0:nl, :], in0=av[:, :, 0:64], in1=dnb, op=ALU.mult),
                            reads=[rab2, "dn"], writes=["htmp"])
                        P.op("dve", lambda e, hs_=hs_, nl=nl: e.tensor_tensor(
                            out=hs_, in0=hs_, in1=htmp[:, 0:nl, :], op=ALU.add),
                            reads=["htmp", "hsum"], writes=["hsum"])

            flat = []
            for J in jobs:
                for i, pc_ in enumerate(J["pieces"]):
                    flat.append((J, pc_, i == 0, i == len(J["pieces"]) - 1))
            prev = None
            pend_end = [None]
            for (J, pc_, first, last_) in flat:
                if first:
                    begin(J)
                    J["npiece"] = 0
                info = stage_a(J, pc_)
                J["npiece"] += 1
                if J["npiece"] == 2:
                    for f_ in J.get("deferred", []):
                        f_()
                    J["deferred"] = []
                if prev is not None:
                    pJ, ppc, pinfo, plast = prev
                    stage_b(pJ, ppc, pinfo, plast)
                    if pend_end[0] is not None and pJ is not pend_end[0]:
                        end(pend_end[0])
                        pend_end[0] = None
                    if plast:
                        pend_end[0] = pJ
                prev = (J, pc_, info, last_)
            pJ, ppc, pinfo, plast = prev
            stage_b(pJ, ppc, pinfo, plast)
            if pend_end[0] is not None and pend_end[0] is not pJ:
                end(pend_end[0])
            end(pJ)
            self.tap("hsum%d" % l, hsum[:], "hsum")
            P.barrier()
            m4.close()
            sq2 = [sb(ph, "msq%d" % i, [128, 256], F32) for i in range(2)]
            s42 = [sb(ph, "ms4%d" % i, [128, 4], F32) for i in range(2)]
            hn2 = [sb(ph, "mhn%d" % i, [128, 256], F32) for i in range(2)]
            hmb = [sb(ph, "hmb%d" % i, [128, 256], BF16) for i in range(2)]

            def m5a(t):
                i2 = t % 2
                sq, s4, hn = sq2[i2], s42[i2], hn2[i2]
                rsq, rs4, rhn = "msq%d" % i2, "ms4%d" % i2, "mhn%d" % i2
                P.op("act", lambda e, t=t, sq=sq: e.activation(out=sq[:], in_=hsum[:, t, :], func=AF.Square),
                     reads=["hsum"], writes=[rsq])
                P.op("dve", lambda e, sq=sq, s4=s4: e.tensor_reduce(out=s4[:], in_=sq[:].rearrange("p (h d) -> p h d", d=64),
                                                                     axis=AX.X, op=ALU.add), reads=[rsq], writes=[rs4])
                P.op("act", lambda e, s4=s4: e.activation(out=s4[:], in_=s4[:], func=AF.Sqrt, scale=1.0 / 64, bias=eps),
                     reads=[rs4, "pv"], writes=[rs4])
                P.op("dve", lambda e, s4=s4: e.reciprocal(out=s4[:], in_=s4[:]), reads=[rs4], writes=[rs4])
                P.op("dve", lambda e, t=t, hn=hn, s4=s4: e.tensor_tensor(
                    out=hn[:].rearrange("p (h d) -> p h d", d=64), in0=hsum[:, t, :].rearrange("p (h d) -> p h d", d=64),
                    in1=s4[:].unsqueeze(2).to_broadcast([128, 4, 64]), op=ALU.mult),
                    reads=["hsum", rs4], writes=[rhn])
                P.op("dve", lambda e, hn=hn: e.tensor_tensor(out=hn[:], in0=hn[:], in1=self.pv[:, PV_GM[l]:PV_GM[l] + 256], op=ALU.mult),
                     reads=[rhn, "pv"], writes=[rhn])
                hb = hmb[i2]
                rhb = "hmb%d" % i2
                P.op("dve", lambda e, hb=hb, t=t, hn=hn: e.tensor_tensor(out=hb[:], in0=hn[:], in1=sigom[:, t, :], op=ALU.mult),
                     reads=[rhn, "sigom%d" % t], writes=[rhb])

            def m5b(t):
                tok = slice(t * 128, (t + 1) * 128)
                hb = hmb[t % 2]
                rhb = "hmb%d" % (t % 2)
                tb_, rtb_ = self.bank()
                tbv = tb_[:].bitcast(BF16).rearrange("p (i c) -> p i c", c=128)
                for c2 in range(2):
                    P.op("pe", lambda e, tbv=tbv, c2=c2, hb=hb: e.transpose(
                        out=tbv[:, c2, :], in_=hb[:, c2 * 128:(c2 + 1) * 128], identity=self.identb[:]),
                        reads=[rhb, "identb"], writes=[rtb_], inc=(c2 == 1))
                P.op("act", lambda e, tbv=tbv, tok=tok: e.copy(out=hmT[:, :, tok], in_=tbv[:, 0:2, :]),
                     reads=[rtb_], writes=["hmT%d" % t])

            m5a(0)
            for t in range(TILES):
                if t + 1 < TILES:
                    m5a(t + 1)
                m5b(t)
            P.barrier()

    def attention_phase(self, l):
        P = self.P
        sb = self.sb
        attnT = self.attnT
        with contextlib.ExitStack() as ph:
            qT = sb(ph, "qT", [128, 4, T], BF16)
            kT2 = sb(ph, "kT2", [128, 4, T + 256], BF16)
            Vaug = sb(ph, "Vaug", [128, 14, 4, 65], BF16)
            atok = [sb(ph, "atok%d" % i, [128, 4, 512], BF16) for i in range(2)]
            rden = [sb(ph, "rden%d" % i, [128, 4], F32) for i in range(2)]
            wq_, rwq = self.wslot()
            wkv_, rwkv = self.wslot()
            wvq = wq_[:, :].rearrange("p (k c) -> p k c", c=512)
            wvkv = wkv_[:, :].rearrange("p (k c) -> p k c", c=512)
            P.dma("pool", wvq, self.w_in[l, :, 0:512].rearrange("(k p) c -> p k c", p=128), writes=[rwq], sem="d_" + rwq)
            P.dma("pool", wvkv, self.w_in[l, :, 512:1024].rearrange("(k p) c -> p k c", p=128), writes=[rwkv], sem="d_" + rwkv)
            self.norm(l, 1, barrier=False, use_pool=False)
            P.op("dve", lambda e: e.memset(Vaug[:, :, :, 64:65], 1.0), writes=["Vaug%d" % i for i in range(14)])
            with contextlib.ExitStack() as sub:
                sqqk = [sb(sub, "sqqk%d" % i, [128, 768], F32) for i in range(2)]
                qkn = [sb(sub, "qkn%d" % i, [128, 768], F32) for i in range(2)]
                qkr = [sb(sub, "qkr%d" % i, [128, 768], BF16) for i in range(2)]
                kdup = [sb(sub, "kdup%d" % i, [128, 4, 2, 64], BF16) for i in range(2)]
                ssq = [sb(sub, "ssq%d" % i, [128, 12], F32) for i in range(2)]
                vst = [sb(sub, "vst%d" % i, [128, 256], F32) for i in range(2)]
                rta = sb(sub, "rta", [128, 12, 2, 16], F32)
                rtb = sb(sub, "rtb", [128, 12, 2, 16], F32)
                kc = sb(sub, "kc", [128, 2, 256], F32)
                vcs = sb(sub, "vcs", [128, 2, 256], F32)
                P.dma("sp", kc[:], self.ck[l].rearrange("(c p) f -> p c f", p=128), writes=["kc"], sem="d_kc")
                P.dma("sp", vcs[:], self.cv[l].rearrange("(c p) f -> p c f", p=128), writes=["vcs"], sem="d_vc")
                for c in range(2):
                    P.op("act", lambda e, c=c: e.copy(out=Vaug[:, 4 + c, :, 0:64],
                                                      in_=vcs[:, c, :].rearrange("p (h d) -> p h d", d=64)),
                         reads=["vcs"], writes=["Vaug%d" % (4 + c)])
                eps = self.pv[:, PV_EPS:PV_EPS + 1]

                def ktranspose(kd, rkd, kcols):
                    tb_, rtb_ = self.bank()
                    tbv = tb_[:].bitcast(BF16).rearrange("p (i c) -> p i c", c=128)
                    for i in range(4):
                        P.op("pe", lambda e, tbv=tbv, i=i, kd=kd: e.transpose(
                            out=tbv[:, i, :], in_=kd[:, i, :, :].rearrange("p a d -> p (a d)"), identity=self.identb[:]),
                            reads=[rkd, "identb"], writes=[rtb_], inc=(i == 3))
                    P.op("dve", lambda e, tbv=tbv, kcols=kcols: e.tensor_copy(out=kT2[:, :, kcols], in_=tbv[:, 0:4, :]),
                         reads=[rtb_], writes=["kT2_%d" % (kcols.start // 128)])


                def stageA(t):
                    is_s = t >= 4
                    tok = slice(t * 128, (t + 1) * 128)
                    rh = "hT%d" % (t // 4)
                    bq, rq = self.bank()
                    bkv, rkv = self.bank()
                    for k in range(8):
                        P.op("pe", lambda e, bq=bq, k=k, tok=tok: e.matmul(
                            bq[:], lhsT=self.hT[:, k, tok], rhs=wvq[:, k, :], start=(k == 0), stop=(k == 7)),
                            reads=[rwq, rh], writes=[rq], inc=False)
                    for k in range(8):
                        P.op("pe", lambda e, bkv=bkv, k=k, tok=tok: e.matmul(
                            bkv[:], lhsT=self.hT[:, k, tok], rhs=wvkv[:, k, :], start=(k == 0), stop=(k == 7)),
                            reads=[rwkv, rh], writes=[rkv], inc=(k == 7))
                    vch = t if t < 4 else t + 2
                    P.op("act", lambda e, vch=vch, bkv=bkv: e.copy(
                        out=Vaug[:, vch, :, 0:64], in_=bkv[:, 256:512].rearrange("p (h d) -> p h d", d=64)),
                        reads=[rkv], writes=["Vaug%d" % vch])
                    i2 = t % 2
                    if not is_s:
                        seq = t // 2
                        r0 = (t % 2) * 128
                        P.op("act", lambda e, i2=i2, bkv=bkv: e.copy(out=vst[i2][:], in_=bkv[:, 256:512]),
                             reads=[rkv], writes=["vst%d" % i2])
                        P.dma("sp", self.newv[seq, l, r0:r0 + 128, :], vst[i2][:], reads=["vst%d" % i2],
                              sem="d_ov%d" % i2, is_output=True)
                    sqt = sqqk[i2]
                    rsq = "sqqk%d" % i2
                    P.op("act", lambda e, sqt=sqt, bq=bq: e.activation(out=sqt[:, 0:512], in_=bq[:], func=AF.Square),
                         reads=[rq], writes=[rsq])
                    P.op("act", lambda e, sqt=sqt, bkv=bkv: e.activation(out=sqt[:, 512:768], in_=bkv[:, 0:256], func=AF.Square),
                         reads=[rkv], writes=[rsq])
                    ss = ssq[i2]
                    rss = "ssq%d" % i2
                    P.op("dve", lambda e, ss=ss, sqt=sqt: e.tensor_reduce(
                        out=ss[:], in_=sqt[:].rearrange("p (h d) -> p h d", d=64), axis=AX.X, op=ALU.add),
                        reads=[rsq], writes=[rss])
                    P.op("act", lambda e, ss=ss: e.activation(out=ss[:], in_=ss[:], func=AF.Sqrt, scale=1.0 / 64, bias=eps),
                         reads=[rss, "pv"], writes=[rss])
                    P.op("dve", lambda e, ss=ss: e.reciprocal(out=ss[:], in_=ss[:]), reads=[rss], writes=[rss])
                    qn = qkn[i2]
                    rqn = "qkn%d" % i2
                    P.op("dve", lambda e, qn=qn, bq=bq, ss=ss: e.tensor_tensor(
                        out=qn[:, 0:512].rearrange("p (h d) -> p h d", d=64), in0=bq[:].rearrange("p (h d) -> p h d", d=64),
                        in1=ss[:, 0:8].unsqueeze(2).to_broadcast([128, 8, 64]), op=ALU.mult),
                        reads=[rq, rss], writes=[rqn])
                    P.op("dve", lambda e, qn=qn, bkv=bkv, ss=ss: e.tensor_tensor(
                        out=qn[:, 512:768].rearrange("p (h d) -> p h d", d=64),
                        in0=bkv[:, 0:256].rearrange("p (h d) -> p h d", d=64),
                        in1=ss[:, 8:12].unsqueeze(2).to_broadcast([128, 4, 64]), op=ALU.mult),
                        reads=[rkv, rss], writes=[rqn])
                    gq = self.pv[:, PV_GQK[l]:PV_GQK[l] + 64].unsqueeze(1).to_broadcast([128, 8, 64])
                    gk = self.pv[:, PV_GQK[l] + 64:PV_GQK[l] + 128].unsqueeze(1).to_broadcast([128, 4, 64])
                    P.op("dve", lambda e, qn=qn, gq=gq: e.tensor_tensor(
                        out=qn[:, 0:512].rearrange("p (h d) -> p h d", d=64), in0=qn[:, 0:512].rearrange("p (h d) -> p h d", d=64),
                        in1=gq, op=ALU.mult), reads=[rqn, "pv"], writes=[rqn])
                    P.op("dve", lambda e, qn=qn, gk=gk: e.tensor_tensor(
                        out=qn[:, 512:768].rearrange("p (h d) -> p h d", d=64), in0=qn[:, 512:768].rearrange("p (h d) -> p h d", d=64),
                        in1=gk, op=ALU.mult), reads=[rqn, "pv"], writes=[rqn])
                    qr = qkr[i2]
                    rqr = "qkr%d" % i2
                    if not is_s:
                        P.dma("sp", self.newk[seq, l, r0:r0 + 128, :], qn[:, 512:768], reads=[rqn],
                              sem="d_ok%d" % i2, is_output=True)
                        P.op("act", lambda e, qr=qr, qn=qn: e.copy(out=qr[:], in_=qn[:]), reads=[rqn], writes=[rqr])
                    else:
                        ts = t - 4
                        v5 = qn[:].rearrange("p (h a b f) -> p h a b f", a=2, b=2, f=16)
                        o5 = qr[:].rearrange("p (h a b f) -> p h a b f", a=2, b=2, f=16)
                        X1, X2 = v5[:, :, :, 0, :], v5[:, :, :, 1, :]
                        O1, O2 = o5[:, :, :, 0, :], o5[:, :, :, 1, :]
                        cosv = self.pv[:, PV_COS + ts * 32:PV_COS + ts * 32 + 32].rearrange(
                            "p (a f) -> p a f", f=16).unsqueeze(1).to_broadcast([128, 12, 2, 16])
                        sinv = self.pv[:, PV_SIN + ts * 32:PV_SIN + ts * 32 + 32].rearrange(
                            "p (a f) -> p a f", f=16).unsqueeze(1).to_broadcast([128, 12, 2, 16])
                        rr = ["rta", "rtb"]
                        P.op("dve", lambda e, X1=X1, cosv=cosv: e.tensor_tensor(out=rta[:], in0=X1, in1=cosv, op=ALU.mult),
                             reads=[rqn, "pv"], writes=["rta"])
                        P.op("dve", lambda e, X2=X2, sinv=sinv: e.tensor_tensor(out=rtb[:], in0=X2, in1=sinv, op=ALU.mult),
                             reads=[rqn, "pv"], writes=["rtb"])
                        P.op("dve", lambda e, O1=O1: e.tensor_tensor(out=O1, in0=rta[:], in1=rtb[:], op=ALU.subtract),
                             reads=rr, writes=[rqr])
                        P.op("dve", lambda e, X2=X2, cosv=cosv: e.tensor_tensor(out=rta[:], in0=X2, in1=cosv, op=ALU.mult),
                             reads=[rqn, "pv", rqr], writes=["rta"])
                        P.op("dve", lambda e, X1=X1, sinv=sinv: e.tensor_tensor(out=rtb[:], in0=X1, in1=sinv, op=ALU.mult),
                             reads=[rqn, "pv", rqr], writes=["rtb"])
                        P.op("dve", lambda e, O2=O2: e.tensor_tensor(out=O2, in0=rta[:], in1=rtb[:], op=ALU.add),
                             reads=rr, writes=[rqr])
                    kd = kdup[i2]
                    rkd = "kdup%d" % i2
                    P.op("dve", lambda e, kd=kd, qr=qr: e.tensor_copy(
                        out=kd[:], in_=qr[:, 512:768].rearrange("p (h d) -> p h d", d=64).unsqueeze(2).to_broadcast([128, 4, 2, 64])),
                        reads=[rqr], writes=[rkd])
                    return (tok, qr, rqr, kd, rkd)

                def stageB(t, st_):
                    tok, qr, rqr, kd, rkd = st_
                    tb_, rtb_ = self.bank()
                    tbv = tb_[:].bitcast(BF16).rearrange("p (i c) -> p i c", c=128)
                    for i in range(4):
                        P.op("pe", lambda e, tbv=tbv, i=i, qr=qr: e.transpose(
                            out=tbv[:, i, :], in_=qr[:, i * 128:(i + 1) * 128], identity=self.identb[:]),
                            reads=[rqr, "identb"], writes=[rtb_], inc=(i == 3))
                    P.op("act", lambda e, tbv=tbv, tok=tok: e.copy(out=qT[:, :, tok], in_=tbv[:, 0:4, :]),
                         reads=[rtb_], writes=["qT%d" % t])
                    kcol0 = t * 128 if t < 4 else 768 + (t - 4) * 128
                    ktranspose(kd, rkd, slice(kcol0, kcol0 + 128))

                stA = {0: stageA(0)}
                for t in range(TILES):
                    if t + 1 < TILES:
                        stA[t + 1] = stageA(t + 1)
                    stageB(t, stA[t])
                for c in range(2):
                    kd = kdup[c % 2]
                    rkd = "kdup%d" % (c % 2)
                    P.op("dve", lambda e, kd=kd, c=c: e.tensor_copy(
                        out=kd[:], in_=kc[:, c, :].rearrange("p (h d) -> p h d", d=64).unsqueeze(2).to_broadcast([128, 4, 2, 64])),
                        reads=["kc"], writes=[rkd])
                    ktranspose(kd, rkd, slice(512 + c * 128, 512 + (c + 1) * 128))
                P.barrier()
            self.tap("qT%d" % l, qT[:], "qT0", BF16)
            self.tap("kT2%d" % l, kT2[:], "kT2_0", BF16)
            self.tap("Vaug%d" % l, Vaug[:], "Vaug0", BF16)
            if self.stop_after == "qkv_%d" % l:
                return
            PT = [sb(ph, "PT%d" % i, [128, 10, 512], BF16) for i in range(2)]
            state = dict(it=0, lh=0)

            iters = []
            for (q0, Tq, k0, nch, vc0, LB) in [(0, 256, 0, 2, 0, 256), (256, 256, 256, 2, 2, 256)]:
                for lh in range(Tq // LB):
                    grp = state["lh"]
                    state["lh"] += 1
                    for h in range(8):
                        iters.append(dict(q0=q0, k0=k0, nch=nch, vc0=vc0, LB=LB, lh=lh, h=h, grp=grp, idx=len(iters), pair=False))
            for lq in range(4):
                grp = state["lh"]
                state["lh"] += 1
                for g_ in range(4):
                    iters.append(dict(q0=512, k0=512, nch=10, vc0=4, LB=256, lh=lq, h=2 * g_ + 1, g=g_, grp=grp,
                                      idx=len(iters), pair=True))
            sbank_i = [0]

            def ptview(pt):
                return pt[:].rearrange("p a b -> p (a b)").rearrange("p (hh a c) -> p hh a c", hh=2, a=10)

            def stage_a_pair(itr):
                q0, k0, lq, g = itr["q0"], itr["k0"], itr["lh"], itr["g"]
                pi = itr["idx"] % 2
                ptv = ptview(PT[pi])
                rpt = "PT%d" % pi
                qreads = ["qT%d" % tt for tt in range((q0 + lq * 256) // 128, (q0 + (lq + 1) * 256) // 128)]
                for sc0 in range(0, 10, 2):
                    pp = sbank_i[0] % 3
                    sbank_i[0] += 1
                    big = self.bigbanks[pp]
                    bks = [self.fbank(2 * pp), self.fbank(2 * pp + 1)]
                    for u in range(2):
                        sc = sc0 + u
                        for hh in range(2):
                            bk, rb = bks[hh]
                            pb = hh * 64
                            P.op("pe", lambda e, bk=bk, u=u, sc=sc, g=g, pb=pb, lq=lq, k0=k0, q0=q0: e.matmul(
                                bk[:, u * 256:(u + 1) * 256],
                                lhsT=kT2[pb:pb + 64, g, k0 + sc * 128:k0 + (sc + 1) * 128],
                                rhs=qT[pb:pb + 64, g, q0 + lq * 256:q0 + (lq + 1) * 256], start=True, stop=True),
                                reads=qreads + ["kT2_%d" % ((k0 + sc * 128) // 128)], writes=[rb], inc=(u == 1))
                    P.op("act", lambda e, ptv=ptv, big=big, sc0=sc0: e.activation(
                        out=ptv[:, :, sc0:sc0 + 2, :], in_=big[:, 0:1024].rearrange("p (hh u c) -> p hh u c", hh=2, u=2),
                        func=AF.Exp, scale=0.125), reads=[bks[0][1], bks[1][1]], writes=[rpt])

            def stage_b_pair(itr):
                q0, vc0, lq, g = itr["q0"], itr["vc0"], itr["lh"], itr["g"]
                pi = itr["idx"] % 2
                ptv = ptview(PT[pi])
                rpt = "PT%d" % pi
                ai = itr["grp"] % 2
                ab = atok[ai]
                rab = "atok%d" % ai
                obk, rob = self.fbank(6 + itr["idx"] % 2)
                for hh in range(2):
                    for lt in range(2):
                        for sc in range(10):
                            P.op("pe", lambda e, obk=obk, hh=hh, lt=lt, sc=sc, ptv=ptv, g=g, vc0=vc0: e.matmul(
                                obk[:, (hh * 2 + lt) * 65:(hh * 2 + lt + 1) * 65], lhsT=ptv[:, hh, sc, lt * 128:(lt + 1) * 128],
                                rhs=Vaug[:, vc0 + sc, g, :], start=(sc == 0), stop=(sc == 9)),
                                reads=[rpt, "Vaug%d" % (vc0 + sc)], writes=[rob], inc=(hh == 1 and lt == 1 and sc == 9))
                ov = obk[:, 0:4 * 65].rearrange("p (a b) -> p a b", b=65)
                rd = rden[g % 2]
                rrd = "rden%d" % (g % 2)
                P.op("dve", lambda e, rd=rd, ov=ov: e.reciprocal(out=rd[:, 0:4], in_=ov[:, :, 64]), reads=[rob], writes=[rrd])
                for hh in range(2):
                    h = 2 * g + hh
                    P.op("dve", lambda e, rd=rd, ov=ov, ab=ab, h=h, hh=hh: e.tensor_tensor(
                        out=ab[:, 0:2, h * 64:(h + 1) * 64], in0=ov[:, hh * 2:hh * 2 + 2, 0:64],
                        in1=rd[:, hh * 2:hh * 2 + 2].unsqueeze(2).to_broadcast([128, 2, 64]), op=ALU.mult),
                        reads=[rob, rrd], writes=[rab])
                if g == 3:
                    for i in range(2):
                        tile = (q0 + lq * 256) // 128 + i
                        tb_, rtb_ = self.fbank(6 + (itr["idx"] + 1) % 2)
                        tbv = tb_[:].bitcast(BF16).rearrange("p (i c) -> p i c", c=128)
                        for c4 in range(4):
                            P.op("pe", lambda e, tbv=tbv, c4=c4, ab=ab, i=i: e.transpose(
                                out=tbv[:, c4, :], in_=ab[:, i, c4 * 128:(c4 + 1) * 128], identity=self.identb[:]),
                                reads=[rab, "identb"], writes=[rtb_], inc=(c4 == 3))
                        P.op("act", lambda e, tbv=tbv, tile=tile: e.copy(
                            out=attnT[:, :, tile * 128:(tile + 1) * 128], in_=tbv[:, 0:4, :]),
                            reads=[rtb_], writes=["attnT%d" % tile])

            def stage_a(itr):
                if itr["pair"]:
                    return stage_a_pair(itr)
                q0, k0, nch, LB, lh, h = itr["q0"], itr["k0"], itr["nch"], itr["LB"], itr["lh"], itr["h"]
                per = 512 // LB
                g = h // 2
                pb = (h % 2) * 64
                pi = itr["idx"] % 2
                pt = PT[pi]
                rpt = "PT%d" % pi
                for sc0 in range(0, nch, per):
                    sbk, rsb = self.fbank(sbank_i[0] % 6)
                    sbank_i[0] += 1
                    for u in range(per):
                        sc = sc0 + u
                        P.op("pe", lambda e, sbk=sbk, u=u, sc=sc, g=g, pb=pb, lh=lh, LB=LB, k0=k0, q0=q0: e.matmul(
                            sbk[:, u * LB:(u + 1) * LB],
                            lhsT=kT2[pb:pb + 64, g, k0 + sc * 128:k0 + (sc + 1) * 128],
                            rhs=qT[pb:pb + 64, g, q0 + lh * LB:q0 + (lh + 1) * LB], start=True, stop=True),
                            reads=["qT%d" % tt for tt in range((q0 + lh * LB) // 128, (q0 + (lh + 1) * LB) // 128)]
                            + ["kT2_%d" % ((k0 + sc * 128) // 128)], writes=[rsb], inc=(u == per - 1))
                    P.op("act", lambda e, pt=pt, sbk=sbk, sc0=sc0, per=per, LB=LB: e.activation(
                        out=pt[:, sc0:sc0 + per, 0:LB], in_=sbk[:, 0:per * LB].rearrange("p (a b) -> p a b", b=LB),
                        func=AF.Exp, scale=0.125),
                        reads=[rsb], writes=[rpt])

            def stage_b(itr):
                if itr["pair"]:
                    return stage_b_pair(itr)
                q0, nch, vc0, LB, lh, h = itr["q0"], itr["nch"], itr["vc0"], itr["LB"], itr["lh"], itr["h"]
                ntl = LB // 128
                g = h // 2
                pi = itr["idx"] % 2
                pt = PT[pi]
                rpt = "PT%d" % pi
                ai = itr["grp"] % 2
                ab = atok[ai]
                rab = "atok%d" % ai
                obk, rob = self.fbank(6 + itr["idx"] % 2)
                for lt in range(ntl):
                    for sc in range(nch):
                        P.op("pe", lambda e, obk=obk, lt=lt, sc=sc, pt=pt, g=g, vc0=vc0, nch=nch: e.matmul(
                            obk[:, lt * 65:(lt + 1) * 65], lhsT=pt[:, sc, lt * 128:(lt + 1) * 128],
                            rhs=Vaug[:, vc0 + sc, g, :], start=(sc == 0), stop=(sc == nch - 1)),
                            reads=[rpt, "Vaug%d" % (vc0 + sc)], writes=[rob], inc=(lt == ntl - 1 and sc == nch - 1))
                ov = obk[:, 0:ntl * 65].rearrange("p (a b) -> p a b", b=65)
                rd = rden[h % 2]
                rrd = "rden%d" % (h % 2)
                P.op("dve", lambda e, rd=rd, ov=ov, ntl=ntl: e.reciprocal(out=rd[:, 0:ntl], in_=ov[:, :, 64]),
                     reads=[rob], writes=[rrd])
                P.op("dve", lambda e, rd=rd, ov=ov, ab=ab, h=h, ntl=ntl: e.tensor_tensor(
                    out=ab[:, 0:ntl, h * 64:(h + 1) * 64], in0=ov[:, :, 0:64],
                    in1=rd[:, 0:ntl].unsqueeze(2).to_broadcast([128, ntl, 64]), op=ALU.mult),
                    reads=[rob, rrd], writes=[rab])
                if h == 7:
                    for i in range(ntl):
                        tile = (q0 + lh * LB) // 128 + i
                        tb_, rtb_ = self.fbank(6 + (itr["idx"] + 1) % 2)
                        tbv = tb_[:].bitcast(BF16).rearrange("p (i c) -> p i c", c=128)
                        for c4 in range(4):
                            P.op("pe", lambda e, tbv=tbv, c4=c4, ab=ab, i=i: e.transpose(
                                out=tbv[:, c4, :], in_=ab[:, i, c4 * 128:(c4 + 1) * 128], identity=self.identb[:]),
                                reads=[rab, "identb"], writes=[rtb_], inc=(c4 == 3))
                        P.op("act", lambda e, tbv=tbv, tile=tile: e.copy(
                            out=attnT[:, :, tile * 128:(tile + 1) * 128], in_=tbv[:, 0:4, :]),
                            reads=[rtb_], writes=["attnT%d" % tile])

            stage_a(iters[0])
            todo = []
            if l == 0:
                todo += [(0, s_) for s_ in range(4, 12)]
            if l + 1 < DEPTH:
                todo += [(l + 1, s_) for s_ in range(12)]
            for i in range(len(iters)):
                if i + 1 < len(iters):
                    stage_a(iters[i + 1])
                stage_b(iters[i])
                if todo and iters[i]["h"] != 7 and (i % 2 == 1 or len(todo) > 12):
                    ll, s_ = todo.pop(0)
                    self.ada(ll, [s_], fixed_bank=7)
            for (ll, s_) in todo:
                self.ada(ll, [s_], fixed_bank=7)
            if l == 0:
                self.ada_derive2(0)
            if l + 1 < DEPTH:
                self.ada_derive(l + 1)
            P.barrier()

    def norm(self, l, which, barrier=True, use_pool=False):
        P = self.P
        sname, shname = ("s1", "shift1") if which == 1 else ("s2", "shift2")
        with contextlib.ExitStack() as ph:
            sq = [self.sb(ph, "nsq%d" % i, [128, 512], BF16) for i in range(2)]
            tmp = [self.sb(ph, "ntmp%d" % i, [128, 512], F32) for i in range(2)]
            rs = [self.sb(ph, "nrs%d" % i, [128, 512], F32) for i in range(3)]
            bks = []
            for nt in range(NT):
                tok = slice(nt * 512, (nt + 1) * 512)
                rx = "xT%d" % nt
                bk, rb = self.bank()
                bks.append((bk, rb))
                for k in range(8):
                    s_ = sq[k % 2]
                    rs_ = "nsq%d" % (k % 2)
                    P.op("act", lambda e, s_=s_, k=k, tok=tok: e.activation(out=s_[:], in_=self.xT[:, k, tok], func=AF.Square),
                         reads=[rx], writes=[rs_])
                    P.op("pe", lambda e, bk=bk, s_=s_, k=k: e.matmul(bk[:], lhsT=self.onesb[:], rhs=s_[:],
                                                                    start=(k == 0), stop=(k == 7)),
                         reads=[rs_, "onesb"], writes=[rb])
            for nt in range(NT):
                j = 0 if nt == 0 else 1
                tok = slice(nt * 512, (nt + 1) * 512)
                rx = "xT%d" % nt
                bk, rb = bks[nt]
                r_ = rs[nt]
                rr_ = "nrs%d" % nt
                P.op("act", lambda e, bk=bk, r_=r_: e.activation(out=r_[:], in_=bk[:], func=AF.Ln, scale=1.0 / D,
                                                                bias=self.pv[:, PV_EPS:PV_EPS + 1]),
                     reads=[rb, "pv"], writes=[rr_])
                P.op("act", lambda e, r_=r_: e.activation(out=r_[:], in_=r_[:], func=AF.Exp, scale=-0.5),
                     reads=[rr_], writes=[rr_])
                for k in range(8):
                    t_ = tmp[k % 2]
                    rt_ = "ntmp%d" % (k % 2)
                    P.op("pool" if (use_pool and k % 2 == 1) else "dve",
                         lambda e, t_=t_, k=k, tok=tok, r_=r_: e.tensor_tensor(out=t_[:], in0=self.xT[:, k, tok], in1=r_[:], op=ALU.mult),
                         reads=[rx, rr_], writes=[rt_])
                    P.op("act", lambda e, t_=t_, k=k, tok=tok, j=j: e.activation(
                        out=self.hT[:, k, tok], in_=t_[:], func=AF.Identity,
                        scale=self.mod(l, j, sname, k), bias=self.mod(l, j, shname, k)),
                        reads=[rt_, "modd%d_%d" % (l, which - 1), "modT%d" % l], writes=["hT%d" % nt])
            if barrier:
                P.barrier()

    def final_out(self):
        P = self.P
        with contextlib.ExitStack() as ph:
            yst = [self.sb(ph, "yst%d" % i, [128, D], F32) for i in range(2)]
            for t in range(TILES):
                ys = yst[t % 2]
                ry = "yst%d" % (t % 2)
                for half in range(2):
                    bk, rb = self.bank()
                    for kk in range(4):
                        c = half * 4 + kk
                        P.op("pe", lambda e, bk=bk, kk=kk, c=c, t=t: e.transpose(
                            out=bk[:, kk * 128:(kk + 1) * 128], in_=self.xT[:, c, t * 128:(t + 1) * 128], identity=self.identf[:]),
                            reads=["xT%d" % (t // 4), "identf"], writes=[rb], inc=(kk == 3))
                    if half == 0:
                        P.op("act", lambda e, bk=bk, ys=ys: e.copy(out=ys[:, 0:512], in_=bk[:]), reads=[rb], writes=[ry])
                    else:
                        P.op("dve", lambda e, bk=bk, ys=ys: e.tensor_copy(out=ys[:, 512:1024], in_=bk[:]), reads=[rb], writes=[ry])
                P.dma("sp", self.y[t * 128:(t + 1) * 128, :], ys[:], reads=[ry], sem="d_" + ry, is_output=True)
            P.barrier()


def _consts():
    ident = np.eye(128, dtype=np.float32)
    s = np.arange(128)[:, None]
    l_ = np.arange(128)[None, :]
    NEG = -1.0e30
    mask = np.zeros((128, 2, 128), np.float32)
    mask[:, 0, :] = np.where(s <= l_, 0.0, NEG)
    mask[:, 1, :] = np.where(s >= l_, 0.0, NEG)
    sel = np.zeros((64, 8, 128), np.float32)
    for d in range(2):
        for h in range(4):
            sel[d * 32 + h, d * 4 + h, :] = 1.0
    c = np.arange(64)
    ang = 2.0 * np.pi * np.outer(c, c) / 64.0
    bd = np.zeros((128, 256), np.float64)
    for g in range(2):
        bd[g * 64:(g + 1) * 64, g * 64:(g + 1) * 64] = np.cos(ang) / 8.0
        bd[g * 64:(g + 1) * 64, 128 + g * 64:128 + (g + 1) * 64] = np.sin(ang) / 8.0
    def dft(n):
        t = np.arange(n)
        a = 2.0 * np.pi * ((np.outer(t, t)) % n) / n
        sc = 1.0 / np.sqrt(n)
        return np.stack([np.cos(a) * sc, -np.sin(a) * sc]).astype(np.float32)
    return dict(c_ident=ident, c_mask=mask, c_sel=sel, c_bd=bd.astype(np.float32),
                c_dft1k=dft(1024), c_dft256=dft(256))


def _chunked(v):
    return np.ascontiguousarray(v.reshape(-1, 128).T)


def _pvec(core, inp):
    pv = np.zeros((128, NP), np.float32)
    cc = np.stack([_chunked(inp["c_ctx"]), _chunked(inp["c"][core])], axis=-1)
    pv[:, PV_C:PV_C + 16] = cc.reshape(128, 16)
    for l in range(DEPTH):
        pv[:, PV_BADA[l]:PV_BADA[l] + 48] = _chunked(inp["b_ada"][l])
        pv[:, PV_N1G[l]:PV_N1G[l] + 8] = _chunked(inp["norm1_g"][l])
        pv[:, PV_N2G[l]:PV_N2G[l] + 8] = _chunked(inp["norm2_g"][l])
        cw = inp["m_conv_w"][l]
        pv[:, PV_CONV[l]:PV_CONV[l] + 12] = np.stack([_chunked(cw[j]) for j in range(3)], axis=-1).reshape(128, 12)
        gb = inp["m_gate_b"][l]
        pv[0:4, PV_GB[l]] = gb[0:4]
        pv[32:36, PV_GB[l]] = gb[8:12]
        pv[0:4, PV_GB[l] + 1] = gb[4:8]
        pv[32:36, PV_GB[l] + 1] = gb[12:16]
        sm = inp["state_m"][core, l]
        pv[0:4, PV_M0[l]] = sm[0]
        pv[32:36, PV_M0[l]] = sm[1]
        pv[:, PV_M0R[l]:PV_M0R[l] + 8] = sm.reshape(1, 8)
        pv[:, PV_GQK[l]:PV_GQK[l] + 64] = inp["q_norm_g"][l][None, :]
        pv[:, PV_GQK[l] + 64:PV_GQK[l] + 128] = inp["k_norm_g"][l][None, :]
        pv[:, PV_GM[l]:PV_GM[l] + 256] = inp["m_norm_g"][l][None, :]
    pos = np.arange(1024)
    row = (pos // 64).astype(np.float64)
    col = (pos % 64).astype(np.float64)
    inv = 1.0 / (10000.0 ** (np.arange(16, dtype=np.float64) / 16.0))
    ang = np.concatenate([row[:, None] * inv[None, :], col[:, None] * inv[None, :]], axis=1)
    cos = np.cos(ang).reshape(8, 128, 32).transpose(1, 0, 2).reshape(128, 256)
    sin = np.sin(ang).reshape(8, 128, 32).transpose(1, 0, 2).reshape(128, 256)
    pv[:, PV_COS:PV_COS + 256] = cos
    pv[:, PV_SIN:PV_SIN + 256] = sin
    pv[:, PV_ONE] = 1.0
    pv[:, PV_EPS] = EPS
    return pv


def make_in_maps(inp):
    inp = {k: np.asarray(v) for k, v in inp.items()}
    consts = _consts()
    shared = dict(w_ada=inp["w_ada"], w_in=inp["w_in"], w_pa=inp["w_proj_attn"], w_pm=inp["w_proj_mlstm"],
                  w_pf=inp["w_proj_fourier"], w_out=inp["w_out"], w_f1=inp["w_ffn_in"], w_f2=inp["w_ffn_out"])
    shared = {k: np.ascontiguousarray(v, dtype=np.float32) for k, v in shared.items()}
    maps = []
    for c in range(NCORES):
        xin = np.concatenate([inp["x_prompt"][2 * c], inp["x_prompt"][2 * c + 1], inp["x_sample"][c]], axis=0)
        stC = np.concatenate([inp["state_C"][c], inp["state_n"][c][..., None]], axis=-1)
        m = dict(xin=np.ascontiguousarray(xin, dtype=np.float32), pvec=_pvec(c, inp),
                 ck=np.ascontiguousarray(inp["cache_k"][c].reshape(DEPTH, 256, 256)),
                 cv=np.ascontiguousarray(inp["cache_v"][c].reshape(DEPTH, 256, 256)),
                 stC=np.ascontiguousarray(stC, dtype=np.float32))
        m.update(shared)
        m.update(consts)
        maps.append(m)
    return maps


_CACHE = {}


def kernel(**inputs):
    if "nc" not in _CACHE:
        _CACHE["nc"] = Builder().build()
    nc = _CACHE["nc"]
    maps = make_in_maps(inputs)
    res = run_bass_kernel_spmd(nc, maps, core_ids=list(range(NCORES)))
    R = res.results
    y_prompt = np.zeros((16, 256, D), np.float32)
    y_sample = np.zeros((8, 1024, D), np.float32)
    new_k = np.zeros((16, DEPTH, 256, 4, 64), np.float32)
    new_v = np.zeros((16, DEPTH, 256, 4, 64), np.float32)
    new_C = np.zeros((16, DEPTH, 2, 4, 64, 64), np.float32)
    new_n = np.zeros((16, DEPTH, 2, 4, 64), np.float32)
    new_m = np.zeros((16, DEPTH, 2, 4), np.float32)
    for c in range(NCORES):
        r = R[c]
        y_prompt[2 * c] = r["y"][0:256]
        y_prompt[2 * c + 1] = r["y"][256:512]
        y_sample[c] = r["y"][512:]
        for s in range(2):
            new_k[2 * c + s] = r["newk"][s].reshape(DEPTH, 256, 4, 64)
            new_v[2 * c + s] = r["newv"][s].reshape(DEPTH, 256, 4, 64)
            new_C[2 * c + s] = r["newC"][s][..., 0:64]
            new_n[2 * c + s] = r["newC"][s][..., 64]
            new_m[2 * c + s] = r["newm"][s]
    return (y_prompt, y_sample, new_k, new_v, new_C, new_n, new_m)
```

```python
import contextlib
import numpy as np
import concourse.bass as bass
import concourse.mybir as mybir
from concourse.bass_utils import run_bass_kernel_spmd

F32 = mybir.dt.float32
BF16 = mybir.dt.bfloat16
AF = mybir.ActivationFunctionType
ALU = mybir.AluOpType
AX = mybir.AxisListType

NCORES = 8
D = 1024
T = 1536
NT = 3
TILES = 12
DEPTH = 2
EPS = 1e-6
IN_COLS = 5392
FFH = 2816
SEQS = [(0, 256, False), (256, 256, False), (512, 1024, True)]

_off = 0


def _alloc(n):
    global _off
    o = _off
    _off += n
    return o


PV_C = _alloc(16)
PV_BADA = [_alloc(48) for _ in range(DEPTH)]
PV_N1G = [_alloc(8) for _ in range(DEPTH)]
PV_N2G = [_alloc(8) for _ in range(DEPTH)]
PV_CONV = [_alloc(12) for _ in range(DEPTH)]
PV_GB = [_alloc(2) for _ in range(DEPTH)]
PV_M0 = [_alloc(1) for _ in range(DEPTH)]
PV_M0R = [_alloc(8) for _ in range(DEPTH)]
PV_GQK = [_alloc(128) for _ in range(DEPTH)]
PV_GM = [_alloc(256) for _ in range(DEPTH)]
PV_COS = _alloc(8 * 32)
PV_SIN = _alloc(8 * 32)
PV_ONE = _alloc(1)
PV_EPS = _alloc(1)
NP = _off


class Prog:
    ENG = ["pe", "act", "dve", "pool", "sp"]
    PERSIST = ("ps", "wb", "xT", "hT", "modT", "modd", "pv", "ident", "onesb", "zerob", "maskn", "sel", "bd", "dft256", "scT")
    BLK = {"pe": "tensor", "act": "scalar", "dve": "vector", "pool": "gpsimd", "sp": "sync"}

    def __init__(self, nc, stack, same_eng_sync=True):
        self.nc = nc
        self.stack = stack
        self.same = same_eng_sync
        self.prog = {e: [] for e in self.ENG}
        self.semh = {}
        self.cnt = {}
        self.waited = {e: {} for e in self.ENG}
        self.lastw = {}
        self.readers = {}
        self.outtoks = []
        self.lazy = None
        self.fresh_done = set()
        self.pe_pending = False
        for e in self.ENG:
            self._sem("e_" + e)

    def _sem(self, name):
        if name not in self.semh:
            self.semh[name] = self.stack.enter_context(self.nc.semaphore(name))
            self.cnt[name] = 0
        return self.semh[name]

    def _deps(self, eng, reads, writes):
        need = {}
        own = "e_" + eng

        def add(tok):
            if tok is None:
                return
            s, v = tok
            if need.get(s, 0) < v:
                need[s] = v

        for r in reads:
            add(self.lastw.get(r))
            if r.startswith("ps"):
                for s, v in self.readers.get(r, {}).items():
                    if s != own:
                        add((s, v))
        for w in writes:
            add(self.lastw.get(w))
            for s, v in self.readers.get(w, {}).items():
                add((s, v))
            if self.lazy is not None and w not in self.fresh_done and not w.startswith(self.PERSIST):
                self.fresh_done.add(w)
                for s, v in self.lazy.items():
                    add((s, v))
        own = "e_" + eng
        waits = []
        for s, v in need.items():
            if s == own and (eng == "pe" or not self.same):
                continue
            if self.waited[eng].get(s, 0) >= v:
                continue
            self.waited[eng][s] = v
            waits.append((s, v))
        return waits

    def _mark(self, tok, reads, writes):
        for r in reads:
            d = self.readers.setdefault(r, {})
            if d.get(tok[0], 0) < tok[1]:
                d[tok[0]] = tok[1]
        for w in writes:
            self.lastw[w] = tok
            self.readers[w] = {}

    def op(self, eng, fn, reads=(), writes=(), inc=True):
        waits = self._deps(eng, reads, writes)
        own = "e_" + eng
        if inc:
            self.cnt[own] += 1
            tok = (own, self.cnt[own])
        else:
            tok = (own, self.cnt[own] + 1)
        if eng == "pe":
            self.pe_pending = not inc
        self.prog[eng].append((waits, fn, (own, 1) if inc else None))
        self._mark(tok, reads, writes)

    def dma(self, q, out, in_, reads=(), writes=(), sem=None, is_output=False):
        waits = self._deps(q, reads, writes)
        self._sem(sem)
        self.cnt[sem] += 16
        tok = (sem, self.cnt[sem])
        self.prog[q].append((waits, lambda e, o=out, i=in_: e.dma_start(out=o, in_=i), (sem, 16)))
        self._mark(tok, reads, writes)
        if is_output:
            self.outtoks.append(tok)

    def dma_multi(self, q, pieces, reads=(), writes=(), sem=None):
        waits = self._deps(q, reads, writes)
        self._sem(sem)
        for i, (out, in_) in enumerate(pieces):
            self.cnt[sem] += 16
            self.prog[q].append((waits if i == 0 else [], lambda e, o=out, i_=in_: e.dma_start(out=o, in_=i_), (sem, 16)))
        tok = (sem, self.cnt[sem])
        self._mark(tok, reads, writes)

    def barrier(self, hard=False):
        assert not self.pe_pending, "open PE group at a phase boundary"
        if not hard:
            self.lazy = {s: v for s, v in self.cnt.items() if v > 0 and not s.startswith("d_wb")}
            self.fresh_done = set()
            return
        for e in self.ENG:
            if e == "pool":
                continue
            waits = []
            for s, v in self.cnt.items():
                if v == 0:
                    continue
                if s.startswith("d_wb"):
                    continue
                if s == "e_pe":
                    pass
                if s == "e_" + e and e == "pe":
                    continue
                if self.waited[e].get(s, 0) >= v:
                    continue
                self.waited[e][s] = v
                waits.append((s, v))
            if waits:
                self.prog[e].append((waits, None, None))

    def finish(self):
        need = {}
        for s, v in self.outtoks:
            need[s] = max(need.get(s, 0), v)
        waits = [(s, v) for s, v in need.items()]
        self.prog["sp"].append((waits, None, None))

    def emit(self):
        nc = self.nc
        with nc.Block() as block:
            for e in self.ENG:
                def body(engh, e=e):
                    for waits, fn, inc in self.prog[e]:
                        for s, v in waits:
                            engh.wait_ge(self.semh[s], v)
                        if fn is not None:
                            ins = fn(engh)
                            if inc is not None:
                                ins.then_inc(self.semh[inc[0]], inc[1])
                getattr(block, self.BLK[e])(body)


class Builder:
    def __init__(self, stop_after=None, taps=()):
        self.stop_after = stop_after
        self.taps = set(taps)
        self.nc = bass.Bass("TRN2", target_bir_lowering=False)
        self.tapnames = []
        self.bank_i = 0
        self.wslot_i = 0

    def din(self, name, shape):
        return self.nc.dram_tensor(name, list(shape), F32, kind="ExternalInput").ap()

    def dout(self, name, shape):
        return self.nc.dram_tensor(name, list(shape), F32, kind="ExternalOutput").ap()

    def sb(self, st, name, shape, dt):
        self.uid = getattr(self, "uid", 0) + 1
        return st.enter_context(self.nc.sbuf_tensor("%s_u%d" % (name, self.uid), list(shape), dt))

    def bank(self):
        b = self.banks[self.bank_i % 8]
        r = "ps%d" % (self.bank_i % 8)
        self.bank_i += 1
        return b, r

    def tap(self, name, ap, region, dt=F32):
        if name not in self.taps:
            return
        d = self.nc.dram_tensor("tap_" + name, list(ap.shape), dt, kind="ExternalOutput").ap()
        self.tapnames.append("tap_" + name)
        self.P.dma("sp", d, ap, reads=[region], sem="tap", is_output=True)

    def build(self):
        nc = self.nc
        with contextlib.ExitStack() as st:
            self.st = st
            self.P = P = Prog(nc, st)
            self.declare_io()
            self.alloc_persistent(st)
            self.phase0()
            done = self.stop_after == "phase0"
            for l in range(DEPTH):
                if done:
                    break
                done = self.layer(l)
            if not done:
                self.final_out()
            P.finish()
            P.emit()
        return nc

    def declare_io(self):
        self.xin = self.din("xin", [T, D])
        self.pvec = self.din("pvec", [128, NP])
        self.ck = self.din("ck", [DEPTH, 256, 256])
        self.cv = self.din("cv", [DEPTH, 256, 256])
        self.stC = self.din("stC", [DEPTH, 2, 4, 64, 65])
        self.w_ada = self.din("w_ada", [DEPTH, D, 6 * D])
        self.w_in = self.din("w_in", [DEPTH, D, IN_COLS])
        self.w_pa = self.din("w_pa", [DEPTH, 512, D])
        self.w_pm = self.din("w_pm", [DEPTH, 256, D])
        self.w_pf = self.din("w_pf", [DEPTH, 256, D])
        self.w_out = self.din("w_out", [DEPTH, D, D])
        self.w_f1 = self.din("w_f1", [DEPTH, D, 2 * FFH])
        self.w_f2 = self.din("w_f2", [DEPTH, FFH, D])
        self.c_ident = self.din("c_ident", [128, 128])
        self.c_mask = self.din("c_mask", [128, 2, 128])
        self.c_sel = self.din("c_sel", [64, 8, 128])
        self.c_bd = self.din("c_bd", [128, 256])
        self.c_dft1k = self.din("c_dft1k", [2, 1024, 1024])
        self.c_dft256 = self.din("c_dft256", [2, 256, 256])
        self.y = self.dout("y", [T, D])
        self.newk = self.dout("newk", [2, DEPTH, 256, 256])
        self.newv = self.dout("newv", [2, DEPTH, 256, 256])
        self.newC = self.dout("newC", [2, DEPTH, 2, 4, 64, 65])
        self.newm = self.dout("newm", [2, DEPTH, 2, 4])

    def alloc_persistent(self, st):
        sb = self.sb
        self.xT = sb(st, "xT", [128, 8, T], F32)
        self.hT = sb(st, "hT", [128, 8, T], BF16)
        self.pv = sb(st, "pv", [128, NP], F32)
        self.identf = sb(st, "identf", [128, 128], F32)
        self.identb = sb(st, "identb", [128, 128], BF16)
        self.onesb = sb(st, "onesb", [128, 128], BF16)
        self.zerob = sb(st, "zerob", [128, 128], BF16)
        self.maskn = sb(st, "maskn", [128, 2, 128], F32)
        self.sel = sb(st, "sel", [64, 8, 128], F32)
        self.bd = sb(st, "bd", [128, 256], BF16)
        self.dft256 = sb(st, "dft256", [128, 2, 2, 256], BF16)
        self.modT = sb(st, "modT", [128, DEPTH, 48, 2], F32)
        self.modd = sb(st, "modd", [128, DEPTH, 2, 2, 8], F32)
        self.scT = sb(st, "scT", [128, 8, 2], BF16)
        self.wb = [sb(st, "wb%d" % i, [128, 4096], BF16) for i in range(4)]
        self.pinned = set()
        self.bigbanks = [st.enter_context(self.nc.psum_tensor("bankpair%d" % i, [128, 1024], F32)) for i in range(4)]
        self.banks = [self.bigbanks[i // 2][:, (i % 2) * 512:(i % 2 + 1) * 512] for i in range(8)]

    def wslot(self, pin=False):
        while True:
            i = self.wslot_i % 4
            self.wslot_i += 1
            if i not in self.pinned:
                break
        if pin:
            self.pinned.add(i)
        return self.wb[i], "wb%d" % i

    def unpin(self, r):
        self.pinned.discard(int(r[2:]))

    def phase0(self):
        P = self.P
        P.dma("sp", self.pv[:], self.pvec, writes=["pv"], sem="d_pv")
        P.dma("sp", self.identf[:], self.c_ident, writes=["identf"], sem="d_c0")
        P.dma("pool", self.identb[:], self.c_ident, writes=["identb"], sem="d_c1")
        P.dma("sp", self.maskn[:], self.c_mask, writes=["maskn"], sem="d_c2")
        P.dma("sp", self.sel[:], self.c_sel, writes=["sel"], sem="d_c3")
        P.dma("pool", self.bd[:], self.c_bd, writes=["bd"], sem="d_c4")
        P.dma("pool", self.dft256[:], self.c_dft256.rearrange("a (c p) t -> p a c t", p=128),
              writes=["dft256"], sem="d_c5")
        P.op("dve", lambda e: e.memset(self.onesb[:], 1.0), writes=["onesb"])
        P.op("dve", lambda e: e.memset(self.zerob[:], 0.0), writes=["zerob"])
        P.op("act", lambda e: e.activation(out=self.scT[:].rearrange("p a b -> p (a b)"),
                                           in_=self.pv[:, PV_C:PV_C + 16], func=AF.Silu),
             reads=["pv"], writes=["scT"])
        with contextlib.ExitStack() as ph:
            xtmp = [self.sb(ph, "xtmp%d" % i, [128, D], F32) for i in range(2)]
            for t in range(TILES):
                xt = xtmp[t % 2]
                rx = "xtmp%d" % (t % 2)
                P.dma("sp", xt[:], self.xin[t * 128:(t + 1) * 128, :], writes=[rx], sem="d_" + rx)
                for half in range(2):
                    bk, rb = self.bank()
                    for kk in range(4):
                        c = half * 4 + kk
                        P.op("pe", lambda e, bk=bk, kk=kk, c=c, xt=xt: e.transpose(
                            out=bk[:, kk * 128:(kk + 1) * 128], in_=xt[:, c * 128:(c + 1) * 128],
                            identity=self.identf[:]),
                            reads=[rx, "identf"], writes=[rb], inc=(kk == 3))
                    eng = "act" if half == 0 else "dve"
                    dst = self.xT[:, half * 4:(half + 1) * 4, t * 128:(t + 1) * 128]
                    src = bk[:].rearrange("p (a b) -> p a b", b=128)
                    if eng == "act":
                        P.op("act", lambda e, dst=dst, src=src: e.copy(out=dst, in_=src),
                             reads=[rb], writes=["xT%d" % (t // 4)])
                    else:
                        P.op("dve", lambda e, dst=dst, src=src: e.tensor_copy(out=dst, in_=src),
                             reads=[rb], writes=["xT%d" % (t // 4)])
            self.ada(0, [0, 1, 2, 3])
            self.ada_derive1(0)
            self.P.barrier()
        self.tap("xT", self.xT[:, :, 0:512], "xT0")
        self.tap("modT", self.modT[:], "modT0")

    def ada(self, l, slabs, fixed_bank=None):
        P = self.P
        for s in slabs:
            w, rw = self.wslot()
            wv = w[:, 0:4096].rearrange("p (k c) -> p k c", c=512)
            P.dma("pool", wv, self.w_ada[l, :, s * 512:(s + 1) * 512].rearrange("(k p) c -> p k c", p=128),
                  writes=[rw], sem="d_" + rw)
            bk, rb = self.bank() if fixed_bank is None else self.fbank(fixed_bank)
            for m in range(4):
                for k in range(8):
                    P.op("pe", lambda e, bk=bk, m=m, k=k, wv=wv: e.matmul(
                        bk[:, m * 2:m * 2 + 2], lhsT=wv[:, k, m * 128:(m + 1) * 128], rhs=self.scT[:, k, :],
                        start=(k == 0), stop=(k == 7)),
                        reads=[rw, "scT"], writes=[rb], inc=(m == 3 and k == 7))
            dst = self.modT[:, l, 4 * s:4 * s + 4, :]
            src = bk[:, 0:8].rearrange("p (a b) -> p a b", b=2)
            bia = self.pv[:, PV_BADA[l] + 4 * s:PV_BADA[l] + 4 * s + 4].unsqueeze(2).to_broadcast([128, 4, 2])
            P.op("dve", lambda e, dst=dst, src=src, bia=bia: e.tensor_tensor(out=dst, in0=src, in1=bia, op=ALU.add),
                 reads=[rb, "pv"], writes=["modT%d" % l])

    def ada_derive1(self, l):
        self.ada_derive(l, which=(0,))

    def ada_derive2(self, l):
        self.ada_derive(l, which=(1,))

    def ada_derive(self, l, which=(0, 1)):
        P = self.P
        for j in range(2):
            for i, (sc0, g) in enumerate([(8, PV_N1G[l]), (32, PV_N2G[l])]):
                if i not in which:
                    continue
                dst = self.modd[:, l, j, i, :]
                src = self.modT[:, l, sc0:sc0 + 8, j]
                gg = self.pv[:, g:g + 8]
                P.op("dve", lambda e, dst=dst, src=src, gg=gg: e.scalar_tensor_tensor(
                    out=dst, in0=src, scalar=1.0, in1=gg, op0=ALU.add, op1=ALU.mult),
                    reads=["modT%d" % l, "pv"], writes=["modd%d_%d" % (l, i)])

    def mod(self, l, j, what, k):
        if what == "s1":
            return self.modd[:, l, j, 0, k:k + 1]
        if what == "s2":
            return self.modd[:, l, j, 1, k:k + 1]
        base = {"shift1": 0, "gate1": 16, "shift2": 24, "gate2": 40}[what]
        return self.modT[:, l, base + k, j:j + 1]

    def layer(self, l):
        with contextlib.ExitStack() as lay:
            self.attnT = self.sb(lay, "attnT", [128, 4, T], BF16)
            self.attention_phase(l)
            self.tap("attnT%d" % l, self.attnT[:], "attnT0", BF16)
            if self.stop_after == "attn_%d" % l:
                return True
            self.hmT = self.sb(lay, "hmT", [128, 2, T], BF16)
            self.mlstm_phase(l)
            self.tap("hmT%d" % l, self.hmT[:], "hmT0", BF16)
            if self.stop_after == "mlstm_%d" % l:
                return True
            self.foT = self.sb(lay, "foT", [128, 2, T], BF16)
            self.fourier_phase(l)
            self.tap("foT%d" % l, self.foT[:], "foT0", BF16)
            if self.stop_after == "fourier_%d" % l:
                return True
            self.mergedT = self.sb(lay, "mergedT", [128, 8, T], BF16)
            self.merge_phase(l)
            self.tap("mergedT%d" % l, self.mergedT[:], "mergedT0", BF16)
            self.tap("xmid%d" % l, self.xT[:, :, 0:512], "xT0")
            if self.stop_after == "merge_%d" % l:
                return True
        self.ffn_phase(l)
        self.tap("xend%d" % l, self.xT[:, :, 0:512], "xT0")
        if self.stop_after == "ffn_%d" % l:
            return True
        return False

    def fourier_phase(self, l):
        P = self.P
        sb = self.sb
        foT = self.foT
        with contextlib.ExitStack() as ph:
            uT = sb(ph, "uT", [128, 2, T], BF16)
            ABt = sb(ph, "ABt", [128, 12, 512], BF16)
            w, rw = self.wslot()
            wv = w[:, 0:2048].rearrange("p (k c) -> p k c", c=256)
            P.dma("pool", wv, self.w_in[l, :, 2064:2320].rearrange("(k p) c -> p k c", p=128), writes=[rw], sem="d_" + rw)
            for c in range(2):
                for nt in range(NT):
                    tok = slice(nt * 512, (nt + 1) * 512)
                    bk, rb = self.bank()
                    for k in range(8):
                        P.op("pe", lambda e, bk=bk, k=k, c=c, tok=tok: e.matmul(
                            bk[:], lhsT=wv[:, k, c * 128:(c + 1) * 128], rhs=self.hT[:, k, tok], start=(k == 0), stop=(k == 7)),
                            reads=[rw, "hT%d" % nt], writes=[rb], inc=(k == 7))
                    P.op("act", lambda e, bk=bk, c=c, tok=tok: e.copy(out=uT[:, c, tok], in_=bk[:]),
                         reads=[rb], writes=["uT%d" % nt])
            for t in range(TILES):
                tok = slice(t * 128, (t + 1) * 128)
                bk, rb = self.bank()
                for c in range(2):
                    P.op("pe", lambda e, bk=bk, c=c, tok=tok: e.matmul(
                        bk[:, c * 256:(c + 1) * 256], lhsT=uT[:, c, tok], rhs=self.bd[:, 0:256], start=True, stop=True),
                        reads=["uT%d" % (t // 4), "bd"], writes=[rb], inc=(c == 1))
                if t % 2 == 0:
                    P.op("dve", lambda e, bk=bk, t=t: e.tensor_copy(out=ABt[:, t, :], in_=bk[:]), reads=[rb], writes=["ABt%d" % t])
                else:
                    P.op("act", lambda e, bk=bk, t=t: e.copy(out=ABt[:, t, :], in_=bk[:]), reads=[rb], writes=["ABt%d" % t])
            for seq in range(2):
                s0 = seq * 256
                for c in range(2):
                    bk, rb = self.bank()
                    n = 0
                    for cs in range(2):
                        for tc in range(2):
                            P.op("pe", lambda e, bk=bk, c=c, cs=cs, tc=tc, seq=seq, n=n: e.matmul(
                                bk[:, 0:256], lhsT=ABt[:, 2 * seq + tc, c * 256 + cs * 128:c * 256 + cs * 128 + 128],
                                rhs=self.dft256[:, cs, tc, :], start=(n == 0), stop=(n == 3)),
                                reads=["ABt%d" % (2 * seq + tc), "dft256"], writes=[rb], inc=(n == 3))
                            n += 1
                    P.op("act", lambda e, bk=bk, c=c, s0=s0: e.copy(out=foT[:, c, s0:s0 + 256], in_=bk[:, 0:256]),
                         reads=[rb], writes=["foT%d" % (2 * seq), "foT%d" % (2 * seq + 1)])
            for pc in range(2):
                Wm = []
                rWm = []
                for cs in range(2):
                    wsl, rws_ = self.wslot()
                    wview = wsl[:, :].rearrange("p (c t) -> p c t", t=512)
                    P.dma("pool", wview, self.c_dft1k[cs][:, pc * 512:(pc + 1) * 512].rearrange("(c p) t -> p c t", p=128),
                          writes=[rws_], sem="d_" + rws_)
                    Wm.append(wview)
                    rWm.append(rws_)
                for c in range(2):
                    bk, rb = self.bank()
                    n = 0
                    for cs in range(2):
                        for tc in range(8):
                            P.op("pe", lambda e, bk=bk, c=c, cs=cs, tc=tc, n=n, Wm=Wm: e.matmul(
                                bk[:], lhsT=ABt[:, 4 + tc, c * 256 + cs * 128:c * 256 + cs * 128 + 128],
                                rhs=Wm[cs][:, tc, :], start=(n == 0), stop=(n == 15)),
                                reads=["ABt%d" % (4 + tc), rWm[cs]], writes=[rb], inc=(n == 15))
                            n += 1
                    P.op("act", lambda e, bk=bk, c=c, pc=pc: e.copy(
                        out=foT[:, c, 512 + pc * 512:512 + (pc + 1) * 512], in_=bk[:]),
                        reads=[rb], writes=["foT%d" % (4 + 4 * pc + i) for i in range(4)])
            P.barrier()

    def merge_phase(self, l):
        P = self.P
        sb = self.sb
        mergedT = self.mergedT
        with contextlib.ExitStack() as ph:
            sg = [sb(ph, "sg%d" % i, [128, 512], F32) for i in range(3)]
            m1 = sb(ph, "mg1", [128, 512], F32)
            m2 = sb(ph, "mg2", [128, 512], F32)
            wp1, rwp1 = self.wslot(pin=True)
            wp2, rwp2 = self.wslot(pin=True)
            wpa = wp1[:, 0:4096].rearrange("p (k c) -> p k c", c=1024)
            wpm = wp2[:, 0:2048].rearrange("p (k c) -> p k c", c=1024)
            wpf = wp2[:, 2048:4096].rearrange("p (k c) -> p k c", c=1024)
            P.dma("pool", wpa, self.w_pa[l].rearrange("(k p) c -> p k c", p=128), writes=[rwp1], sem="d_" + rwp1)
            P.dma_multi("pool", [(wpm, self.w_pm[l].rearrange("(k p) c -> p k c", p=128)),
                                 (wpf, self.w_pf[l].rearrange("(k p) c -> p k c", p=128))],
                        writes=[rwp2], sem="d_" + rwp2)
            rwp = rwp1
            for j in range(8):
                gslot, rg = self.wslot()
                gv = gslot[:, 0:3072].rearrange("p (k c) -> p k c", c=384)
                P.dma_multi("pool", [(gv[:, :, gi * 128:(gi + 1) * 128],
                                      self.w_in[l, :, 2320 + gi * 1024 + j * 128:2320 + gi * 1024 + (j + 1) * 128].rearrange(
                                          "(k p) c -> p k c", p=128)) for gi in range(3)],
                            writes=[rg], sem="d_" + rg)
                if True:
                    jj = 0
                    for nt in range(NT):
                        tok = slice(nt * 512, (nt + 1) * 512)
                        pb = [self.bank() for _ in range(3)]
                        gb = [self.bank() for _ in range(3)]
                        srcs = [(wpa, 4, self.attnT, "attnT"), (wpm, 2, self.hmT, "hmT"), (wpf, 2, self.foT, "foT")]
                        for bi, (wmat, nk, act, rname) in enumerate(srcs):
                            bk, rb = pb[bi]
                            for k in range(nk):
                                P.op("pe", lambda e, bk=bk, k=k, wmat=wmat, act=act, j=j, tok=tok, nk=nk: e.matmul(
                                    bk[:], lhsT=wmat[:, k, j * 128:(j + 1) * 128], rhs=act[:, k, tok],
                                    start=(k == 0), stop=(k == nk - 1)),
                                    reads=[rwp1, rwp2] + ["%s%d" % (rname, 4 * nt + i) for i in range(4)], writes=[rb],
                                    inc=(k == nk - 1))
                        for gi in range(3):
                            bk, rb = gb[gi]
                            for k in range(8):
                                P.op("pe", lambda e, bk=bk, k=k, gi=gi, tok=tok, gv=gv: e.matmul(
                                    bk[:], lhsT=gv[:, k, gi * 128:(gi + 1) * 128], rhs=self.hT[:, k, tok],
                                    start=(k == 0), stop=(k == 7)),
                                    reads=[rg, "hT%d" % nt], writes=[rb], inc=(k == 7))
                            P.op("act", lambda e, bk=bk, gi=gi: e.activation(out=sg[gi][:], in_=bk[:], func=AF.Sigmoid),
                                 reads=[rb], writes=["sg%d" % gi])
                        P.op("dve", lambda e, b0=pb[0][0]: e.tensor_tensor(out=m1[:], in0=b0[:], in1=sg[0][:], op=ALU.mult),
                             reads=[pb[0][1], "sg0"], writes=["mg1"])
                        P.op("dve", lambda e, b1=pb[1][0]: e.tensor_tensor(out=m2[:], in0=b1[:], in1=sg[1][:], op=ALU.mult),
                             reads=[pb[1][1], "sg1"], writes=["mg2"])
                        P.op("dve", lambda e: e.tensor_tensor(out=m1[:], in0=m1[:], in1=m2[:], op=ALU.add),
                             reads=["mg1", "mg2"], writes=["mg1"])
                        P.op("dve", lambda e, b2=pb[2][0]: e.tensor_tensor(out=m2[:], in0=b2[:], in1=sg[2][:], op=ALU.mult),
                             reads=[pb[2][1], "sg2"], writes=["mg2"])
                        P.op("dve", lambda e, j=j, tok=tok: e.tensor_tensor(out=mergedT[:, j, tok], in0=m1[:], in1=m2[:], op=ALU.add),
                             reads=["mg1", "mg2"], writes=["mergedT%d" % nt])
            self.unpin(rwp1)
            self.unpin(rwp2)
            for j in range(8):
                if j % 4 == 0:
                    wo, rwo = self.wslot()
                    wov = wo[:, :].rearrange("p (k c) -> p k c", c=512)
                    P.dma("pool", wov, self.w_out[l][:, (j // 4) * 512:(j // 4 + 1) * 512].rearrange("(k p) c -> p k c", p=128),
                          writes=[rwo], sem="d_" + rwo)
                for nt in range(NT):
                    tok = slice(nt * 512, (nt + 1) * 512)
                    js = 0 if nt == 0 else 1
                    bk, rb = self.bank()
                    for k in range(8):
                        P.op("pe", lambda e, bk=bk, k=k, j=j, tok=tok, wov=wov: e.matmul(
                            bk[:], lhsT=wov[:, k, (j % 4) * 128:(j % 4 + 1) * 128], rhs=mergedT[:, k, tok], start=(k == 0), stop=(k == 7)),
                            reads=[rwo, "mergedT%d" % nt], writes=[rb], inc=(k == 7))
                    g1 = self.mod(l, js, "gate1", j)
                    P.op("dve", lambda e, bk=bk, j=j, tok=tok, g1=g1: e.scalar_tensor_tensor(
                        out=self.xT[:, j, tok], in0=bk[:], scalar=g1, in1=self.xT[:, j, tok], op0=ALU.mult, op1=ALU.add),
                        reads=[rb, "modT%d" % l, "xT%d" % nt], writes=["xT%d" % nt])
            P.barrier()

    def ffn_phase(self, l):
        P = self.P
        sb = self.sb
        with contextlib.ExitStack() as ph:
            hid = sb(ph, "hid", [128, 22, T], BF16)
            sl = [sb(ph, "fsl%d" % i, [128, 512], BF16) for i in range(2)]
            slabs = {}

            def load_slab(si):
                j0 = 2 * si
                w, rw = self.wslot()
                wv = w[:, :].rearrange("p (k c) -> p k c", c=512)
                P.dma_multi("pool", [
                    (wv[:, :, 0:256], self.w_f1[l, :, j0 * 128:(j0 + 2) * 128].rearrange("(k p) c -> p k c", p=128)),
                    (wv[:, :, 256:512],
                     self.w_f1[l, :, FFH + j0 * 128:FFH + (j0 + 2) * 128].rearrange("(k p) c -> p k c", p=128))],
                    writes=[rw], sem="d_" + rw)
                slabs[si] = (wv, rw)

            load_slab(0)
            load_slab(1)
            self.norm(l, 2, barrier=False, use_pool=False)
            it = 0
            for si in range(11):
                j0 = 2 * si
                if si + 2 < 11:
                    load_slab(si + 2)
                wv, rw = slabs[si]
                for jj in range(2):
                    j = j0 + jj
                    for nt in range(NT):
                        tok = slice(nt * 512, (nt + 1) * 512)
                        bg, rbg = self.bank()
                        bv, rbv = self.bank()
                        for k in range(8):
                            P.op("pe", lambda e, bg=bg, k=k, jj=jj, tok=tok, wv=wv: e.matmul(
                                bg[:], lhsT=wv[:, k, jj * 128:(jj + 1) * 128], rhs=self.hT[:, k, tok], start=(k == 0), stop=(k == 7)),
                                reads=[rw, "hT%d" % nt], writes=[rbg], inc=False)
                        for k in range(8):
                            P.op("pe", lambda e, bv=bv, k=k, jj=jj, tok=tok, wv=wv: e.matmul(
                                bv[:], lhsT=wv[:, k, 256 + jj * 128:256 + (jj + 1) * 128], rhs=self.hT[:, k, tok],
                                start=(k == 0), stop=(k == 7)),
                                reads=[rw, "hT%d" % nt], writes=[rbv], inc=(k == 7))
                        s_ = sl[it % 2]
                        rs_ = "fsl%d" % (it % 2)
                        it += 1
                        P.op("act", lambda e, s_=s_, bg=bg: e.activation(out=s_[:], in_=bg[:], func=AF.Silu),
                             reads=[rbg], writes=[rs_])
                        P.op("dve", lambda e, s_=s_, bv=bv, j=j, tok=tok: e.tensor_tensor(
                            out=hid[:, j, tok], in0=bv[:], in1=s_[:], op=ALU.mult),
                            reads=[rbv, rs_], writes=["hid%d" % nt])
            for j in range(8):
                w, rw = self.wslot()
                wv = w[:, 0:22 * 128].rearrange("p (k c) -> p k c", c=128)
                P.dma("pool", wv, self.w_f2[l, :, j * 128:(j + 1) * 128].rearrange("(k p) c -> p k c", p=128),
                      writes=[rw], sem="d_" + rw)
                for nt in range(NT):
                    tok = slice(nt * 512, (nt + 1) * 512)
                    js = 0 if nt == 0 else 1
                    bk, rb = self.bank()
                    for k in range(22):
                        P.op("pe", lambda e, bk=bk, k=k, tok=tok, wv=wv: e.matmul(
                            bk[:], lhsT=wv[:, k, :], rhs=hid[:, k, tok], start=(k == 0), stop=(k == 21)),
                            reads=[rw, "hid%d" % nt], writes=[rb], inc=(k == 21))
                    g2 = self.mod(l, js, "gate2", j)
                    P.op("dve", lambda e, bk=bk, j=j, tok=tok, g2=g2: e.scalar_tensor_tensor(
                        out=self.xT[:, j, tok], in0=bk[:], scalar=g2, in1=self.xT[:, j, tok], op0=ALU.mult, op1=ALU.add),
                        reads=[rb, "modT%d" % l, "xT%d" % nt], writes=["xT%d" % nt])
            P.barrier()

    def fbank(self, i):
        return self.banks[i], "ps%d" % i

    def mlstm_phase(self, l):
        P = self.P
        sb = self.sb
        hmT = self.hmT
        eps = self.pv[:, PV_EPS:PV_EPS + 1]
        with contextlib.ExitStack() as ph:
            qkmT = sb(ph, "qkmT", [128, 4, T], BF16)
            Vm = sb(ph, "Vm", [128, 12, 4, 65], BF16)
            sigom = sb(ph, "sigom", [128, 12, 256], BF16)
            Agt = sb(ph, "Agt", [64, T], F32)
            gtok = sb(ph, "gtok", [128, 12, 3, 64], F32)
            C0 = sb(ph, "C0", [128, 2, 2, 65], BF16)
            mfin = sb(ph, "mfin", [64, 2], F32)
            P.op("dve", lambda e: e.memset(Vm[:, :, :, 64:65], 1.0), writes=["Vm%d" % i for i in range(12)])
            P.op("dve", lambda e: e.memset(Agt[:], 0.0), writes=["Agt"])
            wgs, rwgs = self.wslot(pin=True)
            wg_raw = wgs[:, 0:128].rearrange("p (k c) -> p k c", c=16)
            wgi = wgs[:, 128:640].rearrange("p (k c) -> p k c", c=64)
            wgf = wgs[:, 640:1152].rearrange("p (k c) -> p k c", c=64)
            P.dma("pool", wg_raw, self.w_in[l, :, 2048:2064].rearrange("(k p) c -> p k c", p=128),
                  writes=[rwgs], sem="d_" + rwgs)
            P.op("dve", lambda e: e.memset(wgs[:, 128:1152], 0.0), reads=[rwgs], writes=[rwgs])
            for (dst, c0_, src0) in [(wgi, 0, 0), (wgi, 32, 8), (wgf, 0, 4), (wgf, 32, 12)]:
                P.op("dve", lambda e, dst=dst, c0_=c0_, src0=src0: e.tensor_copy(
                    out=dst[:, :, c0_:c0_ + 4], in_=wg_raw[:, :, src0:src0 + 4]),
                    reads=[rwgs], writes=[rwgs])

            with contextlib.ExitStack() as sub:
                pre2 = [sb(sub, "pre%d" % i, [128, T], F32) for i in range(2)]
                cvb2 = [sb(sub, "cvb%d" % i, [128, T], F32) for i in range(2)]
                C0s = sb(sub, "C0s", [128, 2, 2, 65], F32)
                for d in range(2):
                    P.dma("sp", C0s[:, d, :, :], self.stC[l, d].rearrange("(a b) k e -> (b k) a e", b=2),
                          writes=["C0s%d" % d], sem="d_C0%d" % d)
                    P.op("dve", lambda e, d=d: e.tensor_copy(out=C0[:, d, :, :], in_=C0s[:, d, :, :]),
                         reads=["C0s%d" % d], writes=["C0_%d" % d])
                w, rw = self.wslot()
                wv = w[:, 0:4096].rearrange("p (k c) -> p k c", c=512)
                P.dma("pool", wv, self.w_in[l, :, 1024:1536].rearrange("(k p) c -> p k c", p=128),
                      writes=[rw], sem="d_" + rw)
                def projA(c):
                    pre = pre2[c % 2]
                    cvb = cvb2[c % 2]
                    rpre = "pre%d" % (c % 2)
                    rcvb = "cvb%d" % (c % 2)
                    for nt in range(NT):
                        tok = slice(nt * 512, (nt + 1) * 512)
                        bk, rb = self.bank()
                        for k in range(8):
                            P.op("pe", lambda e, bk=bk, k=k, c=c, tok=tok: e.matmul(
                                bk[:], lhsT=wv[:, k, c * 128:(c + 1) * 128], rhs=self.hT[:, k, tok],
                                start=(k == 0), stop=(k == 7)),
                                reads=[rw, "hT%d" % nt], writes=[rb], inc=(k == 7))
                        P.op("act", lambda e, bk=bk, tok=tok, pre=pre: e.copy(out=pre[:, tok], in_=bk[:]),
                             reads=[rb], writes=[rpre])

                def convB(c):
                    pre = pre2[c % 2]
                    cvb = cvb2[c % 2]
                    rpre = "pre%d" % (c % 2)
                    rcvb = "cvb%d" % (c % 2)
                    cw = [self.pv[:, PV_CONV[l] + c * 3 + j:PV_CONV[l] + c * 3 + j + 1] for j in range(3)]
                    P.op("dve", lambda e, cw=cw, pre=pre, cvb=cvb: e.tensor_scalar(out=cvb[:], in0=pre[:], scalar1=cw[1], scalar2=None,
                                                                op0=ALU.mult),
                         reads=[rpre, "pv"], writes=[rcvb])
                    for (s0, Ts, _) in SEQS:
                        P.op("dve", lambda e, cw=cw, s0=s0, Ts=Ts, pre=pre, cvb=cvb: e.scalar_tensor_tensor(
                            out=cvb[:, s0 + 1:s0 + Ts], in0=pre[:, s0:s0 + Ts - 1], scalar=cw[0],
                            in1=cvb[:, s0 + 1:s0 + Ts], op0=ALU.mult, op1=ALU.add),
                            reads=[rpre, "pv", rcvb], writes=[rcvb])
                        P.op("dve", lambda e, cw=cw, s0=s0, Ts=Ts, pre=pre, cvb=cvb: e.scalar_tensor_tensor(
                            out=cvb[:, s0:s0 + Ts - 1], in0=pre[:, s0 + 1:s0 + Ts], scalar=cw[2],
                            in1=cvb[:, s0:s0 + Ts - 1], op0=ALU.mult, op1=ALU.add),
                            reads=[rpre, "pv", rcvb], writes=[rcvb])
                    P.op("act", lambda e, c=c, cvb=cvb: e.activation(out=qkmT[:, c, :], in_=cvb[:], func=AF.Silu),
                         reads=[rcvb], writes=["qkmT%d" % c])
                    if c >= 2:
                        P.op("dve", lambda e, c=c: e.tensor_scalar(out=qkmT[:, c, :], in0=qkmT[:, c, :], scalar1=0.125,
                                                                  scalar2=None, op0=ALU.mult),
                             reads=["qkmT%d" % c], writes=["qkmT%d" % c])

                projA(0)
                for c in range(4):
                    if c + 1 < 4:
                        projA(c + 1)
                    convB(c)
                P.barrier()
            self.tap("qkmT%d" % l, qkmT[:], "qkmT0", BF16)
            with contextlib.ExitStack() as sub:
                gi = sb(sub, "gi", [64, T], F32)
                gf = sb(sub, "gf", [64, T], F32)
                t1 = sb(sub, "t1", [64, T], F32)
                t2 = sb(sub, "t2", [64, T], F32)
                for (wg, rwg, gt, rgt, bcol) in [(wgi, rwgs, gi, "gi", 0), (wgf, rwgs, gf, "gf", 1)]:
                    for nt in range(NT):
                        tok = slice(nt * 512, (nt + 1) * 512)
                        bk, rb = self.bank()
                        for k in range(8):
                            P.op("pe", lambda e, bk=bk, k=k, wg=wg, tok=tok: e.matmul(
                                bk[0:64, :], lhsT=wg[:, k, :], rhs=self.hT[:, k, tok], start=(k == 0), stop=(k == 7)),
                                reads=[rwg, "hT%d" % nt], writes=[rb], inc=(k == 7))
                        bia = self.pv[0:64, PV_GB[l] + bcol:PV_GB[l] + bcol + 1]
                        P.op("dve", lambda e, bk=bk, gt=gt, tok=tok, bia=bia: e.tensor_scalar(
                            out=gt[:, tok], in0=bk[0:64, :], scalar1=bia, scalar2=None, op0=ALU.add),
                            reads=[rb, "pv"], writes=[rgt])
                self.unpin(rwgs)
                w2_, rw2 = self.wslot()
                wv2 = w2_[:, 0:4096].rearrange("p (k c) -> p k c", c=512)
                P.dma("pool", wv2, self.w_in[l, :, 1536:2048].rearrange("(k p) c -> p k c", p=128),
                      writes=[rw2], sem="d_" + rw2)
                for t in range(TILES):
                    tok = slice(t * 128, (t + 1) * 128)
                    bk, rb = self.bank()
                    for k in range(8):
                        P.op("pe", lambda e, bk=bk, k=k, tok=tok: e.matmul(
                            bk[:], lhsT=self.hT[:, k, tok], rhs=wv2[:, k, :], start=(k == 0), stop=(k == 7)),
                            reads=[rw2, "hT%d" % (t // 4)], writes=[rb], inc=(k == 7))
                    P.op("act", lambda e, bk=bk, t=t: e.copy(
                        out=Vm[:, t, :, 0:64], in_=bk[:, 0:256].rearrange("p (h d) -> p h d", d=64)),
                        reads=[rb], writes=["Vm%d" % t])
                    P.op("act", lambda e, bk=bk, t=t: e.activation(out=sigom[:, t, :], in_=bk[:, 256:512], func=AF.Sigmoid),
                         reads=[rb], writes=["sigom%d" % t])
                P.op("act", lambda e: e.activation(out=t1[:], in_=gf[:], func=AF.Abs), reads=["gf"], writes=["t1"])
                P.op("act", lambda e: e.activation(out=t1[:], in_=t1[:], func=AF.Exp, scale=-1.0), reads=["t1"], writes=["t1"])
                P.op("act", lambda e: e.activation(out=t1[:], in_=t1[:], func=AF.Ln, bias=self.pv[0:64, PV_ONE:PV_ONE + 1]),
                     reads=["t1", "pv"], writes=["t1"])
                P.op("dve", lambda e: e.tensor_scalar_min(out=t2[:], in0=gf[:], scalar1=0.0), reads=["gf"], writes=["t2"])
                P.op("dve", lambda e: e.tensor_tensor(out=gf[:], in0=t2[:], in1=t1[:], op=ALU.subtract),
                     reads=["t1", "t2"], writes=["gf"])

                def rsl(s0, Ts):
                    return slice(s0 + Ts - 1, (s0 - 1) if s0 > 0 else None, -1)

                def ones(p0, n):
                    return self.pv[p0:p0 + 4, PV_ONE:PV_ONE + 1].to_broadcast([4, n])

                for (s0, Ts, is_s) in SEQS:
                    P.op("dve", lambda e, s0=s0, Ts=Ts: e.tensor_tensor_scan(
                        out=t1[0:4, s0:s0 + Ts], data0=ones(0, Ts), data1=gf[0:4, s0:s0 + Ts], initial=0.0,
                        op0=ALU.mult, op1=ALU.add), reads=["gf", "pv"], writes=["t1"])
                    P.op("dve", lambda e, s0=s0, Ts=Ts: e.tensor_tensor_scan(
                        out=t1[32:36, rsl(s0, Ts)], data0=ones(32, Ts), data1=gf[32:36, rsl(s0, Ts)], initial=0.0,
                        op0=ALU.mult, op1=ALU.add), reads=["gf", "pv"], writes=["t1"])
                P.op("dve", lambda e: e.tensor_tensor(out=gi[:], in0=gi[:], in1=t1[:], op=ALU.subtract),
                     reads=["gi", "t1"], writes=["gi"])
                for (s0, Ts, is_s) in SEQS:
                    i0 = self.pv[0:4, PV_M0[l]:PV_M0[l] + 1] if is_s else 0.0
                    i1 = self.pv[32:36, PV_M0[l]:PV_M0[l] + 1] if is_s else 0.0
                    P.op("dve", lambda e, s0=s0, Ts=Ts, i0=i0: e.tensor_tensor_scan(
                        out=Agt[0:4, s0:s0 + Ts], data0=ones(0, Ts), data1=gi[0:4, s0:s0 + Ts], initial=i0,
                        op0=ALU.mult, op1=ALU.max), reads=["gi", "pv"], writes=["Agt"])
                    P.op("dve", lambda e, s0=s0, Ts=Ts, i1=i1: e.tensor_tensor_scan(
                        out=Agt[32:36, rsl(s0, Ts)], data0=ones(32, Ts), data1=gi[32:36, rsl(s0, Ts)], initial=i1,
                        op0=ALU.mult, op1=ALU.max), reads=["gi", "pv"], writes=["Agt"])
                P.op("dve", lambda e: e.scalar_tensor_tensor(out=t2[:], in0=t1[:], scalar=-1.0, in1=Agt[:],
                                                            op0=ALU.mult, op1=ALU.subtract),
                     reads=["t1", "Agt"], writes=["t2"])
                for seq in range(2):
                    s0, Ts, _ = SEQS[seq]
                    P.op("dve", lambda e, seq=seq, s0=s0, Ts=Ts: e.tensor_scalar(
                        out=mfin[0:4, seq:seq + 1], in0=t2[0:4, s0 + Ts - 1:s0 + Ts], scalar1=-1.0, scalar2=None,
                        op0=ALU.mult), reads=["t2"], writes=["mfin"])
                    P.op("dve", lambda e, seq=seq, s0=s0: e.tensor_scalar(
                        out=mfin[32:36, seq:seq + 1], in0=t2[32:36, s0:s0 + 1], scalar1=-1.0, scalar2=None,
                        op0=ALU.mult), reads=["t2"], writes=["mfin"])
                for seq in range(2):
                    P.dma("sp", self.newm[seq, l, 0, :].unsqueeze(1), mfin[0:4, seq:seq + 1], reads=["mfin"],
                          sem="d_om", is_output=True)
                    P.dma("sp", self.newm[seq, l, 1, :].unsqueeze(1), mfin[32:36, seq:seq + 1], reads=["mfin"],
                          sem="d_om", is_output=True)
                for t in range(TILES):
                    tok = slice(t * 128, (t + 1) * 128)
                    bk, rb = self.bank()
                    P.op("pe", lambda e, bk=bk, tok=tok: e.transpose(out=bk[:, 0:64], in_=gi[0:64, tok],
                                                                    identity=self.identf[0:64, 0:64]),
                         reads=["gi", "identf"], writes=[rb], inc=False)
                    P.op("pe", lambda e, bk=bk, tok=tok: e.transpose(out=bk[:, 64:128], in_=t2[0:64, tok],
                                                                    identity=self.identf[0:64, 0:64]),
                         reads=["t2", "identf"], writes=[rb])
                    P.op("dve", lambda e, bk=bk, t=t: e.tensor_copy(out=gtok[:, t, 0, :], in_=bk[:, 0:64]),
                         reads=[rb], writes=["gtok"])
                    P.op("act", lambda e, bk=bk, t=t: e.activation(out=gtok[:, t, 1, :], in_=bk[:, 64:128], func=AF.Exp),
                         reads=[rb], writes=["gtok"])
                    P.op("dve", lambda e, t=t: e.tensor_scalar(out=gtok[:, t, 2, :], in0=gtok[:, t, 0, :], scalar1=-1.0,
                                                              scalar2=None, op0=ALU.mult),
                         reads=["gtok"], writes=["gtok"])
                self.tap("Agt%d" % l, Agt[:], "Agt")
                self.tap("negmj%d" % l, t2[:], "t2")
                self.tap("agate%d" % l, gi[:], "gi")
                P.barrier()
            hsum = sb(ph, "hsum", [128, 12, 256], F32)
            m4 = contextlib.ExitStack()
            wt = [sb(m4, "wt%d" % i, [128, 512], BF16) for i in range(2)]
            ptm = [sb(m4, "ptm%d" % i, [128, 512], BF16) for i in range(2)]
            dtmp = sb(m4, "dtmp", [128, 128], F32)
            mask01 = sb(m4, "mask01", [128, 2, 128], BF16)
            P.op("dve", lambda e: e.tensor_scalar(out=mask01[:], in0=self.maskn[:], scalar1=1.0e-30, scalar2=1.0,
                                                  op0=ALU.mult, op1=ALU.add), reads=["maskn"], writes=["mask01"])
            wib = sb(m4, "wib", [128, 512], F32)
            pin = [sb(m4, "pin%d" % i, [128, 512], BF16) for i in range(2)]
            dn = sb(m4, "dn", [128, 4], F32)
            htmp = sb(m4, "htmp", [128, 4, 64], F32)
            kmtok = sb(m4, "kmtok", [128, 4, 256], BF16)
            nA = sb(m4, "nA", [128, 1], F32)
            wk = sb(m4, "wk", [128, 2], F32)
            kw = [sb(m4, "kw%d" % i, [128, 64], BF16) for i in range(2)]
            Cst = [sb(m4, "Cst", [64, 8, 65], F32)] * 2
            for t in range(4):
                tb_, rtb_ = self.bank()
                tbv = tb_[:].bitcast(BF16).rearrange("p (i c) -> p i c", c=128)
                for pr in range(2):
                    P.op("pe", lambda e, tbv=tbv, pr=pr, t=t: e.transpose(
                        out=tbv[:, pr, :], in_=qkmT[:, 2 + pr, t * 128:(t + 1) * 128], identity=self.identb[:]),
                        reads=["qkmT%d" % (2 + pr), "identb"], writes=[rtb_], inc=(pr == 1))
                P.op("act", lambda e, tbv=tbv, t=t: e.copy(out=kmtok[:, t, :].rearrange("p (a b) -> p a b", b=128),
                                                          in_=tbv[:, 0:2, :]),
                     reads=[rtb_], writes=["kmtok"])
            jobs = []
            for si, (s0, Ts, is_s) in enumerate(SEQS):
                for d in range(2):
                    for h in range(4):
                        nch = Ts // 128
                        J = dict(idx=len(jobs), si=si, s0=s0, Ts=Ts, is_s=is_s, d=d, h=h, nch=nch, tile0=s0 // 128,
                                 npc=max(1, Ts // 512), pw=min(512, Ts), lpb=min(nch, 4), nacc=(nch + 3) // 4,
                                 r=d * 32 + h, ri=d * 4 + h, hp=(h % 2) * 64, pr=h // 2)
                        pieces = []
                        for sc in range(nch):
                            l_lo, l_hi = (sc * 128, Ts) if d == 0 else (0, (sc + 1) * 128)
                            for pc in range(l_lo // 512, (l_hi + 511) // 512):
                                c0, c1 = max(l_lo, pc * 512), min(l_hi, (pc + 1) * 512)
                                pieces.append((sc, pc, c0, c1))
                        lastpv = {}
                        for (sc, pc, c0, c1) in pieces:
                            for lt in range(c0 // 128, c1 // 128):
                                lastpv[lt // J["lpb"]] = (sc, lt)
                        J["pieces"] = pieces
                        J["lastpv"] = lastpv
                        base = 2 if J["idx"] % 2 == 0 else 4
                        if is_s:
                            J["abk"] = [self.fbank(pc) for pc in range(J["npc"])]
                        else:
                            J["abk"] = [self.fbank(J["idx"] % 2)]
                        J["acc"] = [self.fbank(base + i) for i in range(J["nacc"])]
                        J["cb"] = self.fbank(base + 1)
                        jobs.append(J)
            pin4 = pin + [sb(m4, "pinx%d" % i, [128, 512], BF16) for i in range(2)]
            nA2 = [nA, sb(m4, "nAx", [128, 1], F32)]
            wk2 = [wk, sb(m4, "wkx", [128, 2], F32)]
            cnt = dict(k=0, pin=0)

            def begin(J):
                s0, Ts, d, h, nch, tile0 = J["s0"], J["Ts"], J["d"], J["h"], J["nch"], J["tile0"]
                npc, pw, lpb, r, ri, hp, pr = J["npc"], J["pw"], J["lpb"], J["r"], J["ri"], J["hp"], J["pr"]
                for pc in range(npc):
                    bk, rb = J["abk"][pc]
                    P.op("pe", lambda e, bk=bk, pc=pc, ri=ri, s0=s0, pw=pw: e.matmul(
                        bk[:, 0:pw], lhsT=self.sel[0:64, ri, :], rhs=Agt[0:64, s0 + pc * pw:s0 + (pc + 1) * pw],
                        start=True, stop=True), reads=["sel", "Agt"], writes=[rb])
                for (bk, rb) in J["acc"]:
                    P.op("pe", lambda e, bk=bk: e.matmul(bk[:], lhsT=self.zerob[:, 0:128], rhs=self.hT[:, 0, 0:512],
                                                        start=True, stop=False, skip_group_check=True),
                         reads=["zerob", "hT0"], writes=[rb])
                if J["is_s"]:
                    for pc in range(npc):
                        ab_, rab = J["abk"][pc]
                        m0r = self.pv[hp:hp + 64, PV_M0R[l] + ri:PV_M0R[l] + ri + 1]
                        P.op("act", lambda e, ab_=ab_, m0r=m0r, hp=hp, pw=pw: e.activation(
                            out=wib[hp:hp + 64, 0:pw], in_=ab_[hp:hp + 64, 0:pw], func=AF.Exp, bias=m0r, scale=-1.0),
                            reads=[rab, "pv"], writes=["wib"])
                        pn = pin4[cnt["pin"] % 4]
                        rpn = "pin%d" % (cnt["pin"] % 4)
                        cnt["pin"] += 1
                        P.op("dve", lambda e, pn=pn, hp=hp, pr=pr, s0=s0, pc=pc, pw=pw: e.tensor_tensor(
                            out=pn[hp:hp + 64, 0:pw], in0=qkmT[hp:hp + 64, pr, s0 + pc * 512:s0 + pc * 512 + pw],
                            in1=wib[hp:hp + 64, 0:pw], op=ALU.mult),
                            reads=["wib", "qkmT%d" % pr], writes=[rpn])
                        def init_mm(pc=pc, pn=pn, rpn=rpn):
                            for lt in range(pc * 4, pc * 4 + 4):
                                ab2, rab2 = J["acc"][lt // lpb]
                                col = (lt % lpb) * 65
                                P.op("pe", lambda e, ab2=ab2, col=col, pn=pn, lt=lt, hp=hp, d=d, pr=pr: e.matmul(
                                    ab2[:, col:col + 65], lhsT=pn[hp:hp + 64, (lt % 4) * 128:(lt % 4) * 128 + 128],
                                    rhs=C0[hp:hp + 64, d, pr, :], start=False, stop=False, skip_group_check=True),
                                    reads=[rpn, "C0_%d" % d], writes=[rab2], inc=(lt == pc * 4 + 3))
                        J.setdefault("deferred", []).append(init_mm)
                else:
                    si = J["si"]
                    colA = Ts - 1 if d == 0 else 0
                    ab_, rab = J["abk"][0]
                    nA_ = nA2[J["idx"] % 2]
                    rnA = "nA%d" % (J["idx"] % 2)
                    wk_ = wk2[J["idx"] % 2]
                    rwk = "wk%d" % (J["idx"] % 2)
                    P.op("act", lambda e, ab_=ab_, colA=colA, nA_=nA_: e.mul(
                        out=nA_[:, 0:1], in_=ab_[:, colA:colA + 1], mul=-1.0),
                        reads=[rab], writes=[rnA])
                    P.op("act", lambda e, tile0=tile0, nch=nch, r=r, nA_=nA_, wk_=wk_: e.activation(
                        out=wk_[:, 0:nch], in_=gtok[:, tile0:tile0 + nch, 0, r], func=AF.Exp, bias=nA_[:, 0:1], scale=1.0),
                        reads=[rnA, "gtok"], writes=[rwk])
                    cb, rcb = J["cb"]
                    for sc in range(nch):
                        kw_ = kw[sc % 2]
                        P.op("dve", lambda e, kw_=kw_, sc=sc, tile0=tile0, h=h, wk_=wk_: e.tensor_scalar(
                            out=kw_[:], in0=kmtok[:, tile0 + sc, h * 64:(h + 1) * 64], scalar1=wk_[:, sc:sc + 1],
                            scalar2=None, op0=ALU.mult), reads=["kmtok", rwk], writes=["kw%d" % (sc % 2)])

                    def final_state_mm():
                        for sc in range(nch):
                            kw_ = kw[sc % 2]
                            P.op("pe", lambda e, cb=cb, kw_=kw_, sc=sc, tile0=tile0, h=h, nch=nch: e.matmul(
                                cb[0:64, 0:65], lhsT=kw_[:], rhs=Vm[:, tile0 + sc, h, :], start=(sc == 0),
                                stop=(sc == nch - 1)), reads=["kw%d" % (sc % 2), "Vm%d" % (tile0 + sc)],
                                writes=[rcb], inc=(sc == nch - 1))
                        P.op("act", lambda e, cb=cb, si=si, ri=ri: e.copy(out=Cst[si][:, ri, :], in_=cb[0:64, 0:65]),
                             reads=[rcb], writes=["Cst"])
                        if d == 1 and h == 3:
                            P.dma("sp", self.newC[si, l].rearrange("d h k e -> k (d h) e"), Cst[si][:], reads=["Cst"],
                                  sem="d_oC", is_output=True)
                    J.setdefault("deferred", []).append(final_state_mm)

            def stage_a(J, piece):
                sc, pc, c0, c1 = piece
                s0, d, hp, pr, r = J["s0"], J["d"], J["hp"], J["pr"], J["r"]
                k = cnt["k"]
                cnt["k"] += 1
                n = c1 - c0
                stile = J["tile0"] + sc
                sbk, rsb = self.fbank(6 + k % 2)
                P.op("pe", lambda e, sbk=sbk, n=n, hp=hp, pr=pr, s0=s0, sc=sc, c0=c0, c1=c1: e.matmul(
                    sbk[:, 0:n], lhsT=qkmT[hp:hp + 64, 2 + pr, s0 + sc * 128:s0 + (sc + 1) * 128],
                    rhs=qkmT[hp:hp + 64, pr, s0 + c0:s0 + c1], start=True, stop=True),
                    reads=["qkmT%d" % (2 + pr), "qkmT%d" % pr], writes=[rsb])
                ab_, rab = J["abk"][pc]
                W = wt[k % 2]
                rW = "wt%d" % (k % 2)
                a_s = gtok[:, stile, 0, r:r + 1]
                dlo = sc * 128
                rngs = []
                if c0 <= dlo < c1:
                    if dlo > c0:
                        rngs.append((c0, dlo))
                    if dlo + 128 < c1:
                        rngs.append((dlo + 128, c1))
                    na_s = gtok[:, stile, 2, r:r + 1]
                    P.op("act", lambda e, ab_=ab_, dlo=dlo, pc=pc, na_s=na_s: e.activation(
                        out=dtmp[:], in_=ab_[:, dlo - pc * 512:dlo - pc * 512 + 128], func=AF.Relu, bias=na_s, scale=1.0),
                        reads=[rab, "gtok"], writes=["dtmp"])
                    P.op("act", lambda e, W=W, dlo=dlo, c0=c0: e.activation(
                        out=W[:, dlo - c0:dlo - c0 + 128], in_=dtmp[:], func=AF.Exp, scale=-1.0),
                        reads=["dtmp"], writes=[rW])
                else:
                    rngs.append((c0, c1))
                for (x0, x1) in rngs:
                    P.op("act", lambda e, W=W, x0=x0, x1=x1, c0=c0, pc=pc, ab_=ab_, a_s=a_s: e.activation(
                        out=W[:, x0 - c0:x1 - c0], in_=ab_[:, x0 - pc * 512:x1 - pc * 512], func=AF.Exp,
                        bias=a_s, scale=-1.0), reads=[rab, "gtok"], writes=[rW])
                pt = ptm[k % 2]
                rpt = "ptm%d" % (k % 2)
                P.op("dve", lambda e, pt=pt, sbk=sbk, W=W, n=n: e.tensor_tensor(
                    out=pt[:, 0:n], in0=sbk[:, 0:n], in1=W[:, 0:n], op=ALU.mult),
                    reads=[rsb, rW], writes=[rpt])
                if c0 <= dlo < c1:
                    P.op("dve", lambda e, pt=pt, dlo=dlo, c0=c0, d=d: e.tensor_tensor(
                        out=pt[:, dlo - c0:dlo - c0 + 128], in0=pt[:, dlo - c0:dlo - c0 + 128], in1=mask01[:, d, :],
                        op=ALU.mult), reads=[rpt, "mask01"], writes=[rpt])
                return (pt, rpt)

            def stage_b(J, piece, ptinfo, is_last_piece):
                sc, pc, c0, c1 = piece
                pt, rpt = ptinfo
                lpb, h = J["lpb"], J["h"]
                stile = J["tile0"] + sc
                lts = list(range(c0 // 128, c1 // 128))
                for lt in lts:
                    ab2, rab2 = J["acc"][lt // lpb]
                    col = (lt % lpb) * 65
                    last = J["lastpv"][lt // lpb] == (sc, lt)
                    P.op("pe", lambda e, ab2=ab2, col=col, pt=pt, lt=lt, c0=c0, stile=stile, h=h, last=last: e.matmul(
                        ab2[:, col:col + 65], lhsT=pt[:, lt * 128 - c0:lt * 128 - c0 + 128],
                        rhs=Vm[:, stile, h, :], start=False, stop=last, skip_group_check=True),
                        reads=[rpt, "Vm%d" % stile], writes=[rab2], inc=(is_last_piece and lt == lts[-1]))

            def end(J):
                d, h, nch, lpb, tile0, r = J["d"], J["h"], J["nch"], J["lpb"], J["tile0"], J["r"]
                for bi, (ab2, rab2) in enumerate(J["acc"]):
                    nl = min(lpb, nch - bi * lpb)
                    av = ab2[:, 0:nl * 65].rearrange("p (a b) -> p a b", b=65)
                    t0_ = tile0 + bi * lpb
                    P.op("dve", lambda e, av=av, nl=nl, t0_=t0_, r=r: e.tensor_tensor(
                        out=dn[:, 0:nl], in0=av[:, :, 64], in1=gtok[:, t0_:t0_ + nl, 1, r], op=ALU.max),
                        reads=[rab2, "gtok"], writes=["dn"])
                    P.op("dve", lambda e, av=av, nl=nl: e.scalar_tensor_tensor(
                        out=dn[:, 0:nl], in0=av[:, :, 64], scalar=-1.0, in1=dn[:, 0:nl], op0=ALU.mult, op1=ALU.max),
                        reads=[rab2, "dn"], writes=["dn"])
                    P.op("dve", lambda e, nl=nl: e.reciprocal(out=dn[:, 0:nl], in_=dn[:, 0:nl]),
                         reads=["dn"], writes=["dn"])
                    hs_ = hsum[:, t0_:t0_ + nl, h * 64:(h + 1) * 64]
                    dnb = dn[:, 0:nl].unsqueeze(2).to_broadcast([128, nl, 64])
                    if d == 0:
                        P.op("dve", lambda e, hs_=hs_, av=av, dnb=dnb: e.tensor_tensor(
                            out=hs_, in0=av[:, :, 0:64], in1=dnb, op=ALU.mult),
                            reads=[rab2, "dn"], writes=["hsum"])
                    else:
                        P.op("dve", lambda e, av=av, dnb=dnb, nl=nl: e.tensor_tensor(
                            out=htmp[:, 0:nl, :], in0=av[:, :, 0:64], in1=dnb, op=ALU.mult),
                            reads=[rab2, "dn"], writes=["htmp"])
                        P.op("dve", lambda e, hs_=hs_, nl=nl: e.tensor_tensor(
                            out=hs_, in0=hs_, in1=htmp[:, 0:nl, :], op=ALU.add),
                            reads=["htmp", "hsum"], writes=["hsum"])

            flat = []
            for J in jobs:
                for i, pc_ in enumerate(J["pieces"]):
                    flat.append((J, pc_, i == 0, i == len(J["pieces"]) - 1))
            prev = None
            for (J, pc_, first, last_) in flat:
                if first:
                    begin(J)
                    J["npiece"] = 0
                info = stage_a(J, pc_)
                J["npiece"] += 1
                if J["npiece"] == 2:
                    for f_ in J.get("deferred", []):
                        f_()
                    J["deferred"] = []
                if prev is not None:
                    pJ, ppc, pinfo, plast = prev
                    stage_b(pJ, ppc, pinfo, plast)
                    if plast:
                        end(pJ)
                prev = (J, pc_, info, last_)
            pJ, ppc, pinfo, plast = prev
            stage_b(pJ, ppc, pinfo, plast)
            end(pJ)
            self.tap("hsum%d" % l, hsum[:], "hsum")
            P.barrier()
            m4.close()
            sq2 = [sb(ph, "msq%d" % i, [128, 256], F32) for i in range(2)]
            s42 = [sb(ph, "ms4%d" % i, [128, 4], F32) for i in range(2)]
            hn2 = [sb(ph, "mhn%d" % i, [128, 256], F32) for i in range(2)]
            hmb = [sb(ph, "hmb%d" % i, [128, 256], BF16) for i in range(2)]

            def m5a(t):
                i2 = t % 2
                sq, s4, hn = sq2[i2], s42[i2], hn2[i2]
                rsq, rs4, rhn = "msq%d" % i2, "ms4%d" % i2, "mhn%d" % i2
                P.op("act", lambda e, t=t, sq=sq: e.activation(out=sq[:], in_=hsum[:, t, :], func=AF.Square),
                     reads=["hsum"], writes=[rsq])
                P.op("dve", lambda e, sq=sq, s4=s4: e.tensor_reduce(out=s4[:], in_=sq[:].rearrange("p (h d) -> p h d", d=64),
                                                                     axis=AX.X, op=ALU.add), reads=[rsq], writes=[rs4])
                P.op("act", lambda e, s4=s4: e.activation(out=s4[:], in_=s4[:], func=AF.Ln, scale=1.0 / 64, bias=eps),
                     reads=[rs4, "pv"], writes=[rs4])
                P.op("act", lambda e, s4=s4: e.activation(out=s4[:], in_=s4[:], func=AF.Exp, scale=-0.5),
                     reads=[rs4], writes=[rs4])
                P.op("dve", lambda e, t=t, hn=hn, s4=s4: e.tensor_tensor(
                    out=hn[:].rearrange("p (h d) -> p h d", d=64), in0=hsum[:, t, :].rearrange("p (h d) -> p h d", d=64),
                    in1=s4[:].unsqueeze(2).to_broadcast([128, 4, 64]), op=ALU.mult),
                    reads=["hsum", rs4], writes=[rhn])
                P.op("dve", lambda e, hn=hn: e.tensor_tensor(out=hn[:], in0=hn[:], in1=self.pv[:, PV_GM[l]:PV_GM[l] + 256], op=ALU.mult),
                     reads=[rhn, "pv"], writes=[rhn])
                hb = hmb[i2]
                rhb = "hmb%d" % i2
                P.op("dve", lambda e, hb=hb, t=t, hn=hn: e.tensor_tensor(out=hb[:], in0=hn[:], in1=sigom[:, t, :], op=ALU.mult),
                     reads=[rhn, "sigom%d" % t], writes=[rhb])

            def m5b(t):
                tok = slice(t * 128, (t + 1) * 128)
                hb = hmb[t % 2]
                rhb = "hmb%d" % (t % 2)
                tb_, rtb_ = self.bank()
                tbv = tb_[:].bitcast(BF16).rearrange("p (i c) -> p i c", c=128)
                for c2 in range(2):
                    P.op("pe", lambda e, tbv=tbv, c2=c2, hb=hb: e.transpose(
                        out=tbv[:, c2, :], in_=hb[:, c2 * 128:(c2 + 1) * 128], identity=self.identb[:]),
                        reads=[rhb, "identb"], writes=[rtb_], inc=(c2 == 1))
                P.op("act", lambda e, tbv=tbv, tok=tok: e.copy(out=hmT[:, :, tok], in_=tbv[:, 0:2, :]),
                     reads=[rtb_], writes=["hmT%d" % t])

            m5a(0)
            for t in range(TILES):
                if t + 1 < TILES:
                    m5a(t + 1)
                m5b(t)
            P.barrier()

    def attention_phase(self, l):
        P = self.P
        sb = self.sb
        attnT = self.attnT
        with contextlib.ExitStack() as ph:
            qT = sb(ph, "qT", [128, 4, T], BF16)
            kT2 = sb(ph, "kT2", [128, 4, T + 256], BF16)
            Vaug = sb(ph, "Vaug", [128, 14, 4, 65], BF16)
            atok = [sb(ph, "atok%d" % i, [128, 4, 512], BF16) for i in range(2)]
            rden = [sb(ph, "rden%d" % i, [128, 4], F32) for i in range(2)]
            wq_, rwq = self.wslot()
            wkv_, rwkv = self.wslot()
            wvq = wq_[:, :].rearrange("p (k c) -> p k c", c=512)
            wvkv = wkv_[:, :].rearrange("p (k c) -> p k c", c=512)
            P.dma("pool", wvq, self.w_in[l, :, 0:512].rearrange("(k p) c -> p k c", p=128), writes=[rwq], sem="d_" + rwq)
            P.dma("pool", wvkv, self.w_in[l, :, 512:1024].rearrange("(k p) c -> p k c", p=128), writes=[rwkv], sem="d_" + rwkv)
            self.norm(l, 1, barrier=False, use_pool=False)
            P.op("dve", lambda e: e.memset(Vaug[:, :, :, 64:65], 1.0), writes=["Vaug%d" % i for i in range(14)])
            with contextlib.ExitStack() as sub:
                sqqk = [sb(sub, "sqqk%d" % i, [128, 768], F32) for i in range(2)]
                qkn = [sb(sub, "qkn%d" % i, [128, 768], F32) for i in range(2)]
                qkr = [sb(sub, "qkr%d" % i, [128, 768], BF16) for i in range(2)]
                kdup = [sb(sub, "kdup%d" % i, [128, 4, 2, 64], BF16) for i in range(2)]
                ssq = [sb(sub, "ssq%d" % i, [128, 12], F32) for i in range(2)]
                vst = [sb(sub, "vst%d" % i, [128, 256], F32) for i in range(2)]
                rta = sb(sub, "rta", [128, 12, 2, 16], F32)
                rtb = sb(sub, "rtb", [128, 12, 2, 16], F32)
                kc = sb(sub, "kc", [128, 2, 256], F32)
                vcs = sb(sub, "vcs", [128, 2, 256], F32)
                P.dma("sp", kc[:], self.ck[l].rearrange("(c p) f -> p c f", p=128), writes=["kc"], sem="d_kc")
                P.dma("sp", vcs[:], self.cv[l].rearrange("(c p) f -> p c f", p=128), writes=["vcs"], sem="d_vc")
                for c in range(2):
                    P.op("act", lambda e, c=c: e.copy(out=Vaug[:, 4 + c, :, 0:64],
                                                      in_=vcs[:, c, :].rearrange("p (h d) -> p h d", d=64)),
                         reads=["vcs"], writes=["Vaug%d" % (4 + c)])
                eps = self.pv[:, PV_EPS:PV_EPS + 1]

                def ktranspose(kd, rkd, kcols):
                    tb_, rtb_ = self.bank()
                    tbv = tb_[:].bitcast(BF16).rearrange("p (i c) -> p i c", c=128)
                    for i in range(4):
                        P.op("pe", lambda e, tbv=tbv, i=i, kd=kd: e.transpose(
                            out=tbv[:, i, :], in_=kd[:, i, :, :].rearrange("p a d -> p (a d)"), identity=self.identb[:]),
                            reads=[rkd, "identb"], writes=[rtb_], inc=(i == 3))
                    P.op("dve", lambda e, tbv=tbv, kcols=kcols: e.tensor_copy(out=kT2[:, :, kcols], in_=tbv[:, 0:4, :]),
                         reads=[rtb_], writes=["kT2_%d" % (kcols.start // 128)])


                def stageA(t):
                    is_s = t >= 4
                    tok = slice(t * 128, (t + 1) * 128)
                    rh = "hT%d" % (t // 4)
                    bq, rq = self.bank()
                    bkv, rkv = self.bank()
                    for k in range(8):
                        P.op("pe", lambda e, bq=bq, k=k, tok=tok: e.matmul(
                            bq[:], lhsT=self.hT[:, k, tok], rhs=wvq[:, k, :], start=(k == 0), stop=(k == 7)),
                            reads=[rwq, rh], writes=[rq], inc=False)
                    for k in range(8):
                        P.op("pe", lambda e, bkv=bkv, k=k, tok=tok: e.matmul(
                            bkv[:], lhsT=self.hT[:, k, tok], rhs=wvkv[:, k, :], start=(k == 0), stop=(k == 7)),
                            reads=[rwkv, rh], writes=[rkv], inc=(k == 7))
                    vch = t if t < 4 else t + 2
                    P.op("act", lambda e, vch=vch, bkv=bkv: e.copy(
                        out=Vaug[:, vch, :, 0:64], in_=bkv[:, 256:512].rearrange("p (h d) -> p h d", d=64)),
                        reads=[rkv], writes=["Vaug%d" % vch])
                    i2 = t % 2
                    if not is_s:
                        seq = t // 2
                        r0 = (t % 2) * 128
                        P.op("act", lambda e, i2=i2, bkv=bkv: e.copy(out=vst[i2][:], in_=bkv[:, 256:512]),
                             reads=[rkv], writes=["vst%d" % i2])
                        P.dma("sp", self.newv[seq, l, r0:r0 + 128, :], vst[i2][:], reads=["vst%d" % i2],
                              sem="d_ov%d" % i2, is_output=True)
                    sqt = sqqk[i2]
                    rsq = "sqqk%d" % i2
                    P.op("act", lambda e, sqt=sqt, bq=bq: e.activation(out=sqt[:, 0:512], in_=bq[:], func=AF.Square),
                         reads=[rq], writes=[rsq])
                    P.op("act", lambda e, sqt=sqt, bkv=bkv: e.activation(out=sqt[:, 512:768], in_=bkv[:, 0:256], func=AF.Square),
                         reads=[rkv], writes=[rsq])
                    ss = ssq[i2]
                    rss = "ssq%d" % i2
                    P.op("dve", lambda e, ss=ss, sqt=sqt: e.tensor_reduce(
                        out=ss[:], in_=sqt[:].rearrange("p (h d) -> p h d", d=64), axis=AX.X, op=ALU.add),
                        reads=[rsq], writes=[rss])
                    P.op("act", lambda e, ss=ss: e.activation(out=ss[:], in_=ss[:], func=AF.Ln, scale=1.0 / 64, bias=eps),
                         reads=[rss, "pv"], writes=[rss])
                    P.op("act", lambda e, ss=ss: e.activation(out=ss[:], in_=ss[:], func=AF.Exp, scale=-0.5),
                         reads=[rss], writes=[rss])
                    qn = qkn[i2]
                    rqn = "qkn%d" % i2
                    P.op("dve", lambda e, qn=qn, bq=bq, ss=ss: e.tensor_tensor(
                        out=qn[:, 0:512].rearrange("p (h d) -> p h d", d=64), in0=bq[:].rearrange("p (h d) -> p h d", d=64),
                        in1=ss[:, 0:8].unsqueeze(2).to_broadcast([128, 8, 64]), op=ALU.mult),
                        reads=[rq, rss], writes=[rqn])
                    P.op("dve", lambda e, qn=qn, bkv=bkv, ss=ss: e.tensor_tensor(
                        out=qn[:, 512:768].rearrange("p (h d) -> p h d", d=64),
                        in0=bkv[:, 0:256].rearrange("p (h d) -> p h d", d=64),
                        in1=ss[:, 8:12].unsqueeze(2).to_broadcast([128, 4, 64]), op=ALU.mult),
                        reads=[rkv, rss], writes=[rqn])
                    gq = self.pv[:, PV_GQK[l]:PV_GQK[l] + 64].unsqueeze(1).to_broadcast([128, 8, 64])
                    gk = self.pv[:, PV_GQK[l] + 64:PV_GQK[l] + 128].unsqueeze(1).to_broadcast([128, 4, 64])
                    P.op("dve", lambda e, qn=qn, gq=gq: e.tensor_tensor(
                        out=qn[:, 0:512].rearrange("p (h d) -> p h d", d=64), in0=qn[:, 0:512].rearrange("p (h d) -> p h d", d=64),
                        in1=gq, op=ALU.mult), reads=[rqn, "pv"], writes=[rqn])
                    P.op("dve", lambda e, qn=qn, gk=gk: e.tensor_tensor(
                        out=qn[:, 512:768].rearrange("p (h d) -> p h d", d=64), in0=qn[:, 512:768].rearrange("p (h d) -> p h d", d=64),
                        in1=gk, op=ALU.mult), reads=[rqn, "pv"], writes=[rqn])
                    qr = qkr[i2]
                    rqr = "qkr%d" % i2
                    if not is_s:
                        P.dma("sp", self.newk[seq, l, r0:r0 + 128, :], qn[:, 512:768], reads=[rqn],
                              sem="d_ok%d" % i2, is_output=True)
                        P.op("act", lambda e, qr=qr, qn=qn: e.copy(out=qr[:], in_=qn[:]), reads=[rqn], writes=[rqr])
                    else:
                        ts = t - 4
                        v5 = qn[:].rearrange("p (h a b f) -> p h a b f", a=2, b=2, f=16)
                        o5 = qr[:].rearrange("p (h a b f) -> p h a b f", a=2, b=2, f=16)
                        X1, X2 = v5[:, :, :, 0, :], v5[:, :, :, 1, :]
                        O1, O2 = o5[:, :, :, 0, :], o5[:, :, :, 1, :]
                        cosv = self.pv[:, PV_COS + ts * 32:PV_COS + ts * 32 + 32].rearrange(
                            "p (a f) -> p a f", f=16).unsqueeze(1).to_broadcast([128, 12, 2, 16])
                        sinv = self.pv[:, PV_SIN + ts * 32:PV_SIN + ts * 32 + 32].rearrange(
                            "p (a f) -> p a f", f=16).unsqueeze(1).to_broadcast([128, 12, 2, 16])
                        rr = ["rta", "rtb"]
                        P.op("dve", lambda e, X1=X1, cosv=cosv: e.tensor_tensor(out=rta[:], in0=X1, in1=cosv, op=ALU.mult),
                             reads=[rqn, "pv"], writes=["rta"])
                        P.op("dve", lambda e, X2=X2, sinv=sinv: e.tensor_tensor(out=rtb[:], in0=X2, in1=sinv, op=ALU.mult),
                             reads=[rqn, "pv"], writes=["rtb"])
                        P.op("dve", lambda e, O1=O1: e.tensor_tensor(out=O1, in0=rta[:], in1=rtb[:], op=ALU.subtract),
                             reads=rr, writes=[rqr])
                        P.op("dve", lambda e, X2=X2, cosv=cosv: e.tensor_tensor(out=rta[:], in0=X2, in1=cosv, op=ALU.mult),
                             reads=[rqn, "pv", rqr], writes=["rta"])
                        P.op("dve", lambda e, X1=X1, sinv=sinv: e.tensor_tensor(out=rtb[:], in0=X1, in1=sinv, op=ALU.mult),
                             reads=[rqn, "pv", rqr], writes=["rtb"])
                        P.op("dve", lambda e, O2=O2: e.tensor_tensor(out=O2, in0=rta[:], in1=rtb[:], op=ALU.add),
                             reads=rr, writes=[rqr])
                    kd = kdup[i2]
                    rkd = "kdup%d" % i2
                    P.op("dve", lambda e, kd=kd, qr=qr: e.tensor_copy(
                        out=kd[:], in_=qr[:, 512:768].rearrange("p (h d) -> p h d", d=64).unsqueeze(2).to_broadcast([128, 4, 2, 64])),
                        reads=[rqr], writes=[rkd])
                    return (tok, qr, rqr, kd, rkd)

                def stageB(t, st_):
                    tok, qr, rqr, kd, rkd = st_
                    tb_, rtb_ = self.bank()
                    tbv = tb_[:].bitcast(BF16).rearrange("p (i c) -> p i c", c=128)
                    for i in range(4):
                        P.op("pe", lambda e, tbv=tbv, i=i, qr=qr: e.transpose(
                            out=tbv[:, i, :], in_=qr[:, i * 128:(i + 1) * 128], identity=self.identb[:]),
                            reads=[rqr, "identb"], writes=[rtb_], inc=(i == 3))
                    P.op("act", lambda e, tbv=tbv, tok=tok: e.copy(out=qT[:, :, tok], in_=tbv[:, 0:4, :]),
                         reads=[rtb_], writes=["qT%d" % t])
                    kcol0 = t * 128 if t < 4 else 768 + (t - 4) * 128
                    ktranspose(kd, rkd, slice(kcol0, kcol0 + 128))

                stA = {0: stageA(0)}
                for t in range(TILES):
                    if t + 1 < TILES:
                        stA[t + 1] = stageA(t + 1)
                    stageB(t, stA[t])
                for c in range(2):
                    kd = kdup[c % 2]
                    rkd = "kdup%d" % (c % 2)
                    P.op("dve", lambda e, kd=kd, c=c: e.tensor_copy(
                        out=kd[:], in_=kc[:, c, :].rearrange("p (h d) -> p h d", d=64).unsqueeze(2).to_broadcast([128, 4, 2, 64])),
                        reads=["kc"], writes=[rkd])
                    ktranspose(kd, rkd, slice(512 + c * 128, 512 + (c + 1) * 128))
                P.barrier()
            self.tap("qT%d" % l, qT[:], "qT0", BF16)
            self.tap("kT2%d" % l, kT2[:], "kT2_0", BF16)
            self.tap("Vaug%d" % l, Vaug[:], "Vaug0", BF16)
            if self.stop_after == "qkv_%d" % l:
                return
            PT = [sb(ph, "PT%d" % i, [128, 10, 512], BF16) for i in range(2)]
            state = dict(it=0, lh=0)

            iters = []
            for (q0, Tq, k0, nch, vc0, LB) in [(0, 256, 0, 2, 0, 256), (256, 256, 256, 2, 2, 256)]:
                for lh in range(Tq // LB):
                    grp = state["lh"]
                    state["lh"] += 1
                    for h in range(8):
                        iters.append(dict(q0=q0, k0=k0, nch=nch, vc0=vc0, LB=LB, lh=lh, h=h, grp=grp, idx=len(iters), pair=False))
            for lq in range(4):
                grp = state["lh"]
                state["lh"] += 1
                for g_ in range(4):
                    iters.append(dict(q0=512, k0=512, nch=10, vc0=4, LB=256, lh=lq, h=2 * g_ + 1, g=g_, grp=grp,
                                      idx=len(iters), pair=True))
            sbank_i = [0]

            def ptview(pt):
                return pt[:].rearrange("p a b -> p (a b)").rearrange("p (hh a c) -> p hh a c", hh=2, a=10)

            def stage_a_pair(itr):
                q0, k0, lq, g = itr["q0"], itr["k0"], itr["lh"], itr["g"]
                pi = itr["idx"] % 2
                ptv = ptview(PT[pi])
                rpt = "PT%d" % pi
                qreads = ["qT%d" % tt for tt in range((q0 + lq * 256) // 128, (q0 + (lq + 1) * 256) // 128)]
                for sc0 in range(0, 10, 2):
                    pp = sbank_i[0] % 3
                    sbank_i[0] += 1
                    big = self.bigbanks[pp]
                    bks = [self.fbank(2 * pp), self.fbank(2 * pp + 1)]
                    for u in range(2):
                        sc = sc0 + u
                        for hh in range(2):
                            bk, rb = bks[hh]
                            pb = hh * 64
                            P.op("pe", lambda e, bk=bk, u=u, sc=sc, g=g, pb=pb, lq=lq, k0=k0, q0=q0: e.matmul(
                                bk[:, u * 256:(u + 1) * 256],
                                lhsT=kT2[pb:pb + 64, g, k0 + sc * 128:k0 + (sc + 1) * 128],
                                rhs=qT[pb:pb + 64, g, q0 + lq * 256:q0 + (lq + 1) * 256], start=True, stop=True),
                                reads=qreads + ["kT2_%d" % ((k0 + sc * 128) // 128)], writes=[rb], inc=(u == 1))
                    P.op("act", lambda e, ptv=ptv, big=big, sc0=sc0: e.activation(
                        out=ptv[:, :, sc0:sc0 + 2, :], in_=big[:, 0:1024].rearrange("p (hh u c) -> p hh u c", hh=2, u=2),
                        func=AF.Exp, scale=0.125), reads=[bks[0][1], bks[1][1]], writes=[rpt])

            def stage_b_pair(itr):
                q0, vc0, lq, g = itr["q0"], itr["vc0"], itr["lh"], itr["g"]
                pi = itr["idx"] % 2
                ptv = ptview(PT[pi])
                rpt = "PT%d" % pi
                ai = itr["grp"] % 2
                ab = atok[ai]
                rab = "atok%d" % ai
                obk, rob = self.fbank(6 + itr["idx"] % 2)
                for hh in range(2):
                    for lt in range(2):
                        for sc in range(10):
                            P.op("pe", lambda e, obk=obk, hh=hh, lt=lt, sc=sc, ptv=ptv, g=g, vc0=vc0: e.matmul(
                                obk[:, (hh * 2 + lt) * 65:(hh * 2 + lt + 1) * 65], lhsT=ptv[:, hh, sc, lt * 128:(lt + 1) * 128],
                                rhs=Vaug[:, vc0 + sc, g, :], start=(sc == 0), stop=(sc == 9)),
                                reads=[rpt, "Vaug%d" % (vc0 + sc)], writes=[rob], inc=(hh == 1 and lt == 1 and sc == 9))
                ov = obk[:, 0:4 * 65].rearrange("p (a b) -> p a b", b=65)
                rd = rden[g % 2]
                rrd = "rden%d" % (g % 2)
                P.op("dve", lambda e, rd=rd, ov=ov: e.reciprocal(out=rd[:, 0:4], in_=ov[:, :, 64]), reads=[rob], writes=[rrd])
                for hh in range(2):
                    h = 2 * g + hh
                    P.op("dve", lambda e, rd=rd, ov=ov, ab=ab, h=h, hh=hh: e.tensor_tensor(
                        out=ab[:, 0:2, h * 64:(h + 1) * 64], in0=ov[:, hh * 2:hh * 2 + 2, 0:64],
                        in1=rd[:, hh * 2:hh * 2 + 2].unsqueeze(2).to_broadcast([128, 2, 64]), op=ALU.mult),
                        reads=[rob, rrd], writes=[rab])
                if g == 3:
                    for i in range(2):
                        tile = (q0 + lq * 256) // 128 + i
                        tb_, rtb_ = self.fbank(6 + (itr["idx"] + 1) % 2)
                        tbv = tb_[:].bitcast(BF16).rearrange("p (i c) -> p i c", c=128)
                        for c4 in range(4):
                            P.op("pe", lambda e, tbv=tbv, c4=c4, ab=ab, i=i: e.transpose(
                                out=tbv[:, c4, :], in_=ab[:, i, c4 * 128:(c4 + 1) * 128], identity=self.identb[:]),
                                reads=[rab, "identb"], writes=[rtb_], inc=(c4 == 3))
                        P.op("act", lambda e, tbv=tbv, tile=tile: e.copy(
                            out=attnT[:, :, tile * 128:(tile + 1) * 128], in_=tbv[:, 0:4, :]),
                            reads=[rtb_], writes=["attnT%d" % tile])

            def stage_a(itr):
                if itr["pair"]:
                    return stage_a_pair(itr)
                q0, k0, nch, LB, lh, h = itr["q0"], itr["k0"], itr["nch"], itr["LB"], itr["lh"], itr["h"]
                per = 512 // LB
                g = h // 2
                pb = (h % 2) * 64
                pi = itr["idx"] % 2
                pt = PT[pi]
                rpt = "PT%d" % pi
                for sc0 in range(0, nch, per):
                    sbk, rsb = self.fbank(sbank_i[0] % 6)
                    sbank_i[0] += 1
                    for u in range(per):
                        sc = sc0 + u
                        P.op("pe", lambda e, sbk=sbk, u=u, sc=sc, g=g, pb=pb, lh=lh, LB=LB, k0=k0, q0=q0: e.matmul(
                            sbk[:, u * LB:(u + 1) * LB],
                            lhsT=kT2[pb:pb + 64, g, k0 + sc * 128:k0 + (sc + 1) * 128],
                            rhs=qT[pb:pb + 64, g, q0 + lh * LB:q0 + (lh + 1) * LB], start=True, stop=True),
                            reads=["qT%d" % tt for tt in range((q0 + lh * LB) // 128, (q0 + (lh + 1) * LB) // 128)]
                            + ["kT2_%d" % ((k0 + sc * 128) // 128)], writes=[rsb], inc=(u == per - 1))
                    P.op("act", lambda e, pt=pt, sbk=sbk, sc0=sc0, per=per, LB=LB: e.activation(
                        out=pt[:, sc0:sc0 + per, 0:LB], in_=sbk[:, 0:per * LB].rearrange("p (a b) -> p a b", b=LB),
                        func=AF.Exp, scale=0.125),
                        reads=[rsb], writes=[rpt])

            def stage_b(itr):
                if itr["pair"]:
                    return stage_b_pair(itr)
                q0, nch, vc0, LB, lh, h = itr["q0"], itr["nch"], itr["vc0"], itr["LB"], itr["lh"], itr["h"]
                ntl = LB // 128
                g = h // 2
                pi = itr["idx"] % 2
                pt = PT[pi]
                rpt = "PT%d" % pi
                ai = itr["grp"] % 2
                ab = atok[ai]
                rab = "atok%d" % ai
                obk, rob = self.fbank(6 + itr["idx"] % 2)
                for lt in range(ntl):
                    for sc in range(nch):
                        P.op("pe", lambda e, obk=obk, lt=lt, sc=sc, pt=pt, g=g, vc0=vc0, nch=nch: e.matmul(
                            obk[:, lt * 65:(lt + 1) * 65], lhsT=pt[:, sc, lt * 128:(lt + 1) * 128],
                            rhs=Vaug[:, vc0 + sc, g, :], start=(sc == 0), stop=(sc == nch - 1)),
                            reads=[rpt, "Vaug%d" % (vc0 + sc)], writes=[rob], inc=(lt == ntl - 1 and sc == nch - 1))
                ov = obk[:, 0:ntl * 65].rearrange("p (a b) -> p a b", b=65)
                rd = rden[h % 2]
                rrd = "rden%d" % (h % 2)
                P.op("dve", lambda e, rd=rd, ov=ov, ntl=ntl: e.reciprocal(out=rd[:, 0:ntl], in_=ov[:, :, 64]),
                     reads=[rob], writes=[rrd])
                P.op("dve", lambda e, rd=rd, ov=ov, ab=ab, h=h, ntl=ntl: e.tensor_tensor(
                    out=ab[:, 0:ntl, h * 64:(h + 1) * 64], in0=ov[:, :, 0:64],
                    in1=rd[:, 0:ntl].unsqueeze(2).to_broadcast([128, ntl, 64]), op=ALU.mult),
                    reads=[rob, rrd], writes=[rab])
                if h == 7:
                    for i in range(ntl):
                        tile = (q0 + lh * LB) // 128 + i
                        tb_, rtb_ = self.fbank(6 + (itr["idx"] + 1) % 2)
                        tbv = tb_[:].bitcast(BF16).rearrange("p (i c) -> p i c", c=128)
                        for c4 in range(4):
                            P.op("pe", lambda e, tbv=tbv, c4=c4, ab=ab, i=i: e.transpose(
                                out=tbv[:, c4, :], in_=ab[:, i, c4 * 128:(c4 + 1) * 128], identity=self.identb[:]),
                                reads=[rab, "identb"], writes=[rtb_], inc=(c4 == 3))
                        P.op("act", lambda e, tbv=tbv, tile=tile: e.copy(
                            out=attnT[:, :, tile * 128:(tile + 1) * 128], in_=tbv[:, 0:4, :]),
                            reads=[rtb_], writes=["attnT%d" % tile])

            stage_a(iters[0])
            todo = []
            if l == 0:
                todo += [(0, s_) for s_ in range(4, 12)]
            if l + 1 < DEPTH:
                todo += [(l + 1, s_) for s_ in range(12)]
            for i in range(len(iters)):
                if i + 1 < len(iters):
                    stage_a(iters[i + 1])
                stage_b(iters[i])
                if todo and iters[i]["h"] != 7 and (i % 2 == 1 or len(todo) > 12):
                    ll, s_ = todo.pop(0)
                    self.ada(ll, [s_], fixed_bank=7)
            for (ll, s_) in todo:
                self.ada(ll, [s_], fixed_bank=7)
            if l == 0:
                self.ada_derive2(0)
            if l + 1 < DEPTH:
                self.ada_derive(l + 1)
            P.barrier()

    def norm(self, l, which, barrier=True, use_pool=False):
        P = self.P
        sname, shname = ("s1", "shift1") if which == 1 else ("s2", "shift2")
        with contextlib.ExitStack() as ph:
            sq = [self.sb(ph, "nsq%d" % i, [128, 512], BF16) for i in range(2)]
            tmp = [self.sb(ph, "ntmp%d" % i, [128, 512], F32) for i in range(2)]
            rs = [self.sb(ph, "nrs%d" % i, [128, 512], F32) for i in range(3)]
            bks = []
            for nt in range(NT):
                tok = slice(nt * 512, (nt + 1) * 512)
                rx = "xT%d" % nt
                bk, rb = self.bank()
                bks.append((bk, rb))
                for k in range(8):
                    s_ = sq[k % 2]
                    rs_ = "nsq%d" % (k % 2)
                    P.op("act", lambda e, s_=s_, k=k, tok=tok: e.activation(out=s_[:], in_=self.xT[:, k, tok], func=AF.Square),
                         reads=[rx], writes=[rs_])
                    P.op("pe", lambda e, bk=bk, s_=s_, k=k: e.matmul(bk[:], lhsT=self.onesb[:], rhs=s_[:],
                                                                    start=(k == 0), stop=(k == 7)),
                         reads=[rs_, "onesb"], writes=[rb])
            for nt in range(NT):
                j = 0 if nt == 0 else 1
                tok = slice(nt * 512, (nt + 1) * 512)
                rx = "xT%d" % nt
                bk, rb = bks[nt]
                r_ = rs[nt]
                rr_ = "nrs%d" % nt
                P.op("act", lambda e, bk=bk, r_=r_: e.activation(out=r_[:], in_=bk[:], func=AF.Ln, scale=1.0 / D,
                                                                bias=self.pv[:, PV_EPS:PV_EPS + 1]),
                     reads=[rb, "pv"], writes=[rr_])
                P.op("act", lambda e, r_=r_: e.activation(out=r_[:], in_=r_[:], func=AF.Exp, scale=-0.5),
                     reads=[rr_], writes=[rr_])
                for k in range(8):
                    t_ = tmp[k % 2]
                    rt_ = "ntmp%d" % (k % 2)
                    P.op("pool" if (use_pool and k % 2 == 1) else "dve",
                         lambda e, t_=t_, k=k, tok=tok, r_=r_: e.tensor_tensor(out=t_[:], in0=self.xT[:, k, tok], in1=r_[:], op=ALU.mult),
                         reads=[rx, rr_], writes=[rt_])
                    P.op("act", lambda e, t_=t_, k=k, tok=tok, j=j: e.activation(
                        out=self.hT[:, k, tok], in_=t_[:], func=AF.Identity,
                        scale=self.mod(l, j, sname, k), bias=self.mod(l, j, shname, k)),
                        reads=[rt_, "modd%d_%d" % (l, which - 1), "modT%d" % l], writes=["hT%d" % nt])
            if barrier:
                P.barrier()

    def final_out(self):
        P = self.P
        with contextlib.ExitStack() as ph:
            yst = [self.sb(ph, "yst%d" % i, [128, D], F32) for i in range(2)]
            for t in range(TILES):
                ys = yst[t % 2]
                ry = "yst%d" % (t % 2)
                for half in range(2):
                    bk, rb = self.bank()
                    for kk in range(4):
                        c = half * 4 + kk
                        P.op("pe", lambda e, bk=bk, kk=kk, c=c, t=t: e.transpose(
                            out=bk[:, kk * 128:(kk + 1) * 128], in_=self.xT[:, c, t * 128:(t + 1) * 128], identity=self.identf[:]),
                            reads=["xT%d" % (t // 4), "identf"], writes=[rb], inc=(kk == 3))
                    if half == 0:
                        P.op("act", lambda e, bk=bk, ys=ys: e.copy(out=ys[:, 0:512], in_=bk[:]), reads=[rb], writes=[ry])
                    else:
                        P.op("dve", lambda e, bk=bk, ys=ys: e.tensor_copy(out=ys[:, 512:1024], in_=bk[:]), reads=[rb], writes=[ry])
                P.dma("sp", self.y[t * 128:(t + 1) * 128, :], ys[:], reads=[ry], sem="d_" + ry, is_output=True)
            P.barrier()


def _consts():
    ident = np.eye(128, dtype=np.float32)
    s = np.arange(128)[:, None]
    l_ = np.arange(128)[None, :]
    NEG = -1.0e30
    mask = np.zeros((128, 2, 128), np.float32)
    mask[:, 0, :] = np.where(s <= l_, 0.0, NEG)
    mask[:, 1, :] = np.where(s >= l_, 0.0, NEG)
    sel = np.zeros((64, 8, 128), np.float32)
    for d in range(2):
        for h in range(4):
            sel[d * 32 + h, d * 4 + h, :] = 1.0
    c = np.arange(64)
    ang = 2.0 * np.pi * np.outer(c, c) / 64.0
    bd = np.zeros((128, 256), np.float64)
    for g in range(2):
        bd[g * 64:(g + 1) * 64, g * 64:(g + 1) * 64] = np.cos(ang) / 8.0
        bd[g * 64:(g + 1) * 64, 128 + g * 64:128 + (g + 1) * 64] = np.sin(ang) / 8.0
    def dft(n):
        t = np.arange(n)
        a = 2.0 * np.pi * ((np.outer(t, t)) % n) / n
        sc = 1.0 / np.sqrt(n)
        return np.stack([np.cos(a) * sc, -np.sin(a) * sc]).astype(np.float32)
    return dict(c_ident=ident, c_mask=mask, c_sel=sel, c_bd=bd.astype(np.float32),
                c_dft1k=dft(1024), c_dft256=dft(256))


def _chunked(v):
    return np.ascontiguousarray(v.reshape(-1, 128).T)


def _pvec(core, inp):
    pv = np.zeros((128, NP), np.float32)
    cc = np.stack([_chunked(inp["c_ctx"]), _chunked(inp["c"][core])], axis=-1)
    pv[:, PV_C:PV_C + 16] = cc.reshape(128, 16)
    for l in range(DEPTH):
        pv[:, PV_BADA[l]:PV_BADA[l] + 48] = _chunked(inp["b_ada"][l])
        pv[:, PV_N1G[l]:PV_N1G[l] + 8] = _chunked(inp["norm1_g"][l])
        pv[:, PV_N2G[l]:PV_N2G[l] + 8] = _chunked(inp["norm2_g"][l])
        cw = inp["m_conv_w"][l]
        pv[:, PV_CONV[l]:PV_CONV[l] + 12] = np.stack([_chunked(cw[j]) for j in range(3)], axis=-1).reshape(128, 12)
        gb = inp["m_gate_b"][l]
        pv[0:4, PV_GB[l]] = gb[0:4]
        pv[32:36, PV_GB[l]] = gb[8:12]
        pv[0:4, PV_GB[l] + 1] = gb[4:8]
        pv[32:36, PV_GB[l] + 1] = gb[12:16]
        sm = inp["state_m"][core, l]
        pv[0:4, PV_M0[l]] = sm[0]
        pv[32:36, PV_M0[l]] = sm[1]
        pv[:, PV_M0R[l]:PV_M0R[l] + 8] = sm.reshape(1, 8)
        pv[:, PV_GQK[l]:PV_GQK[l] + 64] = inp["q_norm_g"][l][None, :]
        pv[:, PV_GQK[l] + 64:PV_GQK[l] + 128] = inp["k_norm_g"][l][None, :]
        pv[:, PV_GM[l]:PV_GM[l] + 256] = inp["m_norm_g"][l][None, :]
    pos = np.arange(1024)
    row = (pos // 64).astype(np.float64)
    col = (pos % 64).astype(np.float64)
    inv = 1.0 / (10000.0 ** (np.arange(16, dtype=np.float64) / 16.0))
    ang = np.concatenate([row[:, None] * inv[None, :], col[:, None] * inv[None, :]], axis=1)
    cos = np.cos(ang).reshape(8, 128, 32).transpose(1, 0, 2).reshape(128, 256)
    sin = np.sin(ang).reshape(8, 128, 32).transpose(1, 0, 2).reshape(128, 256)
    pv[:, PV_COS:PV_COS + 256] = cos
    pv[:, PV_SIN:PV_SIN + 256] = sin
    pv[:, PV_ONE] = 1.0
    pv[:, PV_EPS] = EPS
    return pv


def make_in_maps(inp):
    inp = {k: np.asarray(v) for k, v in inp.items()}
    consts = _consts()
    shared = dict(w_ada=inp["w_ada"], w_in=inp["w_in"], w_pa=inp["w_proj_attn"], w_pm=inp["w_proj_mlstm"],
                  w_pf=inp["w_proj_fourier"], w_out=inp["w_out"], w_f1=inp["w_ffn_in"], w_f2=inp["w_ffn_out"])
    shared = {k: np.ascontiguousarray(v, dtype=np.float32) for k, v in shared.items()}
    maps = []
    for c in range(NCORES):
        xin = np.concatenate([inp["x_prompt"][2 * c], inp["x_prompt"][2 * c + 1], inp["x_sample"][c]], axis=0)
        stC = np.concatenate([inp["state_C"][c], inp["state_n"][c][..., None]], axis=-1)
        m = dict(xin=np.ascontiguousarray(xin, dtype=np.float32), pvec=_pvec(c, inp),
                 ck=np.ascontiguousarray(inp["cache_k"][c].reshape(DEPTH, 256, 256)),
                 cv=np.ascontiguousarray(inp["cache_v"][c].reshape(DEPTH, 256, 256)),
                 stC=np.ascontiguousarray(stC, dtype=np.float32))
        m.update(shared)
        m.update(consts)
        maps.append(m)
    return maps


_CACHE = {}


def kernel(**inputs):
    if "nc" not in _CACHE:
        _CACHE["nc"] = Builder().build()
    nc = _CACHE["nc"]
    maps = make_in_maps(inputs)
    res = run_bass_kernel_spmd(nc, maps, core_ids=list(range(NCORES)))
    R = res.results
    y_prompt = np.zeros((16, 256, D), np.float32)
    y_sample = np.zeros((8, 1024, D), np.float32)
    new_k = np.zeros((16, DEPTH, 256, 4, 64), np.float32)
    new_v = np.zeros((16, DEPTH, 256, 4, 64), np.float32)
    new_C = np.zeros((16, DEPTH, 2, 4, 64, 64), np.float32)
    new_n = np.zeros((16, DEPTH, 2, 4, 64), np.float32)
    new_m = np.zeros((16, DEPTH, 2, 4), np.float32)
    for c in range(NCORES):
        r = R[c]
        y_prompt[2 * c] = r["y"][0:256]
        y_prompt[2 * c + 1] = r["y"][256:512]
        y_sample[c] = r["y"][512:]
        for s in range(2):
            new_k[2 * c + s] = r["newk"][s].reshape(DEPTH, 256, 4, 64)
            new_v[2 * c + s] = r["newv"][s].reshape(DEPTH, 256, 4, 64)
            new_C[2 * c + s] = r["newC"][s][..., 0:64]
            new_n[2 * c + s] = r["newC"][s][..., 64]
            new_m[2 * c + s] = r["newm"][s]
    return (y_prompt, y_sample, new_k, new_v, new_C, new_n, new_m)
```

```python
import contextlib
import numpy as np
import concourse.bass as bass
import concourse.mybir as mybir
from concourse.bass_utils import run_bass_kernel_spmd

F32 = mybir.dt.float32
BF16 = mybir.dt.bfloat16
AF = mybir.ActivationFunctionType
ALU = mybir.AluOpType
AX = mybir.AxisListType

NCORES = 8
D = 1024
T = 1536
NT = 3
TILES = 12
DEPTH = 2
EPS = 1e-6
IN_COLS = 5392
FFH = 2816
SEQS = [(0, 256, False), (256, 256, False), (512, 1024, True)]

_off = 0


def _alloc(n):
    global _off
    o = _off
    _off += n
    return o


PV_C = _alloc(16)
PV_BADA = [_alloc(48) for _ in range(DEPTH)]
PV_N1G = [_alloc(8) for _ in range(DEPTH)]
PV_N2G = [_alloc(8) for _ in range(DEPTH)]
PV_CONV = [_alloc(12) for _ in range(DEPTH)]
PV_GB = [_alloc(2) for _ in range(DEPTH)]
PV_M0 = [_alloc(1) for _ in range(DEPTH)]
PV_M0R = [_alloc(8) for _ in range(DEPTH)]
PV_GQK = [_alloc(128) for _ in range(DEPTH)]
PV_GM = [_alloc(256) for _ in range(DEPTH)]
PV_COS = _alloc(8 * 32)
PV_SIN = _alloc(8 * 32)
PV_ONE = _alloc(1)
PV_EPS = _alloc(1)
NP = _off


class Prog:
    ENG = ["pe", "act", "dve", "pool", "sp"]
    PERSIST = ("ps", "wb", "xT", "hT", "modT", "modd", "pv", "ident", "onesb", "zerob", "maskn", "sel", "bd", "dft256", "scT")
    BLK = {"pe": "tensor", "act": "scalar", "dve": "vector", "pool": "gpsimd", "sp": "sync"}

    def __init__(self, nc, stack, same_eng_sync=True):
        self.nc = nc
        self.stack = stack
        self.same = same_eng_sync
        self.prog = {e: [] for e in self.ENG}
        self.semh = {}
        self.cnt = {}
        self.waited = {e: {} for e in self.ENG}
        self.lastw = {}
        self.readers = {}
        self.outtoks = []
        self.lazy = None
        self.fresh_done = set()
        self.pe_pending = False
        for e in self.ENG:
            self._sem("e_" + e)

    def _sem(self, name):
        if name not in self.semh:
            self.semh[name] = self.stack.enter_context(self.nc.semaphore(name))
            self.cnt[name] = 0
        return self.semh[name]

    def _deps(self, eng, reads, writes):
        need = {}
        own = "e_" + eng

        def add(tok):
            if tok is None:
                return
            s, v = tok
            if need.get(s, 0) < v:
                need[s] = v

        for r in reads:
            add(self.lastw.get(r))
            if r.startswith("ps"):
                for s, v in self.readers.get(r, {}).items():
                    if s != own:
                        add((s, v))
        for w in writes:
            add(self.lastw.get(w))
            for s, v in self.readers.get(w, {}).items():
                add((s, v))
            if self.lazy is not None and w not in self.fresh_done and not w.startswith(self.PERSIST):
                self.fresh_done.add(w)
                for s, v in self.lazy.items():
                    add((s, v))
        own = "e_" + eng
        waits = []
        for s, v in need.items():
            if s == own and (eng == "pe" or not self.same):
                continue
            if self.waited[eng].get(s, 0) >= v:
                continue
            self.waited[eng][s] = v
            waits.append((s, v))
        return waits

    def _mark(self, tok, reads, writes):
        for r in reads:
            d = self.readers.setdefault(r, {})
            if d.get(tok[0], 0) < tok[1]:
                d[tok[0]] = tok[1]
        for w in writes:
            self.lastw[w] = tok
            self.readers[w] = {}

    def op(self, eng, fn, reads=(), writes=(), inc=True):
        waits = self._deps(eng, reads, writes)
        own = "e_" + eng
        if inc:
            self.cnt[own] += 1
            tok = (own, self.cnt[own])
        else:
            tok = (own, self.cnt[own] + 1)
        if eng == "pe":
            self.pe_pending = not inc
        self.prog[eng].append((waits, fn, (own, 1) if inc else None))
        self._mark(tok, reads, writes)

    def dma(self, q, out, in_, reads=(), writes=(), sem=None, is_output=False):
        waits = self._deps(q, reads, writes)
        self._sem(sem)
        self.cnt[sem] += 16
        tok = (sem, self.cnt[sem])
        self.prog[q].append((waits, lambda e, o=out, i=in_: e.dma_start(out=o, in_=i), (sem, 16)))
        self._mark(tok, reads, writes)
        if is_output:
            self.outtoks.append(tok)

    def dma_multi(self, q, pieces, reads=(), writes=(), sem=None):
        waits = self._deps(q, reads, writes)
        self._sem(sem)
        for i, (out, in_) in enumerate(pieces):
            self.cnt[sem] += 16
            self.prog[q].append((waits if i == 0 else [], lambda e, o=out, i_=in_: e.dma_start(out=o, in_=i_), (sem, 16)))
        tok = (sem, self.cnt[sem])
        self._mark(tok, reads, writes)

    def barrier(self, hard=False):
        assert not self.pe_pending, "open PE group at a phase boundary"
        if not hard:
            self.lazy = {s: v for s, v in self.cnt.items() if v > 0 and not s.startswith("d_wb")}
            self.fresh_done = set()
            return
        for e in self.ENG:
            if e == "pool":
                continue
            waits = []
            for s, v in self.cnt.items():
                if v == 0:
                    continue
                if s.startswith("d_wb"):
                    continue
                if s == "e_pe":
                    pass
                if s == "e_" + e and e == "pe":
                    continue
                if self.waited[e].get(s, 0) >= v:
                    continue
                self.waited[e][s] = v
                waits.append((s, v))
            if waits:
                self.prog[e].append((waits, None, None))

    def finish(self):
        need = {}
        for s, v in self.outtoks:
            need[s] = max(need.get(s, 0), v)
        waits = [(s, v) for s, v in need.items()]
        self.prog["sp"].append((waits, None, None))

    def emit(self):
        nc = self.nc
        with nc.Block() as block:
            for e in self.ENG:
                def body(engh, e=e):
                    for waits, fn, inc in self.prog[e]:
                        for s, v in waits:
                            engh.wait_ge(self.semh[s], v)
                        if fn is not None:
                            ins = fn(engh)
                            if inc is not None:
                                ins.then_inc(self.semh[inc[0]], inc[1])
                getattr(block, self.BLK[e])(body)


class Builder:
    def __init__(self, stop_after=None, taps=()):
        self.stop_after = stop_after
        self.taps = set(taps)
        self.nc = bass.Bass("TRN2", target_bir_lowering=False)
        self.tapnames = []
        self.bank_i = 0
        self.wslot_i = 0

    def din(self, name, shape):
        return self.nc.dram_tensor(name, list(shape), F32, kind="ExternalInput").ap()

    def dout(self, name, shape):
        return self.nc.dram_tensor(name, list(shape), F32, kind="ExternalOutput").ap()

    def sb(self, st, name, shape, dt):
        self.uid = getattr(self, "uid", 0) + 1
        return st.enter_context(self.nc.sbuf_tensor("%s_u%d" % (name, self.uid), list(shape), dt))

    def bank(self):
        b = self.banks[self.bank_i % 8]
        r = "ps%d" % (self.bank_i % 8)
        self.bank_i += 1
        return b, r

    def tap(self, name, ap, region, dt=F32):
        if name not in self.taps:
            return
        d = self.nc.dram_tensor("tap_" + name, list(ap.shape), dt, kind="ExternalOutput").ap()
        self.tapnames.append("tap_" + name)
        self.P.dma("sp", d, ap, reads=[region], sem="tap", is_output=True)

    def build(self):
        nc = self.nc
        with contextlib.ExitStack() as st:
            self.st = st
            self.P = P = Prog(nc, st)
            self.declare_io()
            self.alloc_persistent(st)
            self.phase0()
            done = self.stop_after == "phase0"
            for l in range(DEPTH):
                if done:
                    break
                done = self.layer(l)
            if not done:
                self.final_out()
            P.finish()
            P.emit()
        return nc

    def declare_io(self):
        self.xin = self.din("xin", [T, D])
        self.pvec = self.din("pvec", [128, NP])
        self.ck = self.din("ck", [DEPTH, 256, 256])
        self.cv = self.din("cv", [DEPTH, 256, 256])
        self.stC = self.din("stC", [DEPTH, 2, 4, 64, 65])
        self.w_ada = self.din("w_ada", [DEPTH, D, 6 * D])
        self.w_in = self.din("w_in", [DEPTH, D, IN_COLS])
        self.w_pa = self.din("w_pa", [DEPTH, 512, D])
        self.w_pm = self.din("w_pm", [DEPTH, 256, D])
        self.w_pf = self.din("w_pf", [DEPTH, 256, D])
        self.w_out = self.din("w_out", [DEPTH, D, D])
        self.w_f1 = self.din("w_f1", [DEPTH, D, 2 * FFH])
        self.w_f2 = self.din("w_f2", [DEPTH, FFH, D])
        self.c_ident = self.din("c_ident", [128, 128])
        self.c_mask = self.din("c_mask", [128, 2, 128])
        self.c_sel = self.din("c_sel", [64, 8, 128])
        self.c_bd = self.din("c_bd", [128, 256])
        self.c_dft1k = self.din("c_dft1k", [2, 1024, 1024])
        self.c_dft256 = self.din("c_dft256", [2, 256, 256])
        self.y = self.dout("y", [T, D])
        self.newk = self.dout("newk", [2, DEPTH, 256, 256])
        self.newv = self.dout("newv", [2, DEPTH, 256, 256])
        self.newC = self.dout("newC", [2, DEPTH, 2, 4, 64, 65])
        self.newm = self.dout("newm", [2, DEPTH, 2, 4])

    def alloc_persistent(self, st):
        sb = self.sb
        self.xT = sb(st, "xT", [128, 8, T], F32)
        self.hT = sb(st, "hT", [128, 8, T], BF16)
        self.pv = sb(st, "pv", [128, NP], F32)
        self.identf = sb(st, "identf", [128, 128], F32)
        self.identb = sb(st, "identb", [128, 128], BF16)
        self.onesb = sb(st, "onesb", [128, 128], BF16)
        self.zerob = sb(st, "zerob", [128, 128], BF16)
        self.maskn = sb(st, "maskn", [128, 2, 128], F32)
        self.sel = sb(st, "sel", [64, 8, 128], F32)
        self.bd = sb(st, "bd", [128, 256], BF16)
        self.dft256 = sb(st, "dft256", [128, 2, 2, 256], BF16)
        self.modT = sb(st, "modT", [128, DEPTH, 48, 2], F32)
        self.modd = sb(st, "modd", [128, DEPTH, 2, 2, 8], F32)
        self.scT = sb(st, "scT", [128, 8, 2], BF16)
        self.wb = [sb(st, "wb%d" % i, [128, 4096], BF16) for i in range(4)]
        self.pinned = set()
        self.bigbanks = [st.enter_context(self.nc.psum_tensor("bankpair%d" % i, [128, 1024], F32)) for i in range(4)]
        self.banks = [self.bigbanks[i // 2][:, (i % 2) * 512:(i % 2 + 1) * 512] for i in range(8)]

    def wslot(self, pin=False):
        while True:
            i = self.wslot_i % 4
            self.wslot_i += 1
            if i not in self.pinned:
                break
        if pin:
            self.pinned.add(i)
        return self.wb[i], "wb%d" % i

    def unpin(self, r):
        self.pinned.discard(int(r[2:]))

    def phase0(self):
        P = self.P
        P.dma("sp", self.pv[:], self.pvec, writes=["pv"], sem="d_pv")
        P.dma("sp", self.identf[:], self.c_ident, writes=["identf"], sem="d_c0")
        P.dma("pool", self.identb[:], self.c_ident, writes=["identb"], sem="d_c1")
        P.dma("sp", self.maskn[:], self.c_mask, writes=["maskn"], sem="d_c2")
        P.dma("sp", self.sel[:], self.c_sel, writes=["sel"], sem="d_c3")
        P.dma("pool", self.bd[:], self.c_bd, writes=["bd"], sem="d_c4")
        P.dma("pool", self.dft256[:], self.c_dft256.rearrange("a (c p) t -> p a c t", p=128),
              writes=["dft256"], sem="d_c5")
        P.op("dve", lambda e: e.memset(self.onesb[:], 1.0), writes=["onesb"])
        P.op("dve", lambda e: e.memset(self.zerob[:], 0.0), writes=["zerob"])
        P.op("act", lambda e: e.activation(out=self.scT[:].rearrange("p a b -> p (a b)"),
                                           in_=self.pv[:, PV_C:PV_C + 16], func=AF.Silu),
             reads=["pv"], writes=["scT"])
        with contextlib.ExitStack() as ph:
            xtmp = [self.sb(ph, "xtmp%d" % i, [128, D], F32) for i in range(2)]
            for t in range(TILES):
                xt = xtmp[t % 2]
                rx = "xtmp%d" % (t % 2)
                P.dma("sp", xt[:], self.xin[t * 128:(t + 1) * 128, :], writes=[rx], sem="d_" + rx)
                for half in range(2):
                    bk, rb = self.bank()
                    for kk in range(4):
                        c = half * 4 + kk
                        P.op("pe", lambda e, bk=bk, kk=kk, c=c, xt=xt: e.transpose(
                            out=bk[:, kk * 128:(kk + 1) * 128], in_=xt[:, c * 128:(c + 1) * 128],
                            identity=self.identf[:]),
                            reads=[rx, "identf"], writes=[rb], inc=(kk == 3))
                    eng = "act" if half == 0 else "dve"
                    dst = self.xT[:, half * 4:(half + 1) * 4, t * 128:(t + 1) * 128]
                    src = bk[:].rearrange("p (a b) -> p a b", b=128)
                    if eng == "act":
                        P.op("act", lambda e, dst=dst, src=src: e.copy(out=dst, in_=src),
                             reads=[rb], writes=["xT%d" % (t // 4)])
                    else:
                        P.op("dve", lambda e, dst=dst, src=src: e.tensor_copy(out=dst, in_=src),
                             reads=[rb], writes=["xT%d" % (t // 4)])
            self.ada(0, [0, 1, 2, 3])
            self.ada_derive1(0)
            self.P.barrier()
        self.tap("xT", self.xT[:, :, 0:512], "xT0")
        self.tap("modT", self.modT[:], "modT0")

    def ada(self, l, slabs, fixed_bank=None):
        P = self.P
        for s in slabs:
            w, rw = self.wslot()
            wv = w[:, 0:4096].rearrange("p (k c) -> p k c", c=512)
            P.dma("pool", wv, self.w_ada[l, :, s * 512:(s + 1) * 512].rearrange("(k p) c -> p k c", p=128),
                  writes=[rw], sem="d_" + rw)
            bk, rb = self.bank() if fixed_bank is None else self.fbank(fixed_bank)
            for m in range(4):
                for k in range(8):
                    P.op("pe", lambda e, bk=bk, m=m, k=k, wv=wv: e.matmul(
                        bk[:, m * 2:m * 2 + 2], lhsT=wv[:, k, m * 128:(m + 1) * 128], rhs=self.scT[:, k, :],
                        start=(k == 0), stop=(k == 7)),
                        reads=[rw, "scT"], writes=[rb], inc=(m == 3 and k == 7))
            dst = self.modT[:, l, 4 * s:4 * s + 4, :]
            src = bk[:, 0:8].rearrange("p (a b) -> p a b", b=2)
            bia = self.pv[:, PV_BADA[l] + 4 * s:PV_BADA[l] + 4 * s + 4].unsqueeze(2).to_broadcast([128, 4, 2])
            P.op("dve", lambda e, dst=dst, src=src, bia=bia: e.tensor_tensor(out=dst, in0=src, in1=bia, op=ALU.add),
                 reads=[rb, "pv"], writes=["modT%d" % l])

    def ada_derive1(self, l):
        self.ada_derive(l, which=(0,))

    def ada_derive2(self, l):
        self.ada_derive(l, which=(1,))

    def ada_derive(self, l, which=(0, 1)):
        P = self.P
        for j in range(2):
            for i, (sc0, g) in enumerate([(8, PV_N1G[l]), (32, PV_N2G[l])]):
                if i not in which:
                    continue
                dst = self.modd[:, l, j, i, :]
                src = self.modT[:, l, sc0:sc0 + 8, j]
                gg = self.pv[:, g:g + 8]
                P.op("dve", lambda e, dst=dst, src=src, gg=gg: e.scalar_tensor_tensor(
                    out=dst, in0=src, scalar=1.0, in1=gg, op0=ALU.add, op1=ALU.mult),
                    reads=["modT%d" % l, "pv"], writes=["modd%d_%d" % (l, i)])

    def mod(self, l, j, what, k):
        if what == "s1":
            return self.modd[:, l, j, 0, k:k + 1]
        if what == "s2":
            return self.modd[:, l, j, 1, k:k + 1]
        base = {"shift1": 0, "gate1": 16, "shift2": 24, "gate2": 40}[what]
        return self.modT[:, l, base + k, j:j + 1]

    def layer(self, l):
        with contextlib.ExitStack() as lay:
            self.attnT = self.sb(lay, "attnT", [128, 4, T], BF16)
            self.attention_phase(l)
            self.tap("attnT%d" % l, self.attnT[:], "attnT0", BF16)
            if self.stop_after == "attn_%d" % l:
                return True
            self.hmT = self.sb(lay, "hmT", [128, 2, T], BF16)
            self.mlstm_phase(l)
            self.tap("hmT%d" % l, self.hmT[:], "hmT0", BF16)
            if self.stop_after == "mlstm_%d" % l:
                return True
            self.foT = self.sb(lay, "foT", [128, 2, T], BF16)
            self.fourier_phase(l)
            self.tap("foT%d" % l, self.foT[:], "foT0", BF16)
            if self.stop_after == "fourier_%d" % l:
                return True
            self.mergedT = self.sb(lay, "mergedT", [128, 8, T], BF16)
            self.merge_phase(l)
            self.tap("mergedT%d" % l, self.mergedT[:], "mergedT0", BF16)
            self.tap("xmid%d" % l, self.xT[:, :, 0:512], "xT0")
            if self.stop_after == "merge_%d" % l:
                return True
        self.ffn_phase(l)
        self.tap("xend%d" % l, self.xT[:, :, 0:512], "xT0")
        if self.stop_after == "ffn_%d" % l:
            return True
        return False

    def fourier_phase(self, l):
        P = self.P
        sb = self.sb
        foT = self.foT
        with contextlib.ExitStack() as ph:
            uT = sb(ph, "uT", [128, 2, T], BF16)
            ABt = sb(ph, "ABt", [128, 12, 512], BF16)
            w, rw = self.wslot()
            wv = w[:, 0:2048].rearrange("p (k c) -> p k c", c=256)
            P.dma("pool", wv, self.w_in[l, :, 2064:2320].rearrange("(k p) c -> p k c", p=128), writes=[rw], sem="d_" + rw)
            for c in range(2):
                for nt in range(NT):
                    tok = slice(nt * 512, (nt + 1) * 512)
                    bk, rb = self.bank()
                    for k in range(8):
                        P.op("pe", lambda e, bk=bk, k=k, c=c, tok=tok: e.matmul(
                            bk[:], lhsT=wv[:, k, c * 128:(c + 1) * 128], rhs=self.hT[:, k, tok], start=(k == 0), stop=(k == 7)),
                            reads=[rw, "hT%d" % nt], writes=[rb], inc=(k == 7))
                    P.op("act", lambda e, bk=bk, c=c, tok=tok: e.copy(out=uT[:, c, tok], in_=bk[:]),
                         reads=[rb], writes=["uT%d" % nt])
            for t in range(TILES):
                tok = slice(t * 128, (t + 1) * 128)
                bk, rb = self.bank()
                for c in range(2):
                    P.op("pe", lambda e, bk=bk, c=c, tok=tok: e.matmul(
                        bk[:, c * 256:(c + 1) * 256], lhsT=uT[:, c, tok], rhs=self.bd[:, 0:256], start=True, stop=True),
                        reads=["uT%d" % (t // 4), "bd"], writes=[rb], inc=(c == 1))
                if t % 2 == 0:
                    P.op("dve", lambda e, bk=bk, t=t: e.tensor_copy(out=ABt[:, t, :], in_=bk[:]), reads=[rb], writes=["ABt%d" % t])
                else:
                    P.op("act", lambda e, bk=bk, t=t: e.copy(out=ABt[:, t, :], in_=bk[:]), reads=[rb], writes=["ABt%d" % t])
            for seq in range(2):
                s0 = seq * 256
                for c in range(2):
                    bk, rb = self.bank()
                    n = 0
                    for cs in range(2):
                        for tc in range(2):
                            P.op("pe", lambda e, bk=bk, c=c, cs=cs, tc=tc, seq=seq, n=n: e.matmul(
                                bk[:, 0:256], lhsT=ABt[:, 2 * seq + tc, c * 256 + cs * 128:c * 256 + cs * 128 + 128],
                                rhs=self.dft256[:, cs, tc, :], start=(n == 0), stop=(n == 3)),
                                reads=["ABt%d" % (2 * seq + tc), "dft256"], writes=[rb], inc=(n == 3))
                            n += 1
                    P.op("act", lambda e, bk=bk, c=c, s0=s0: e.copy(out=foT[:, c, s0:s0 + 256], in_=bk[:, 0:256]),
                         reads=[rb], writes=["foT%d" % (2 * seq), "foT%d" % (2 * seq + 1)])
            for pc in range(2):
                Wm = []
                rWm = []
                for cs in range(2):
                    wsl, rws_ = self.wslot()
                    wview = wsl[:, :].rearrange("p (c t) -> p c t", t=512)
                    P.dma("pool", wview, self.c_dft1k[cs][:, pc * 512:(pc + 1) * 512].rearrange("(c p) t -> p c t", p=128),
                          writes=[rws_], sem="d_" + rws_)
                    Wm.append(wview)
                    rWm.append(rws_)
                for c in range(2):
                    bk, rb = self.bank()
                    n = 0
                    for cs in range(2):
                        for tc in range(8):
                            P.op("pe", lambda e, bk=bk, c=c, cs=cs, tc=tc, n=n, Wm=Wm: e.matmul(
                                bk[:], lhsT=ABt[:, 4 + tc, c * 256 + cs * 128:c * 256 + cs * 128 + 128],
                                rhs=Wm[cs][:, tc, :], start=(n == 0), stop=(n == 15)),
                                reads=["ABt%d" % (4 + tc), rWm[cs]], writes=[rb], inc=(n == 15))
                            n += 1
                    P.op("act", lambda e, bk=bk, c=c, pc=pc: e.copy(
                        out=foT[:, c, 512 + pc * 512:512 + (pc + 1) * 512], in_=bk[:]),
                        reads=[rb], writes=["foT%d" % (4 + 4 * pc + i) for i in range(4)])
            P.barrier()

    def merge_phase(self, l):
        P = self.P
        sb = self.sb
        mergedT = self.mergedT
        with contextlib.ExitStack() as ph:
            sg = [sb(ph, "sg%d" % i, [128, 512], F32) for i in range(3)]
            m1 = sb(ph, "mg1", [128, 512], F32)
            m2 = sb(ph, "mg2", [128, 512], F32)
            wp1, rwp1 = self.wslot(pin=True)
            wp2, rwp2 = self.wslot(pin=True)
            wpa = wp1[:, 0:4096].rearrange("p (k c) -> p k c", c=1024)
            wpm = wp2[:, 0:2048].rearrange("p (k c) -> p k c", c=1024)
            wpf = wp2[:, 2048:4096].rearrange("p (k c) -> p k c", c=1024)
            P.dma("pool", wpa, self.w_pa[l].rearrange("(k p) c -> p k c", p=128), writes=[rwp1], sem="d_" + rwp1)
            P.dma_multi("pool", [(wpm, self.w_pm[l].rearrange("(k p) c -> p k c", p=128)),
                                 (wpf, self.w_pf[l].rearrange("(k p) c -> p k c", p=128))],
                        writes=[rwp2], sem="d_" + rwp2)
            rwp = rwp1
            for j in range(8):
                gslot, rg = self.wslot()
                gv = gslot[:, 0:3072].rearrange("p (k c) -> p k c", c=384)
                P.dma_multi("pool", [(gv[:, :, gi * 128:(gi + 1) * 128],
                                      self.w_in[l, :, 2320 + gi * 1024 + j * 128:2320 + gi * 1024 + (j + 1) * 128].rearrange(
                                          "(k p) c -> p k c", p=128)) for gi in range(3)],
                            writes=[rg], sem="d_" + rg)
                if True:
                    jj = 0
                    for nt in range(NT):
                        tok = slice(nt * 512, (nt + 1) * 512)
                        pb = [self.bank() for _ in range(3)]
                        gb = [self.bank() for _ in range(3)]
                        srcs = [(wpa, 4, self.attnT, "attnT"), (wpm, 2, self.hmT, "hmT"), (wpf, 2, self.foT, "foT")]
                        for bi, (wmat, nk, act, rname) in enumerate(srcs):
                            bk, rb = pb[bi]
                            for k in range(nk):
                                P.op("pe", lambda e, bk=bk, k=k, wmat=wmat, act=act, j=j, tok=tok, nk=nk: e.matmul(
                                    bk[:], lhsT=wmat[:, k, j * 128:(j + 1) * 128], rhs=act[:, k, tok],
                                    start=(k == 0), stop=(k == nk - 1)),
                                    reads=[rwp1, rwp2] + ["%s%d" % (rname, 4 * nt + i) for i in range(4)], writes=[rb],
                                    inc=(k == nk - 1))
                        for gi in range(3):
                            bk, rb = gb[gi]
                            for k in range(8):
                                P.op("pe", lambda e, bk=bk, k=k, gi=gi, tok=tok, gv=gv: e.matmul(
                                    bk[:], lhsT=gv[:, k, gi * 128:(gi + 1) * 128], rhs=self.hT[:, k, tok],
                                    start=(k == 0), stop=(k == 7)),
                                    reads=[rg, "hT%d" % nt], writes=[rb], inc=(k == 7))
                            P.op("act", lambda e, bk=bk, gi=gi: e.activation(out=sg[gi][:], in_=bk[:], func=AF.Sigmoid),
                                 reads=[rb], writes=["sg%d" % gi])
                        P.op("dve", lambda e, b0=pb[0][0]: e.tensor_tensor(out=m1[:], in0=b0[:], in1=sg[0][:], op=ALU.mult),
                             reads=[pb[0][1], "sg0"], writes=["mg1"])
                        P.op("dve", lambda e, b1=pb[1][0]: e.tensor_tensor(out=m2[:], in0=b1[:], in1=sg[1][:], op=ALU.mult),
                             reads=[pb[1][1], "sg1"], writes=["mg2"])
                        P.op("dve", lambda e: e.tensor_tensor(out=m1[:], in0=m1[:], in1=m2[:], op=ALU.add),
                             reads=["mg1", "mg2"], writes=["mg1"])
                        P.op("dve", lambda e, b2=pb[2][0]: e.tensor_tensor(out=m2[:], in0=b2[:], in1=sg[2][:], op=ALU.mult),
                             reads=[pb[2][1], "sg2"], writes=["mg2"])
                        P.op("dve", lambda e, j=j, tok=tok: e.tensor_tensor(out=mergedT[:, j, tok], in0=m1[:], in1=m2[:], op=ALU.add),
                             reads=["mg1", "mg2"], writes=["mergedT%d" % nt])
            self.unpin(rwp1)
            self.unpin(rwp2)
            for j in range(8):
                if j % 4 == 0:
                    wo, rwo = self.wslot()
                    wov = wo[:, :].rearrange("p (k c) -> p k c", c=512)
                    P.dma("pool", wov, self.w_out[l][:, (j // 4) * 512:(j // 4 + 1) * 512].rearrange("(k p) c -> p k c", p=128),
                          writes=[rwo], sem="d_" + rwo)
                for nt in range(NT):
                    tok = slice(nt * 512, (nt + 1) * 512)
                    js = 0 if nt == 0 else 1
                    bk, rb = self.bank()
                    for k in range(8):
                        P.op("pe", lambda e, bk=bk, k=k, j=j, tok=tok, wov=wov: e.matmul(
                            bk[:], lhsT=wov[:, k, (j % 4) * 128:(j % 4 + 1) * 128], rhs=mergedT[:, k, tok], start=(k == 0), stop=(k == 7)),
                            reads=[rwo, "mergedT%d" % nt], writes=[rb], inc=(k == 7))
                    g1 = self.mod(l, js, "gate1", j)
                    P.op("dve", lambda e, bk=bk, j=j, tok=tok, g1=g1: e.scalar_tensor_tensor(
                        out=self.xT[:, j, tok], in0=bk[:], scalar=g1, in1=self.xT[:, j, tok], op0=ALU.mult, op1=ALU.add),
                        reads=[rb, "modT%d" % l, "xT%d" % nt], writes=["xT%d" % nt])
            P.barrier()

    def ffn_phase(self, l):
        P = self.P
        sb = self.sb
        with contextlib.ExitStack() as ph:
            hid = sb(ph, "hid", [128, 22, T], BF16)
            sl = [sb(ph, "fsl%d" % i, [128, 512], BF16) for i in range(2)]
            slabs = {}

            def load_slab(si):
                j0 = 2 * si
                w, rw = self.wslot()
                wv = w[:, :].rearrange("p (k c) -> p k c", c=512)
                P.dma_multi("pool", [
                    (wv[:, :, 0:256], self.w_f1[l, :, j0 * 128:(j0 + 2) * 128].rearrange("(k p) c -> p k c", p=128)),
                    (wv[:, :, 256:512],
                     self.w_f1[l, :, FFH + j0 * 128:FFH + (j0 + 2) * 128].rearrange("(k p) c -> p k c", p=128))],
                    writes=[rw], sem="d_" + rw)
                slabs[si] = (wv, rw)

            load_slab(0)
            load_slab(1)
            self.norm(l, 2, barrier=False, use_pool=False)
            it = 0
            for si in range(11):
                j0 = 2 * si
                if si + 2 < 11:
                    load_slab(si + 2)
                wv, rw = slabs[si]
                for jj in range(2):
                    j = j0 + jj
                    for nt in range(NT):
                        tok = slice(nt * 512, (nt + 1) * 512)
                        bg, rbg = self.bank()
                        bv, rbv = self.bank()
                        for k in range(8):
                            P.op("pe", lambda e, bg=bg, k=k, jj=jj, tok=tok, wv=wv: e.matmul(
                                bg[:], lhsT=wv[:, k, jj * 128:(jj + 1) * 128], rhs=self.hT[:, k, tok], start=(k == 0), stop=(k == 7)),
                                reads=[rw, "hT%d" % nt], writes=[rbg], inc=False)
                        for k in range(8):
                            P.op("pe", lambda e, bv=bv, k=k, jj=jj, tok=tok, wv=wv: e.matmul(
                                bv[:], lhsT=wv[:, k, 256 + jj * 128:256 + (jj + 1) * 128], rhs=self.hT[:, k, tok],
                                start=(k == 0), stop=(k == 7)),
                                reads=[rw, "hT%d" % nt], writes=[rbv], inc=(k == 7))
                        s_ = sl[it % 2]
                        rs_ = "fsl%d" % (it % 2)
                        it += 1
                        P.op("act", lambda e, s_=s_, bg=bg: e.activation(out=s_[:], in_=bg[:], func=AF.Silu),
                             reads=[rbg], writes=[rs_])
                        P.op("dve", lambda e, s_=s_, bv=bv, j=j, tok=tok: e.tensor_tensor(
                            out=hid[:, j, tok], in0=bv[:], in1=s_[:], op=ALU.mult),
                            reads=[rbv, rs_], writes=["hid%d" % nt])
            for j in range(8):
                w, rw = self.wslot()
                wv = w[:, 0:22 * 128].rearrange("p (k c) -> p k c", c=128)
                P.dma("pool", wv, self.w_f2[l, :, j * 128:(j + 1) * 128].rearrange("(k p) c -> p k c", p=128),
                      writes=[rw], sem="d_" + rw)
                for nt in range(NT):
                    tok = slice(nt * 512, (nt + 1) * 512)
                    js = 0 if nt == 0 else 1
                    bk, rb = self.bank()
                    for k in range(22):
                        P.op("pe", lambda e, bk=bk, k=k, tok=tok, wv=wv: e.matmul(
                            bk[:], lhsT=wv[:, k, :], rhs=hid[:, k, tok], start=(k == 0), stop=(k == 21)),
                            reads=[rw, "hid%d" % nt], writes=[rb], inc=(k == 21))
                    g2 = self.mod(l, js, "gate2", j)
                    P.op("dve", lambda e, bk=bk, j=j, tok=tok, g2=g2: e.scalar_tensor_tensor(
                        out=self.xT[:, j, tok], in0=bk[:], scalar=g2, in1=self.xT[:, j, tok], op0=ALU.mult, op1=ALU.add),
                        reads=[rb, "modT%d" % l, "xT%d" % nt], writes=["xT%d" % nt])
            P.barrier()

    def fbank(self, i):
        return self.banks[i], "ps%d" % i

    def mlstm_phase(self, l):
        P = self.P
        sb = self.sb
        hmT = self.hmT
        eps = self.pv[:, PV_EPS:PV_EPS + 1]
        with contextlib.ExitStack() as ph:
            qkmT = sb(ph, "qkmT", [128, 4, T], BF16)
            Vm = sb(ph, "Vm", [128, 12, 4, 65], BF16)
            sigom = sb(ph, "sigom", [128, 12, 256], BF16)
            Agt = sb(ph, "Agt", [64, T], F32)
            gtok = sb(ph, "gtok", [128, 12, 3, 64], F32)
            C0 = sb(ph, "C0", [128, 2, 2, 65], BF16)
            mfin = sb(ph, "mfin", [64, 2], F32)
            P.op("dve", lambda e: e.memset(Vm[:, :, :, 64:65], 1.0), writes=["Vm%d" % i for i in range(12)])
            P.op("dve", lambda e: e.memset(Agt[:], 0.0), writes=["Agt"])
            wgs, rwgs = self.wslot(pin=True)
            wg_raw = wgs[:, 0:128].rearrange("p (k c) -> p k c", c=16)
            wgi = wgs[:, 128:640].rearrange("p (k c) -> p k c", c=64)
            wgf = wgs[:, 640:1152].rearrange("p (k c) -> p k c", c=64)
            P.dma("pool", wg_raw, self.w_in[l, :, 2048:2064].rearrange("(k p) c -> p k c", p=128),
                  writes=[rwgs], sem="d_" + rwgs)
            P.op("dve", lambda e: e.memset(wgs[:, 128:1152], 0.0), reads=[rwgs], writes=[rwgs])
            for (dst, c0_, src0) in [(wgi, 0, 0), (wgi, 32, 8), (wgf, 0, 4), (wgf, 32, 12)]:
                P.op("dve", lambda e, dst=dst, c0_=c0_, src0=src0: e.tensor_copy(
                    out=dst[:, :, c0_:c0_ + 4], in_=wg_raw[:, :, src0:src0 + 4]),
                    reads=[rwgs], writes=[rwgs])

            with contextlib.ExitStack() as sub:
                pre2 = [sb(sub, "pre%d" % i, [128, T], F32) for i in range(2)]
                cvb2 = [sb(sub, "cvb%d" % i, [128, T], F32) for i in range(2)]
                C0s = sb(sub, "C0s", [128, 2, 2, 65], F32)
                for d in range(2):
                    P.dma("sp", C0s[:, d, :, :], self.stC[l, d].rearrange("(a b) k e -> (b k) a e", b=2),
                          writes=["C0s%d" % d], sem="d_C0%d" % d)
                    P.op("dve", lambda e, d=d: e.tensor_copy(out=C0[:, d, :, :], in_=C0s[:, d, :, :]),
                         reads=["C0s%d" % d], writes=["C0_%d" % d])
                w, rw = self.wslot()
                wv = w[:, 0:4096].rearrange("p (k c) -> p k c", c=512)
                P.dma("pool", wv, self.w_in[l, :, 1024:1536].rearrange("(k p) c -> p k c", p=128),
                      writes=[rw], sem="d_" + rw)
                def projA(c):
                    pre = pre2[c % 2]
                    cvb = cvb2[c % 2]
                    rpre = "pre%d" % (c % 2)
                    rcvb = "cvb%d" % (c % 2)
                    for nt in range(NT):
                        tok = slice(nt * 512, (nt + 1) * 512)
                        bk, rb = self.bank()
                        for k in range(8):
                            P.op("pe", lambda e, bk=bk, k=k, c=c, tok=tok: e.matmul(
                                bk[:], lhsT=wv[:, k, c * 128:(c + 1) * 128], rhs=self.hT[:, k, tok],
                                start=(k == 0), stop=(k == 7)),
                                reads=[rw, "hT%d" % nt], writes=[rb], inc=(k == 7))
                        P.op("act", lambda e, bk=bk, tok=tok, pre=pre: e.copy(out=pre[:, tok], in_=bk[:]),
                             reads=[rb], writes=[rpre])

                def convB(c):
                    pre = pre2[c % 2]
                    cvb = cvb2[c % 2]
                    rpre = "pre%d" % (c % 2)
                    rcvb = "cvb%d" % (c % 2)
                    cw = [self.pv[:, PV_CONV[l] + c * 3 + j:PV_CONV[l] + c * 3 + j + 1] for j in range(3)]
                    P.op("dve", lambda e, cw=cw, pre=pre, cvb=cvb: e.tensor_scalar(out=cvb[:], in0=pre[:], scalar1=cw[1], scalar2=None,
                                                                op0=ALU.mult),
                         reads=[rpre, "pv"], writes=[rcvb])
                    for (s0, Ts, _) in SEQS:
                        P.op("dve", lambda e, cw=cw, s0=s0, Ts=Ts, pre=pre, cvb=cvb: e.scalar_tensor_tensor(
                            out=cvb[:, s0 + 1:s0 + Ts], in0=pre[:, s0:s0 + Ts - 1], scalar=cw[0],
                            in1=cvb[:, s0 + 1:s0 + Ts], op0=ALU.mult, op1=ALU.add),
                            reads=[rpre, "pv", rcvb], writes=[rcvb])
                        P.op("dve", lambda e, cw=cw, s0=s0, Ts=Ts, pre=pre, cvb=cvb: e.scalar_tensor_tensor(
                            out=cvb[:, s0:s0 + Ts - 1], in0=pre[:, s0 + 1:s0 + Ts], scalar=cw[2],
                            in1=cvb[:, s0:s0 + Ts - 1], op0=ALU.mult, op1=ALU.add),
                            reads=[rpre, "pv", rcvb], writes=[rcvb])
                    P.op("act", lambda e, c=c, cvb=cvb: e.activation(out=qkmT[:, c, :], in_=cvb[:], func=AF.Silu),
                         reads=[rcvb], writes=["qkmT%d" % c])
                    if c >= 2:
                        P.op("dve", lambda e, c=c: e.tensor_scalar(out=qkmT[:, c, :], in0=qkmT[:, c, :], scalar1=0.125,
                                                                  scalar2=None, op0=ALU.mult),
                             reads=["qkmT%d" % c], writes=["qkmT%d" % c])

                projA(0)
                for c in range(4):
                    if c + 1 < 4:
                        projA(c + 1)
                    convB(c)
                P.barrier()
            self.tap("qkmT%d" % l, qkmT[:], "qkmT0", BF16)
            with contextlib.ExitStack() as sub:
                gi = sb(sub, "gi", [64, T], F32)
                gf = sb(sub, "gf", [64, T], F32)
                t1 = sb(sub, "t1", [64, T], F32)
                t2 = sb(sub, "t2", [64, T], F32)
                for (wg, rwg, gt, rgt, bcol) in [(wgi, rwgs, gi, "gi", 0), (wgf, rwgs, gf, "gf", 1)]:
                    for nt in range(NT):
                        tok = slice(nt * 512, (nt + 1) * 512)
                        bk, rb = self.bank()
                        for k in range(8):
                            P.op("pe", lambda e, bk=bk, k=k, wg=wg, tok=tok: e.matmul(
                                bk[0:64, :], lhsT=wg[:, k, :], rhs=self.hT[:, k, tok], start=(k == 0), stop=(k == 7)),
                                reads=[rwg, "hT%d" % nt], writes=[rb], inc=(k == 7))
                        bia = self.pv[0:64, PV_GB[l] + bcol:PV_GB[l] + bcol + 1]
                        P.op("dve", lambda e, bk=bk, gt=gt, tok=tok, bia=bia: e.tensor_scalar(
                            out=gt[:, tok], in0=bk[0:64, :], scalar1=bia, scalar2=None, op0=ALU.add),
                            reads=[rb, "pv"], writes=[rgt])
                self.unpin(rwgs)
                w2_, rw2 = self.wslot()
                wv2 = w2_[:, 0:4096].rearrange("p (k c) -> p k c", c=512)
                P.dma("pool", wv2, self.w_in[l, :, 1536:2048].rearrange("(k p) c -> p k c", p=128),
                      writes=[rw2], sem="d_" + rw2)
                for t in range(TILES):
                    tok = slice(t * 128, (t + 1) * 128)
                    bk, rb = self.bank()
                    for k in range(8):
                        P.op("pe", lambda e, bk=bk, k=k, tok=tok: e.matmul(
                            bk[:], lhsT=self.hT[:, k, tok], rhs=wv2[:, k, :], start=(k == 0), stop=(k == 7)),
                            reads=[rw2, "hT%d" % (t // 4)], writes=[rb], inc=(k == 7))
                    P.op("act", lambda e, bk=bk, t=t: e.copy(
                        out=Vm[:, t, :, 0:64], in_=bk[:, 0:256].rearrange("p (h d) -> p h d", d=64)),
                        reads=[rb], writes=["Vm%d" % t])
                    P.op("act", lambda e, bk=bk, t=t: e.activation(out=sigom[:, t, :], in_=bk[:, 256:512], func=AF.Sigmoid),
                         reads=[rb], writes=["sigom%d" % t])
                P.op("act", lambda e: e.activation(out=t1[:], in_=gf[:], func=AF.Abs), reads=["gf"], writes=["t1"])
                P.op("act", lambda e: e.activation(out=t1[:], in_=t1[:], func=AF.Exp, scale=-1.0), reads=["t1"], writes=["t1"])
                P.op("act", lambda e: e.activation(out=t1[:], in_=t1[:], func=AF.Ln, bias=self.pv[0:64, PV_ONE:PV_ONE + 1]),
                     reads=["t1", "pv"], writes=["t1"])
                P.op("dve", lambda e: e.tensor_scalar_min(out=t2[:], in0=gf[:], scalar1=0.0), reads=["gf"], writes=["t2"])
                P.op("dve", lambda e: e.tensor_tensor(out=gf[:], in0=t2[:], in1=t1[:], op=ALU.subtract),
                     reads=["t1", "t2"], writes=["gf"])

                def rsl(s0, Ts):
                    return slice(s0 + Ts - 1, (s0 - 1) if s0 > 0 else None, -1)

                def ones(p0, n):
                    return self.pv[p0:p0 + 4, PV_ONE:PV_ONE + 1].to_broadcast([4, n])

                for (s0, Ts, is_s) in SEQS:
                    P.op("dve", lambda e, s0=s0, Ts=Ts: e.tensor_tensor_scan(
                        out=t1[0:4, s0:s0 + Ts], data0=ones(0, Ts), data1=gf[0:4, s0:s0 + Ts], initial=0.0,
                        op0=ALU.mult, op1=ALU.add), reads=["gf", "pv"], writes=["t1"])
                    P.op("dve", lambda e, s0=s0, Ts=Ts: e.tensor_tensor_scan(
                        out=t1[32:36, rsl(s0, Ts)], data0=ones(32, Ts), data1=gf[32:36, rsl(s0, Ts)], initial=0.0,
                        op0=ALU.mult, op1=ALU.add), reads=["gf", "pv"], writes=["t1"])
                P.op("dve", lambda e: e.tensor_tensor(out=gi[:], in0=gi[:], in1=t1[:], op=ALU.subtract),
                     reads=["gi", "t1"], writes=["gi"])
                for (s0, Ts, is_s) in SEQS:
                    i0 = self.pv[0:4, PV_M0[l]:PV_M0[l] + 1] if is_s else 0.0
                    i1 = self.pv[32:36, PV_M0[l]:PV_M0[l] + 1] if is_s else 0.0
                    P.op("dve", lambda e, s0=s0, Ts=Ts, i0=i0: e.tensor_tensor_scan(
                        out=Agt[0:4, s0:s0 + Ts], data0=ones(0, Ts), data1=gi[0:4, s0:s0 + Ts], initial=i0,
                        op0=ALU.mult, op1=ALU.max), reads=["gi", "pv"], writes=["Agt"])
                    P.op("dve", lambda e, s0=s0, Ts=Ts, i1=i1: e.tensor_tensor_scan(
                        out=Agt[32:36, rsl(s0, Ts)], data0=ones(32, Ts), data1=gi[32:36, rsl(s0, Ts)], initial=i1,
                        op0=ALU.mult, op1=ALU.max), reads=["gi", "pv"], writes=["Agt"])
                P.op("dve", lambda e: e.scalar_tensor_tensor(out=t2[:], in0=t1[:], scalar=-1.0, in1=Agt[:],
                                                            op0=ALU.mult, op1=ALU.subtract),
                     reads=["t1", "Agt"], writes=["t2"])
                for seq in range(2):
                    s0, Ts, _ = SEQS[seq]
                    P.op("dve", lambda e, seq=seq, s0=s0, Ts=Ts: e.tensor_scalar(
                        out=mfin[0:4, seq:seq + 1], in0=t2[0:4, s0 + Ts - 1:s0 + Ts], scalar1=-1.0, scalar2=None,
                        op0=ALU.mult), reads=["t2"], writes=["mfin"])
                    P.op("dve", lambda e, seq=seq, s0=s0: e.tensor_scalar(
                        out=mfin[32:36, seq:seq + 1], in0=t2[32:36, s0:s0 + 1], scalar1=-1.0, scalar2=None,
                        op0=ALU.mult), reads=["t2"], writes=["mfin"])
                for seq in range(2):
                    P.dma("sp", self.newm[seq, l, 0, :].unsqueeze(1), mfin[0:4, seq:seq + 1], reads=["mfin"],
                          sem="d_om", is_output=True)
                    P.dma("sp", self.newm[seq, l, 1, :].unsqueeze(1), mfin[32:36, seq:seq + 1], reads=["mfin"],
                          sem="d_om", is_output=True)
                for t in range(TILES):
                    tok = slice(t * 128, (t + 1) * 128)
                    bk, rb = self.bank()
                    P.op("pe", lambda e, bk=bk, tok=tok: e.transpose(out=bk[:, 0:64], in_=gi[0:64, tok],
                                                                    identity=self.identf[0:64, 0:64]),
                         reads=["gi", "identf"], writes=[rb], inc=False)
                    P.op("pe", lambda e, bk=bk, tok=tok: e.transpose(out=bk[:, 64:128], in_=t2[0:64, tok],
                                                                    identity=self.identf[0:64, 0:64]),
                         reads=["t2", "identf"], writes=[rb])
                    P.op("dve", lambda e, bk=bk, t=t: e.tensor_copy(out=gtok[:, t, 0, :], in_=bk[:, 0:64]),
                         reads=[rb], writes=["gtok"])
                    P.op("act", lambda e, bk=bk, t=t: e.activation(out=gtok[:, t, 1, :], in_=bk[:, 64:128], func=AF.Exp),
                         reads=[rb], writes=["gtok"])
                    P.op("dve", lambda e, t=t: e.tensor_scalar(out=gtok[:, t, 2, :], in0=gtok[:, t, 0, :], scalar1=-1.0,
                                                              scalar2=None, op0=ALU.mult),
                         reads=["gtok"], writes=["gtok"])
                self.tap("Agt%d" % l, Agt[:], "Agt")
                self.tap("negmj%d" % l, t2[:], "t2")
                self.tap("agate%d" % l, gi[:], "gi")
                P.barrier()
            hsum = sb(ph, "hsum", [128, 12, 256], F32)
            m4 = contextlib.ExitStack()
            wt = [sb(m4, "wt%d" % i, [128, 512], BF16) for i in range(2)]
            ptm = [sb(m4, "ptm%d" % i, [128, 512], BF16) for i in range(2)]
            dtmp = sb(m4, "dtmp", [128, 128], F32)
            mask01 = sb(m4, "mask01", [128, 2, 128], BF16)
            P.op("dve", lambda e: e.tensor_scalar(out=mask01[:], in0=self.maskn[:], scalar1=1.0e-30, scalar2=1.0,
                                                  op0=ALU.mult, op1=ALU.add), reads=["maskn"], writes=["mask01"])
            wib = sb(m4, "wib", [128, 512], F32)
            pin = [sb(m4, "pin%d" % i, [128, 512], BF16) for i in range(2)]
            dn = sb(m4, "dn", [128, 4], F32)
            htmp = sb(m4, "htmp", [128, 4, 64], F32)
            kmtok = sb(m4, "kmtok", [128, 4, 256], BF16)
            nA = sb(m4, "nA", [128, 1], F32)
            wk = sb(m4, "wk", [128, 2], F32)
            kw = [sb(m4, "kw%d" % i, [128, 64], BF16) for i in range(2)]
            Cst = [sb(m4, "Cst", [64, 8, 65], F32)] * 2
            for t in range(4):
                tb_, rtb_ = self.bank()
                tbv = tb_[:].bitcast(BF16).rearrange("p (i c) -> p i c", c=128)
                for pr in range(2):
                    P.op("pe", lambda e, tbv=tbv, pr=pr, t=t: e.transpose(
                        out=tbv[:, pr, :], in_=qkmT[:, 2 + pr, t * 128:(t + 1) * 128], identity=self.identb[:]),
                        reads=["qkmT%d" % (2 + pr), "identb"], writes=[rtb_], inc=(pr == 1))
                P.op("act", lambda e, tbv=tbv, t=t: e.copy(out=kmtok[:, t, :].rearrange("p (a b) -> p a b", b=128),
                                                          in_=tbv[:, 0:2, :]),
                     reads=[rtb_], writes=["kmtok"])
            jobs = []
            for si, (s0, Ts, is_s) in enumerate(SEQS):
                for d in range(2):
                    for h in range(4):
                        nch = Ts // 128
                        J = dict(idx=len(jobs), si=si, s0=s0, Ts=Ts, is_s=is_s, d=d, h=h, nch=nch, tile0=s0 // 128,
                                 npc=max(1, Ts // 512), pw=min(512, Ts), lpb=min(nch, 4), nacc=(nch + 3) // 4,
                                 r=d * 32 + h, ri=d * 4 + h, hp=(h % 2) * 64, pr=h // 2)
                        pieces = []
                        for sc in range(nch):
                            l_lo, l_hi = (sc * 128, Ts) if d == 0 else (0, (sc + 1) * 128)
                            for pc in range(l_lo // 512, (l_hi + 511) // 512):
                                c0, c1 = max(l_lo, pc * 512), min(l_hi, (pc + 1) * 512)
                                pieces.append((sc, pc, c0, c1))
                        lastpv = {}
                        for (sc, pc, c0, c1) in pieces:
                            for lt in range(c0 // 128, c1 // 128):
                                lastpv[lt // J["lpb"]] = (sc, lt)
                        J["pieces"] = pieces
                        J["lastpv"] = lastpv
                        base = 2 if J["idx"] % 2 == 0 else 4
                        if is_s:
                            J["abk"] = [self.fbank(pc) for pc in range(J["npc"])]
                        else:
                            J["abk"] = [self.fbank(J["idx"] % 2)]
                        J["acc"] = [self.fbank(base + i) for i in range(J["nacc"])]
                        J["cb"] = self.fbank(base + 1)
                        jobs.append(J)
            pin4 = pin + [sb(m4, "pinx%d" % i, [128, 512], BF16) for i in range(2)]
            nA2 = [nA, sb(m4, "nAx", [128, 1], F32)]
            wk2 = [wk, sb(m4, "wkx", [128, 2], F32)]
            cnt = dict(k=0, pin=0)

            def begin(J):
                s0, Ts, d, h, nch, tile0 = J["s0"], J["Ts"], J["d"], J["h"], J["nch"], J["tile0"]
                npc, pw, lpb, r, ri, hp, pr = J["npc"], J["pw"], J["lpb"], J["r"], J["ri"], J["hp"], J["pr"]
                for pc in range(npc):
                    bk, rb = J["abk"][pc]
                    P.op("pe", lambda e, bk=bk, pc=pc, ri=ri, s0=s0, pw=pw: e.matmul(
                        bk[:, 0:pw], lhsT=self.sel[0:64, ri, :], rhs=Agt[0:64, s0 + pc * pw:s0 + (pc + 1) * pw],
                        start=True, stop=True), reads=["sel", "Agt"], writes=[rb])
                for (bk, rb) in J["acc"]:
                    P.op("pe", lambda e, bk=bk: e.matmul(bk[:], lhsT=self.zerob[:, 0:128], rhs=self.hT[:, 0, 0:512],
                                                        start=True, stop=False, skip_group_check=True),
                         reads=["zerob", "hT0"], writes=[rb])
                if J["is_s"]:
                    for pc in range(npc):
                        ab_, rab = J["abk"][pc]
                        m0r = self.pv[hp:hp + 64, PV_M0R[l] + ri:PV_M0R[l] + ri + 1]
                        P.op("act", lambda e, ab_=ab_, m0r=m0r, hp=hp, pw=pw: e.activation(
                            out=wib[hp:hp + 64, 0:pw], in_=ab_[hp:hp + 64, 0:pw], func=AF.Exp, bias=m0r, scale=-1.0),
                            reads=[rab, "pv"], writes=["wib"])
                        pn = pin4[cnt["pin"] % 4]
                        rpn = "pin%d" % (cnt["pin"] % 4)
                        cnt["pin"] += 1
                        P.op("dve", lambda e, pn=pn, hp=hp, pr=pr, s0=s0, pc=pc, pw=pw: e.tensor_tensor(
                            out=pn[hp:hp + 64, 0:pw], in0=qkmT[hp:hp + 64, pr, s0 + pc * 512:s0 + pc * 512 + pw],
                            in1=wib[hp:hp + 64, 0:pw], op=ALU.mult),
                            reads=["wib", "qkmT%d" % pr], writes=[rpn])
                        def init_mm(pc=pc, pn=pn, rpn=rpn):
                            for lt in range(pc * 4, pc * 4 + 4):
                                ab2, rab2 = J["acc"][lt // lpb]
                                col = (lt % lpb) * 65
                                P.op("pe", lambda e, ab2=ab2, col=col, pn=pn, lt=lt, hp=hp, d=d, pr=pr: e.matmul(
                                    ab2[:, col:col + 65], lhsT=pn[hp:hp + 64, (lt % 4) * 128:(lt % 4) * 128 + 128],
                                    rhs=C0[hp:hp + 64, d, pr, :], start=False, stop=False, skip_group_check=True),
                                    reads=[rpn, "C0_%d" % d], writes=[rab2], inc=(lt == pc * 4 + 3))
                        J.setdefault("deferred", []).append(init_mm)
                else:
                    si = J["si"]
                    colA = Ts - 1 if d == 0 else 0
                    ab_, rab = J["abk"][0]
                    nA_ = nA2[J["idx"] % 2]
                    rnA = "nA%d" % (J["idx"] % 2)
                    wk_ = wk2[J["idx"] % 2]
                    rwk = "wk%d" % (J["idx"] % 2)
                    P.op("act", lambda e, ab_=ab_, colA=colA, nA_=nA_: e.mul(
                        out=nA_[:, 0:1], in_=ab_[:, colA:colA + 1], mul=-1.0),
                        reads=[rab], writes=[rnA])
                    P.op("act", lambda e, tile0=tile0, nch=nch, r=r, nA_=nA_, wk_=wk_: e.activation(
                        out=wk_[:, 0:nch], in_=gtok[:, tile0:tile0 + nch, 0, r], func=AF.Exp, bias=nA_[:, 0:1], scale=1.0),
                        reads=[rnA, "gtok"], writes=[rwk])
                    cb, rcb = J["cb"]
                    for sc in range(nch):
                        kw_ = kw[sc % 2]
                        P.op("dve", lambda e, kw_=kw_, sc=sc, tile0=tile0, h=h, wk_=wk_: e.tensor_scalar(
                            out=kw_[:], in0=kmtok[:, tile0 + sc, h * 64:(h + 1) * 64], scalar1=wk_[:, sc:sc + 1],
                            scalar2=None, op0=ALU.mult), reads=["kmtok", rwk], writes=["kw%d" % (sc % 2)])

                    def final_state_mm():
                        for sc in range(nch):
                            kw_ = kw[sc % 2]
                            P.op("pe", lambda e, cb=cb, kw_=kw_, sc=sc, tile0=tile0, h=h, nch=nch: e.matmul(
                                cb[0:64, 0:65], lhsT=kw_[:], rhs=Vm[:, tile0 + sc, h, :], start=(sc == 0),
                                stop=(sc == nch - 1)), reads=["kw%d" % (sc % 2), "Vm%d" % (tile0 + sc)],
                                writes=[rcb], inc=(sc == nch - 1))
                        P.op("act", lambda e, cb=cb, si=si, ri=ri: e.copy(out=Cst[si][:, ri, :], in_=cb[0:64, 0:65]),
                             reads=[rcb], writes=["Cst"])
                        if d == 1 and h == 3:
                            P.dma("sp", self.newC[si, l].rearrange("d h k e -> k (d h) e"), Cst[si][:], reads=["Cst"],
                                  sem="d_oC", is_output=True)
                    J.setdefault("deferred", []).append(final_state_mm)

            def stage_a(J, piece):
                sc, pc, c0, c1 = piece
                s0, d, hp, pr, r = J["s0"], J["d"], J["hp"], J["pr"], J["r"]
                k = cnt["k"]
                cnt["k"] += 1
                n = c1 - c0
                stile = J["tile0"] + sc
                sbk, rsb = self.fbank(6 + k % 2)
                P.op("pe", lambda e, sbk=sbk, n=n, hp=hp, pr=pr, s0=s0, sc=sc, c0=c0, c1=c1: e.matmul(
                    sbk[:, 0:n], lhsT=qkmT[hp:hp + 64, 2 + pr, s0 + sc * 128:s0 + (sc + 1) * 128],
                    rhs=qkmT[hp:hp + 64, pr, s0 + c0:s0 + c1], start=True, stop=True),
                    reads=["qkmT%d" % (2 + pr), "qkmT%d" % pr], writes=[rsb])
                ab_, rab = J["abk"][pc]
                W = wt[k % 2]
                rW = "wt%d" % (k % 2)
                a_s = gtok[:, stile, 0, r:r + 1]
                dlo = sc * 128
                rngs = []
                if c0 <= dlo < c1:
                    if dlo > c0:
                        rngs.append((c0, dlo))
                    if dlo + 128 < c1:
                        rngs.append((dlo + 128, c1))
                    na_s = gtok[:, stile, 2, r:r + 1]
                    P.op("act", lambda e, ab_=ab_, dlo=dlo, pc=pc, na_s=na_s: e.activation(
                        out=dtmp[:], in_=ab_[:, dlo - pc * 512:dlo - pc * 512 + 128], func=AF.Relu, bias=na_s, scale=1.0),
                        reads=[rab, "gtok"], writes=["dtmp"])
                    P.op("act", lambda e, W=W, dlo=dlo, c0=c0: e.activation(
                        out=W[:, dlo - c0:dlo - c0 + 128], in_=dtmp[:], func=AF.Exp, scale=-1.0),
                        reads=["dtmp"], writes=[rW])
                else:
                    rngs.append((c0, c1))
                for (x0, x1) in rngs:
                    P.op("act", lambda e, W=W, x0=x0, x1=x1, c0=c0, pc=pc, ab_=ab_, a_s=a_s: e.activation(
                        out=W[:, x0 - c0:x1 - c0], in_=ab_[:, x0 - pc * 512:x1 - pc * 512], func=AF.Exp,
                        bias=a_s, scale=-1.0), reads=[rab, "gtok"], writes=[rW])
                pt = ptm[k % 2]
                rpt = "ptm%d" % (k % 2)
                P.op("dve", lambda e, pt=pt, sbk=sbk, W=W, n=n: e.tensor_tensor(
                    out=pt[:, 0:n], in0=sbk[:, 0:n], in1=W[:, 0:n], op=ALU.mult),
                    reads=[rsb, rW], writes=[rpt])
                if c0 <= dlo < c1:
                    P.op("dve", lambda e, pt=pt, dlo=dlo, c0=c0, d=d: e.tensor_tensor(
                        out=pt[:, dlo - c0:dlo - c0 + 128], in0=pt[:, dlo - c0:dlo - c0 + 128], in1=mask01[:, d, :],
                        op=ALU.mult), reads=[rpt, "mask01"], writes=[rpt])
                return (pt, rpt)

            def stage_b(J, piece, ptinfo, is_last_piece):
                sc, pc, c0, c1 = piece
                pt, rpt = ptinfo
                lpb, h = J["lpb"], J["h"]
                stile = J["tile0"] + sc
                lts = list(range(c0 // 128, c1 // 128))
                for lt in lts:
                    ab2, rab2 = J["acc"][lt // lpb]
                    col = (lt % lpb) * 65
                    last = J["lastpv"][lt // lpb] == (sc, lt)
                    P.op("pe", lambda e, ab2=ab2, col=col, pt=pt, lt=lt, c0=c0, stile=stile, h=h, last=last: e.matmul(
                        ab2[:, col:col + 65], lhsT=pt[:, lt * 128 - c0:lt * 128 - c0 + 128],
                        rhs=Vm[:, stile, h, :], start=False, stop=last, skip_group_check=True),
                        reads=[rpt, "Vm%d" % stile], writes=[rab2], inc=(is_last_piece and lt == lts[-1]))

            def end(J):
                d, h, nch, lpb, tile0, r = J["d"], J["h"], J["nch"], J["lpb"], J["tile0"], J["r"]
                for bi, (ab2, rab2) in enumerate(J["acc"]):
                    nl = min(lpb, nch - bi * lpb)
                    av = ab2[:, 0:nl * 65].rearrange("p (a b) -> p a b", b=65)
                    t0_ = tile0 + bi * lpb
                    P.op("dve", lambda e, av=av, nl=nl, t0_=t0_, r=r: e.tensor_tensor(
                        out=dn[:, 0:nl], in0=av[:, :, 64], in1=gtok[:, t0_:t0_ + nl, 1, r], op=ALU.max),
                        reads=[rab2, "gtok"], writes=["dn"])
                    P.op("dve", lambda e, av=av, nl=nl: e.scalar_tensor_tensor(
                        out=dn[:, 0:nl], in0=av[:, :, 64], scalar=-1.0, in1=dn[:, 0:nl], op0=ALU.mult, op1=ALU.max),
                        reads=[rab2, "dn"], writes=["dn"])
                    P.op("dve", lambda e, nl=nl: e.reciprocal(out=dn[:, 0:nl], in_=dn[:, 0:nl]),
                         reads=["dn"], writes=["dn"])
                    hs_ = hsum[:, t0_:t0_ + nl, h * 64:(h + 1) * 64]
                    dnb = dn[:, 0:nl].unsqueeze(2).to_broadcast([128, nl, 64])
                    if d == 0:
                        P.op("dve", lambda e, hs_=hs_, av=av, dnb=dnb: e.tensor_tensor(
                            out=hs_, in0=av[:, :, 0:64], in1=dnb, op=ALU.mult),
                            reads=[rab2, "dn"], writes=["hsum"])
                    else:
                        P.op("dve", lambda e, av=av, dnb=dnb, nl=nl: e.tensor_tensor(
                            out=htmp[:, 0:nl, :], in0=av[:, :, 0:64], in1=dnb, op=ALU.mult),
                            reads=[rab2, "dn"], writes=["htmp"])
                        P.op("dve", lambda e, hs_=hs_, nl=nl: e.tensor_tensor(
                            out=hs_, in0=hs_, in1=htmp[:, 0:nl, :], op=ALU.add),
                            reads=["htmp", "hsum"], writes=["hsum"])

            flat = []
            for J in jobs:
                for i, pc_ in enumerate(J["pieces"]):
                    flat.append((J, pc_, i == 0, i == len(J["pieces"]) - 1))
            prev = None
            for (J, pc_, first, last_) in flat:
                if first:
                    begin(J)
                    J["npiece"] = 0
                info = stage_a(J, pc_)
                J["npiece"] += 1
                if J["npiece"] == 2:
                    for f_ in J.get("deferred", []):
                        f_()
                    J["deferred"] = []
                if prev is not None:
                    pJ, ppc, pinfo, plast = prev
                    stage_b(pJ, ppc, pinfo, plast)
                    if plast:
                        end(pJ)
                prev = (J, pc_, info, last_)
            pJ, ppc, pinfo, plast = prev
            stage_b(pJ, ppc, pinfo, plast)
            end(pJ)
            self.tap("hsum%d" % l, hsum[:], "hsum")
            P.barrier()
            m4.close()
            sq2 = [sb(ph, "msq%d" % i, [128, 256], F32) for i in range(2)]
            s42 = [sb(ph, "ms4%d" % i, [128, 4], F32) for i in range(2)]
            hn2 = [sb(ph, "mhn%d" % i, [128, 256], F32) for i in range(2)]
            hmb = [sb(ph, "hmb%d" % i, [128, 256], BF16) for i in range(2)]

            def m5a(t):
                i2 = t % 2
                sq, s4, hn = sq2[i2], s42[i2], hn2[i2]
                rsq, rs4, rhn = "msq%d" % i2, "ms4%d" % i2, "mhn%d" % i2
                P.op("act", lambda e, t=t, sq=sq: e.activation(out=sq[:], in_=hsum[:, t, :], func=AF.Square),
                     reads=["hsum"], writes=[rsq])
                P.op("dve", lambda e, sq=sq, s4=s4: e.tensor_reduce(out=s4[:], in_=sq[:].rearrange("p (h d) -> p h d", d=64),
                                                                     axis=AX.X, op=ALU.add), reads=[rsq], writes=[rs4])
                P.op("act", lambda e, s4=s4: e.activation(out=s4[:], in_=s4[:], func=AF.Ln, scale=1.0 / 64, bias=eps),
                     reads=[rs4, "pv"], writes=[rs4])
                P.op("act", lambda e, s4=s4: e.activation(out=s4[:], in_=s4[:], func=AF.Exp, scale=-0.5),
                     reads=[rs4], writes=[rs4])
                P.op("dve", lambda e, t=t, hn=hn, s4=s4: e.tensor_tensor(
                    out=hn[:].rearrange("p (h d) -> p h d", d=64), in0=hsum[:, t, :].rearrange("p (h d) -> p h d", d=64),
                    in1=s4[:].unsqueeze(2).to_broadcast([128, 4, 64]), op=ALU.mult),
                    reads=["hsum", rs4], writes=[rhn])
                P.op("dve", lambda e, hn=hn: e.tensor_tensor(out=hn[:], in0=hn[:], in1=self.pv[:, PV_GM[l]:PV_GM[l] + 256], op=ALU.mult),
                     reads=[rhn, "pv"], writes=[rhn])
                hb = hmb[i2]
                rhb = "hmb%d" % i2
                P.op("dve", lambda e, hb=hb, t=t, hn=hn: e.tensor_tensor(out=hb[:], in0=hn[:], in1=sigom[:, t, :], op=ALU.mult),
                     reads=[rhn, "sigom%d" % t], writes=[rhb])

            def m5b(t):
                tok = slice(t * 128, (t + 1) * 128)
                hb = hmb[t % 2]
                rhb = "hmb%d" % (t % 2)
                tb_, rtb_ = self.bank()
                tbv = tb_[:].bitcast(BF16).rearrange("p (i c) -> p i c", c=128)
                for c2 in range(2):
                    P.op("pe", lambda e, tbv=tbv, c2=c2, hb=hb: e.transpose(
                        out=tbv[:, c2, :], in_=hb[:, c2 * 128:(c2 + 1) * 128], identity=self.identb[:]),
                        reads=[rhb, "identb"], writes=[rtb_], inc=(c2 == 1))
                P.op("act", lambda e, tbv=tbv, tok=tok: e.copy(out=hmT[:, :, tok], in_=tbv[:, 0:2, :]),
                     reads=[rtb_], writes=["hmT%d" % t])

            m5a(0)
            for t in range(TILES):
                if t + 1 < TILES:
                    m5a(t + 1)
                m5b(t)
            P.barrier()

    def attention_phase(self, l):
        P = self.P
        sb = self.sb
        attnT = self.attnT
        with contextlib.ExitStack() as ph:
            qT = sb(ph, "qT", [128, 4, T], BF16)
            kT2 = sb(ph, "kT2", [128, 4, T + 256], BF16)
            Vaug = sb(ph, "Vaug", [128, 14, 4, 65], BF16)
            atok = [sb(ph, "atok%d" % i, [128, 4, 512], BF16) for i in range(2)]
            rden = [sb(ph, "rden%d" % i, [128, 4], F32) for i in range(2)]
            wq_, rwq = self.wslot()
            wkv_, rwkv = self.wslot()
            wvq = wq_[:, :].rearrange("p (k c) -> p k c", c=512)
            wvkv = wkv_[:, :].rearrange("p (k c) -> p k c", c=512)
            P.dma("pool", wvq, self.w_in[l, :, 0:512].rearrange("(k p) c -> p k c", p=128), writes=[rwq], sem="d_" + rwq)
            P.dma("pool", wvkv, self.w_in[l, :, 512:1024].rearrange("(k p) c -> p k c", p=128), writes=[rwkv], sem="d_" + rwkv)
            self.norm(l, 1, barrier=False, use_pool=False)
            P.op("dve", lambda e: e.memset(Vaug[:, :, :, 64:65], 1.0), writes=["Vaug%d" % i for i in range(14)])
            with contextlib.ExitStack() as sub:
                sqqk = [sb(sub, "sqqk%d" % i, [128, 768], F32) for i in range(2)]
                qkn = [sb(sub, "qkn%d" % i, [128, 768], F32) for i in range(2)]
                qkr = [sb(sub, "qkr%d" % i, [128, 768], BF16) for i in range(2)]
                kdup = [sb(sub, "kdup%d" % i, [128, 4, 2, 64], BF16) for i in range(2)]
                ssq = [sb(sub, "ssq%d" % i, [128, 12], F32) for i in range(2)]
                vst = [sb(sub, "vst%d" % i, [128, 256], F32) for i in range(2)]
                rta = sb(sub, "rta", [128, 12, 2, 16], F32)
                rtb = sb(sub, "rtb", [128, 12, 2, 16], F32)
                kc = sb(sub, "kc", [128, 2, 256], F32)
                vcs = sb(sub, "vcs", [128, 2, 256], F32)
                P.dma("sp", kc[:], self.ck[l].rearrange("(c p) f -> p c f", p=128), writes=["kc"], sem="d_kc")
                P.dma("sp", vcs[:], self.cv[l].rearrange("(c p) f -> p c f", p=128), writes=["vcs"], sem="d_vc")
                for c in range(2):
                    P.op("act", lambda e, c=c: e.copy(out=Vaug[:, 4 + c, :, 0:64],
                                                      in_=vcs[:, c, :].rearrange("p (h d) -> p h d", d=64)),
                         reads=["vcs"], writes=["Vaug%d" % (4 + c)])
                eps = self.pv[:, PV_EPS:PV_EPS + 1]

                def ktranspose(kd, rkd, kcols):
                    tb_, rtb_ = self.bank()
                    tbv = tb_[:].bitcast(BF16).rearrange("p (i c) -> p i c", c=128)
                    for i in range(4):
                        P.op("pe", lambda e, tbv=tbv, i=i, kd=kd: e.transpose(
                            out=tbv[:, i, :], in_=kd[:, i, :, :].rearrange("p a d -> p (a d)"), identity=self.identb[:]),
                            reads=[rkd, "identb"], writes=[rtb_], inc=(i == 3))
                    P.op("dve", lambda e, tbv=tbv, kcols=kcols: e.tensor_copy(out=kT2[:, :, kcols], in_=tbv[:, 0:4, :]),
                         reads=[rtb_], writes=["kT2_%d" % (kcols.start // 128)])


                def stageA(t):
                    is_s = t >= 4
                    tok = slice(t * 128, (t + 1) * 128)
                    rh = "hT%d" % (t // 4)
                    bq, rq = self.bank()
                    bkv, rkv = self.bank()
                    for k in range(8):
                        P.op("pe", lambda e, bq=bq, k=k, tok=tok: e.matmul(
                            bq[:], lhsT=self.hT[:, k, tok], rhs=wvq[:, k, :], start=(k == 0), stop=(k == 7)),
                            reads=[rwq, rh], writes=[rq], inc=False)
                    for k in range(8):
                        P.op("pe", lambda e, bkv=bkv, k=k, tok=tok: e.matmul(
                            bkv[:], lhsT=self.hT[:, k, tok], rhs=wvkv[:, k, :], start=(k == 0), stop=(k == 7)),
                            reads=[rwkv, rh], writes=[rkv], inc=(k == 7))
                    vch = t if t < 4 else t + 2
                    i2 = t % 2
                    if not is_s:
                        seq = t // 2
                        r0 = (t % 2) * 128
                    sqt = sqqk[i2]
                    rsq = "sqqk%d" % i2
                    P.op("act", lambda e, sqt=sqt, bq=bq: e.activation(out=sqt[:, 0:512], in_=bq[:], func=AF.Square),
                         reads=[rq], writes=[rsq])
                    P.op("act", lambda e, sqt=sqt, bkv=bkv: e.activation(out=sqt[:, 512:768], in_=bkv[:, 0:256], func=AF.Square),
                         reads=[rkv], writes=[rsq])
                    ss = ssq[i2]
                    rss = "ssq%d" % i2
                    P.op("dve", lambda e, ss=ss, sqt=sqt: e.tensor_reduce(
                        out=ss[:], in_=sqt[:].rearrange("p (h d) -> p h d", d=64), axis=AX.X, op=ALU.add),
                        reads=[rsq], writes=[rss])
                    P.op("act", lambda e, ss=ss: e.activation(out=ss[:], in_=ss[:], func=AF.Ln, scale=1.0 / 64, bias=eps),
                         reads=[rss, "pv"], writes=[rss])
                    P.op("act", lambda e, ss=ss: e.activation(out=ss[:], in_=ss[:], func=AF.Exp, scale=-0.5),
                         reads=[rss], writes=[rss])
                    qn = qkn[i2]
                    rqn = "qkn%d" % i2
                    P.op("dve", lambda e, qn=qn, bq=bq, ss=ss: e.tensor_tensor(
                        out=qn[:, 0:512].rearrange("p (h d) -> p h d", d=64), in0=bq[:].rearrange("p (h d) -> p h d", d=64),
                        in1=ss[:, 0:8].unsqueeze(2).to_broadcast([128, 8, 64]), op=ALU.mult),
                        reads=[rq, rss], writes=[rqn])
                    P.op("dve", lambda e, qn=qn, bkv=bkv, ss=ss: e.tensor_tensor(
                        out=qn[:, 512:768].rearrange("p (h d) -> p h d", d=64),
                        in0=bkv[:, 0:256].rearrange("p (h d) -> p h d", d=64),
                        in1=ss[:, 8:12].unsqueeze(2).to_broadcast([128, 4, 64]), op=ALU.mult),
                        reads=[rkv, rss], writes=[rqn])
                    P.op("act", lambda e, vch=vch, bkv=bkv: e.copy(
                        out=Vaug[:, vch, :, 0:64], in_=bkv[:, 256:512].rearrange("p (h d) -> p h d", d=64)),
                        reads=[rkv], writes=["Vaug%d" % vch])
                    if not is_s:
                        P.op("act", lambda e, i2=i2, bkv=bkv: e.copy(out=vst[i2][:], in_=bkv[:, 256:512]),
                             reads=[rkv], writes=["vst%d" % i2])
                        P.dma("sp", self.newv[seq, l, r0:r0 + 128, :], vst[i2][:], reads=["vst%d" % i2],
                              sem="d_ov%d" % i2, is_output=True)
                    gq = self.pv[:, PV_GQK[l]:PV_GQK[l] + 64].unsqueeze(1).to_broadcast([128, 8, 64])
                    gk = self.pv[:, PV_GQK[l] + 64:PV_GQK[l] + 128].unsqueeze(1).to_broadcast([128, 4, 64])
                    P.op("dve", lambda e, qn=qn, gq=gq: e.tensor_tensor(
                        out=qn[:, 0:512].rearrange("p (h d) -> p h d", d=64), in0=qn[:, 0:512].rearrange("p (h d) -> p h d", d=64),
                        in1=gq, op=ALU.mult), reads=[rqn, "pv"], writes=[rqn])
                    P.op("dve", lambda e, qn=qn, gk=gk: e.tensor_tensor(
                        out=qn[:, 512:768].rearrange("p (h d) -> p h d", d=64), in0=qn[:, 512:768].rearrange("p (h d) -> p h d", d=64),
                        in1=gk, op=ALU.mult), reads=[rqn, "pv"], writes=[rqn])
                    qr = qkr[i2]
                    rqr = "qkr%d" % i2
                    if not is_s:
                        P.dma("sp", self.newk[seq, l, r0:r0 + 128, :], qn[:, 512:768], reads=[rqn],
                              sem="d_ok%d" % i2, is_output=True)
                        P.op("act", lambda e, qr=qr, qn=qn: e.copy(out=qr[:], in_=qn[:]), reads=[rqn], writes=[rqr])
                    else:
                        ts = t - 4
                        v5 = qn[:].rearrange("p (h a b f) -> p h a b f", a=2, b=2, f=16)
                        o5 = qr[:].rearrange("p (h a b f) -> p h a b f", a=2, b=2, f=16)
                        X1, X2 = v5[:, :, :, 0, :], v5[:, :, :, 1, :]
                        O1, O2 = o5[:, :, :, 0, :], o5[:, :, :, 1, :]
                        cosv = self.pv[:, PV_COS + ts * 32:PV_COS + ts * 32 + 32].rearrange(
                            "p (a f) -> p a f", f=16).unsqueeze(1).to_broadcast([128, 12, 2, 16])
                        sinv = self.pv[:, PV_SIN + ts * 32:PV_SIN + ts * 32 + 32].rearrange(
                            "p (a f) -> p a f", f=16).unsqueeze(1).to_broadcast([128, 12, 2, 16])
                        rr = ["rta", "rtb"]
                        P.op("dve", lambda e, X1=X1, cosv=cosv: e.tensor_tensor(out=rta[:], in0=X1, in1=cosv, op=ALU.mult),
                             reads=[rqn, "pv"], writes=["rta"])
                        P.op("dve", lambda e, X2=X2, sinv=sinv: e.tensor_tensor(out=rtb[:], in0=X2, in1=sinv, op=ALU.mult),
                             reads=[rqn, "pv"], writes=["rtb"])
                        P.op("dve", lambda e, O1=O1: e.tensor_tensor(out=O1, in0=rta[:], in1=rtb[:], op=ALU.subtract),
                             reads=rr, writes=[rqr])
                        P.op("dve", lambda e, X2=X2, cosv=cosv: e.tensor_tensor(out=rta[:], in0=X2, in1=cosv, op=ALU.mult),
                             reads=[rqn, "pv", rqr], writes=["rta"])
                        P.op("dve", lambda e, X1=X1, sinv=sinv: e.tensor_tensor(out=rtb[:], in0=X1, in1=sinv, op=ALU.mult),
                             reads=[rqn, "pv", rqr], writes=["rtb"])
                        P.op("dve", lambda e, O2=O2: e.tensor_tensor(out=O2, in0=rta[:], in1=rtb[:], op=ALU.add),
                             reads=rr, writes=[rqr])
                    kd = kdup[i2]
                    rkd = "kdup%d" % i2
                    P.op("dve", lambda e, kd=kd, qr=qr: e.tensor_copy(
                        out=kd[:], in_=qr[:, 512:768].rearrange("p (h d) -> p h d", d=64).unsqueeze(2).to_broadcast([128, 4, 2, 64])),
                        reads=[rqr], writes=[rkd])
                    return (tok, qr, rqr, kd, rkd)

                def stageB(t, st_):
                    tok, qr, rqr, kd, rkd = st_
                    tb_, rtb_ = self.bank()
                    tbv = tb_[:].bitcast(BF16).rearrange("p (i c) -> p i c", c=128)
                    for i in range(4):
                        P.op("pe", lambda e, tbv=tbv, i=i, qr=qr: e.transpose(
                            out=tbv[:, i, :], in_=qr[:, i * 128:(i + 1) * 128], identity=self.identb[:]),
                            reads=[rqr, "identb"], writes=[rtb_], inc=(i == 3))
                    P.op("act", lambda e, tbv=tbv, tok=tok: e.copy(out=qT[:, :, tok], in_=tbv[:, 0:4, :]),
                         reads=[rtb_], writes=["qT%d" % t])
                    kcol0 = t * 128 if t < 4 else 768 + (t - 4) * 128
                    ktranspose(kd, rkd, slice(kcol0, kcol0 + 128))

                stA = {0: stageA(0)}
                for t in range(TILES):
                    if t + 1 < TILES:
                        stA[t + 1] = stageA(t + 1)
                    stageB(t, stA[t])
                for c in range(2):
                    kd = kdup[c % 2]
                    rkd = "kdup%d" % (c % 2)
                    P.op("dve", lambda e, kd=kd, c=c: e.tensor_copy(
                        out=kd[:], in_=kc[:, c, :].rearrange("p (h d) -> p h d", d=64).unsqueeze(2).to_broadcast([128, 4, 2, 64])),
                        reads=["kc"], writes=[rkd])
                    ktranspose(kd, rkd, slice(512 + c * 128, 512 + (c + 1) * 128))
                P.barrier()
            self.tap("qT%d" % l, qT[:], "qT0", BF16)
            self.tap("kT2%d" % l, kT2[:], "kT2_0", BF16)
            self.tap("Vaug%d" % l, Vaug[:], "Vaug0", BF16)
            if self.stop_after == "qkv_%d" % l:
                return
            PT = [sb(ph, "PT%d" % i, [128, 10, 512], BF16) for i in range(2)]
            state = dict(it=0, lh=0)

            iters = []
            for (q0, Tq, k0, nch, vc0, LB) in [(0, 256, 0, 2, 0, 256), (256, 256, 256, 2, 2, 256)]:
                for lh in range(Tq // LB):
                    grp = state["lh"]
                    state["lh"] += 1
                    for h in range(8):
                        iters.append(dict(q0=q0, k0=k0, nch=nch, vc0=vc0, LB=LB, lh=lh, h=h, grp=grp, idx=len(iters), pair=False))
            for lq in range(4):
                grp = state["lh"]
                state["lh"] += 1
                for g_ in range(4):
                    iters.append(dict(q0=512, k0=512, nch=10, vc0=4, LB=256, lh=lq, h=2 * g_ + 1, g=g_, grp=grp,
                                      idx=len(iters), pair=True))
            sbank_i = [0]

            def ptview(pt):
                return pt[:].rearrange("p a b -> p (a b)").rearrange("p (hh a c) -> p hh a c", hh=2, a=10)

            def stage_a_pair(itr):
                q0, k0, lq, g = itr["q0"], itr["k0"], itr["lh"], itr["g"]
                pi = itr["idx"] % 2
                ptv = ptview(PT[pi])
                rpt = "PT%d" % pi
                qreads = ["qT%d" % tt for tt in range((q0 + lq * 256) // 128, (q0 + (lq + 1) * 256) // 128)]
                for sc0 in range(0, 10, 2):
                    pp = sbank_i[0] % 3
                    sbank_i[0] += 1
                    big = self.bigbanks[pp]
                    bks = [self.fbank(2 * pp), self.fbank(2 * pp + 1)]
                    for u in range(2):
                        sc = sc0 + u
                        for hh in range(2):
                            bk, rb = bks[hh]
                            pb = hh * 64
                            P.op("pe", lambda e, bk=bk, u=u, sc=sc, g=g, pb=pb, lq=lq, k0=k0, q0=q0: e.matmul(
                                bk[:, u * 256:(u + 1) * 256],
                                lhsT=kT2[pb:pb + 64, g, k0 + sc * 128:k0 + (sc + 1) * 128],
                                rhs=qT[pb:pb + 64, g, q0 + lq * 256:q0 + (lq + 1) * 256], start=True, stop=True),
                                reads=qreads + ["kT2_%d" % ((k0 + sc * 128) // 128)], writes=[rb], inc=(u == 1))
                    P.op("act", lambda e, ptv=ptv, big=big, sc0=sc0: e.activation(
                        out=ptv[:, :, sc0:sc0 + 2, :], in_=big[:, 0:1024].rearrange("p (hh u c) -> p hh u c", hh=2, u=2),
                        func=AF.Exp, scale=0.125), reads=[bks[0][1], bks[1][1]], writes=[rpt])

            def stage_b_pair(itr):
                q0, vc0, lq, g = itr["q0"], itr["vc0"], itr["lh"], itr["g"]
                pi = itr["idx"] % 2
                ptv = ptview(PT[pi])
                rpt = "PT%d" % pi
                ai = itr["grp"] % 2
                ab = atok[ai]
                rab = "atok%d" % ai
                obk, rob = self.fbank(6 + itr["idx"] % 2)
                for hh in range(2):
                    for lt in range(2):
                        for sc in range(10):
                            P.op("pe", lambda e, obk=obk, hh=hh, lt=lt, sc=sc, ptv=ptv, g=g, vc0=vc0: e.matmul(
                                obk[:, (hh * 2 + lt) * 65:(hh * 2 + lt + 1) * 65], lhsT=ptv[:, hh, sc, lt * 128:(lt + 1) * 128],
                                rhs=Vaug[:, vc0 + sc, g, :], start=(sc == 0), stop=(sc == 9)),
                                reads=[rpt, "Vaug%d" % (vc0 + sc)], writes=[rob], inc=(hh == 1 and lt == 1 and sc == 9))
                ov = obk[:, 0:4 * 65].rearrange("p (a b) -> p a b", b=65)
                rd = rden[g % 2]
                rrd = "rden%d" % (g % 2)
                P.op("dve", lambda e, rd=rd, ov=ov: e.reciprocal(out=rd[:, 0:4], in_=ov[:, :, 64]), reads=[rob], writes=[rrd])
                for hh in range(2):
                    h = 2 * g + hh
                    P.op("dve", lambda e, rd=rd, ov=ov, ab=ab, h=h, hh=hh: e.tensor_tensor(
                        out=ab[:, 0:2, h * 64:(h + 1) * 64], in0=ov[:, hh * 2:hh * 2 + 2, 0:64],
                        in1=rd[:, hh * 2:hh * 2 + 2].unsqueeze(2).to_broadcast([128, 2, 64]), op=ALU.mult),
                        reads=[rob, rrd], writes=[rab])
                if g == 3:
                    for i in range(2):
                        tile = (q0 + lq * 256) // 128 + i
                        tb_, rtb_ = self.fbank(6 + (itr["idx"] + 1) % 2)
                        tbv = tb_[:].bitcast(BF16).rearrange("p (i c) -> p i c", c=128)
                        for c4 in range(4):
                            P.op("pe", lambda e, tbv=tbv, c4=c4, ab=ab, i=i: e.transpose(
                                out=tbv[:, c4, :], in_=ab[:, i, c4 * 128:(c4 + 1) * 128], identity=self.identb[:]),
                                reads=[rab, "identb"], writes=[rtb_], inc=(c4 == 3))
                        P.op("act", lambda e, tbv=tbv, tile=tile: e.copy(
                            out=attnT[:, :, tile * 128:(tile + 1) * 128], in_=tbv[:, 0:4, :]),
                            reads=[rtb_], writes=["attnT%d" % tile])

            def stage_a(itr):
                if itr["pair"]:
                    return stage_a_pair(itr)
                q0, k0, nch, LB, lh, h = itr["q0"], itr["k0"], itr["nch"], itr["LB"], itr["lh"], itr["h"]
                per = 512 // LB
                g = h // 2
                pb = (h % 2) * 64
                pi = itr["idx"] % 2
                pt = PT[pi]
                rpt = "PT%d" % pi
                for sc0 in range(0, nch, per):
                    sbk, rsb = self.fbank(sbank_i[0] % 6)
                    sbank_i[0] += 1
                    for u in range(per):
                        sc = sc0 + u
                        P.op("pe", lambda e, sbk=sbk, u=u, sc=sc, g=g, pb=pb, lh=lh, LB=LB, k0=k0, q0=q0: e.matmul(
                            sbk[:, u * LB:(u + 1) * LB],
                            lhsT=kT2[pb:pb + 64, g, k0 + sc * 128:k0 + (sc + 1) * 128],
                            rhs=qT[pb:pb + 64, g, q0 + lh * LB:q0 + (lh + 1) * LB], start=True, stop=True),
                            reads=["qT%d" % tt for tt in range((q0 + lh * LB) // 128, (q0 + (lh + 1) * LB) // 128)]
                            + ["kT2_%d" % ((k0 + sc * 128) // 128)], writes=[rsb], inc=(u == per - 1))
                    P.op("act", lambda e, pt=pt, sbk=sbk, sc0=sc0, per=per, LB=LB: e.activation(
                        out=pt[:, sc0:sc0 + per, 0:LB], in_=sbk[:, 0:per * LB].rearrange("p (a b) -> p a b", b=LB),
                        func=AF.Exp, scale=0.125),
                        reads=[rsb], writes=[rpt])

            def stage_b(itr):
                if itr["pair"]:
                    return stage_b_pair(itr)
                q0, nch, vc0, LB, lh, h = itr["q0"], itr["nch"], itr["vc0"], itr["LB"], itr["lh"], itr["h"]
                ntl = LB // 128
                g = h // 2
                pi = itr["idx"] % 2
                pt = PT[pi]
                rpt = "PT%d" % pi
                ai = itr["grp"] % 2
                ab = atok[ai]
                rab = "atok%d" % ai
                obk, rob = self.fbank(6 + itr["idx"] % 2)
                for lt in range(ntl):
                    for sc in range(nch):
                        P.op("pe", lambda e, obk=obk, lt=lt, sc=sc, pt=pt, g=g, vc0=vc0, nch=nch: e.matmul(
                            obk[:, lt * 65:(lt + 1) * 65], lhsT=pt[:, sc, lt * 128:(lt + 1) * 128],
                            rhs=Vaug[:, vc0 + sc, g, :], start=(sc == 0), stop=(sc == nch - 1)),
                            reads=[rpt, "Vaug%d" % (vc0 + sc)], writes=[rob], inc=(lt == ntl - 1 and sc == nch - 1))
                ov = obk[:, 0:ntl * 65].rearrange("p (a b) -> p a b", b=65)
                rd = rden[h % 2]
                rrd = "rden%d" % (h % 2)
                P.op("dve", lambda e, rd=rd, ov=ov, ntl=ntl: e.reciprocal(out=rd[:, 0:ntl], in_=ov[:, :, 64]),
                     reads=[rob], writes=[rrd])
                P.op("dve", lambda e, rd=rd, ov=ov, ab=ab, h=h, ntl=ntl: e.tensor_tensor(
                    out=ab[:, 0:ntl, h * 64:(h + 1) * 64], in0=ov[:, :, 0:64],
                    in1=rd[:, 0:ntl].unsqueeze(2).to_broadcast([128, ntl, 64]), op=ALU.mult),
                    reads=[rob, rrd], writes=[rab])
                if h == 7:
                    for i in range(ntl):
                        tile = (q0 + lh * LB) // 128 + i
                        tb_, rtb_ = self.fbank(6 + (itr["idx"] + 1) % 2)
                        tbv = tb_[:].bitcast(BF16).rearrange("p (i c) -> p i c", c=128)
                        for c4 in range(4):
                            P.op("pe", lambda e, tbv=tbv, c4=c4, ab=ab, i=i: e.transpose(
                                out=tbv[:, c4, :], in_=ab[:, i, c4 * 128:(c4 + 1) * 128], identity=self.identb[:]),
                                reads=[rab, "identb"], writes=[rtb_], inc=(c4 == 3))
                        P.op("act", lambda e, tbv=tbv, tile=tile: e.copy(
                            out=attnT[:, :, tile * 128:(tile + 1) * 128], in_=tbv[:, 0:4, :]),
                            reads=[rtb_], writes=["attnT%d" % tile])

            stage_a(iters[0])
            todo = []
            if l == 0:
                todo += [(0, s_) for s_ in range(4, 12)]
            if l + 1 < DEPTH:
                todo += [(l + 1, s_) for s_ in range(12)]
            for i in range(len(iters)):
                if i + 1 < len(iters):
                    stage_a(iters[i + 1])
                stage_b(iters[i])
                if todo and iters[i]["h"] != 7 and (i % 2 == 1 or len(todo) > 12):
                    ll, s_ = todo.pop(0)
                    self.ada(ll, [s_], fixed_bank=7)
            for (ll, s_) in todo:
                self.ada(ll, [s_], fixed_bank=7)
            if l == 0:
                self.ada_derive2(0)
            if l + 1 < DEPTH:
                self.ada_derive(l + 1)
            P.barrier()

    def norm(self, l, which, barrier=True, use_pool=False):
        P = self.P
        sname, shname = ("s1", "shift1") if which == 1 else ("s2", "shift2")
        with contextlib.ExitStack() as ph:
            sq = [self.sb(ph, "nsq%d" % i, [128, 512], BF16) for i in range(2)]
            tmp = [self.sb(ph, "ntmp%d" % i, [128, 512], F32) for i in range(2)]
            rs = [self.sb(ph, "nrs%d" % i, [128, 512], F32) for i in range(3)]
            bks = []
            for nt in range(NT):
                tok = slice(nt * 512, (nt + 1) * 512)
                rx = "xT%d" % nt
                bk, rb = self.bank()
                bks.append((bk, rb))
                for k in range(8):
                    s_ = sq[k % 2]
                    rs_ = "nsq%d" % (k % 2)
                    P.op("act", lambda e, s_=s_, k=k, tok=tok: e.activation(out=s_[:], in_=self.xT[:, k, tok], func=AF.Square),
                         reads=[rx], writes=[rs_])
                    P.op("pe", lambda e, bk=bk, s_=s_, k=k: e.matmul(bk[:], lhsT=self.onesb[:], rhs=s_[:],
                                                                    start=(k == 0), stop=(k == 7)),
                         reads=[rs_, "onesb"], writes=[rb])
            for nt in range(NT):
                j = 0 if nt == 0 else 1
                tok = slice(nt * 512, (nt + 1) * 512)
                rx = "xT%d" % nt
                bk, rb = bks[nt]
                r_ = rs[nt]
                rr_ = "nrs%d" % nt
                P.op("act", lambda e, bk=bk, r_=r_: e.activation(out=r_[:], in_=bk[:], func=AF.Ln, scale=1.0 / D,
                                                                bias=self.pv[:, PV_EPS:PV_EPS + 1]),
                     reads=[rb, "pv"], writes=[rr_])
                P.op("act", lambda e, r_=r_: e.activation(out=r_[:], in_=r_[:], func=AF.Exp, scale=-0.5),
                     reads=[rr_], writes=[rr_])
                for k in range(8):
                    t_ = tmp[k % 2]
                    rt_ = "ntmp%d" % (k % 2)
                    P.op("pool" if (use_pool and k % 2 == 1) else "dve",
                         lambda e, t_=t_, k=k, tok=tok, r_=r_: e.tensor_tensor(out=t_[:], in0=self.xT[:, k, tok], in1=r_[:], op=ALU.mult),
                         reads=[rx, rr_], writes=[rt_])
                    P.op("act", lambda e, t_=t_, k=k, tok=tok, j=j: e.activation(
                        out=self.hT[:, k, tok], in_=t_[:], func=AF.Identity,
                        scale=self.mod(l, j, sname, k), bias=self.mod(l, j, shname, k)),
                        reads=[rt_, "modd%d_%d" % (l, which - 1), "modT%d" % l], writes=["hT%d" % nt])
            if barrier:
                P.barrier()

    def final_out(self):
        P = self.P
        with contextlib.ExitStack() as ph:
            yst = [self.sb(ph, "yst%d" % i, [128, D], F32) for i in range(2)]
            for t in range(TILES):
                ys = yst[t % 2]
                ry = "yst%d" % (t % 2)
                for half in range(2):
                    bk, rb = self.bank()
                    for kk in range(4):
                        c = half * 4 + kk
                        P.op("pe", lambda e, bk=bk, kk=kk, c=c, t=t: e.transpose(
                            out=bk[:, kk * 128:(kk + 1) * 128], in_=self.xT[:, c, t * 128:(t + 1) * 128], identity=self.identf[:]),
                            reads=["xT%d" % (t // 4), "identf"], writes=[rb], inc=(kk == 3))
                    if half == 0:
                        P.op("act", lambda e, bk=bk, ys=ys: e.copy(out=ys[:, 0:512], in_=bk[:]), reads=[rb], writes=[ry])
                    else:
                        P.op("dve", lambda e, bk=bk, ys=ys: e.tensor_copy(out=ys[:, 512:1024], in_=bk[:]), reads=[rb], writes=[ry])
                P.dma("sp", self.y[t * 128:(t + 1) * 128, :], ys[:], reads=[ry], sem="d_" + ry, is_output=True)
            P.barrier()


def _consts():
    ident = np.eye(128, dtype=np.float32)
    s = np.arange(128)[:, None]
    l_ = np.arange(128)[None, :]
    NEG = -1.0e30
    mask = np.zeros((128, 2, 128), np.float32)
    mask[:, 0, :] = np.where(s <= l_, 0.0, NEG)
    mask[:, 1, :] = np.where(s >= l_, 0.0, NEG)
    sel = np.zeros((64, 8, 128), np.float32)
    for d in range(2):
        for h in range(4):
            sel[d * 32 + h, d * 4 + h, :] = 1.0
    c = np.arange(64)
    ang = 2.0 * np.pi * np.outer(c, c) / 64.0
    bd = np.zeros((128, 256), np.float64)
    for g in range(2):
        bd[g * 64:(g + 1) * 64, g * 64:(g + 1) * 64] = np.cos(ang) / 8.0
        bd[g * 64:(g + 1) * 64, 128 + g * 64:128 + (g + 1) * 64] = np.sin(ang) / 8.0
    def dft(n):
        t = np.arange(n)
        a = 2.0 * np.pi * ((np.outer(t, t)) % n) / n
        sc = 1.0 / np.sqrt(n)
        return np.stack([np.cos(a) * sc, -np.sin(a) * sc]).astype(np.float32)
    return dict(c_ident=ident, c_mask=mask, c_sel=sel, c_bd=bd.astype(np.float32),
                c_dft1k=dft(1024), c_dft256=dft(256))


def _chunked(v):
    return np.ascontiguousarray(v.reshape(-1, 128).T)


def _pvec(core, inp):
    pv = np.zeros((128, NP), np.float32)
    cc = np.stack([_chunked(inp["c_ctx"]), _chunked(inp["c"][core])], axis=-1)
    pv[:, PV_C:PV_C + 16] = cc.reshape(128, 16)
    for l in range(DEPTH):
        pv[:, PV_BADA[l]:PV_BADA[l] + 48] = _chunked(inp["b_ada"][l])
        pv[:, PV_N1G[l]:PV_N1G[l] + 8] = _chunked(inp["norm1_g"][l])
        pv[:, PV_N2G[l]:PV_N2G[l] + 8] = _chunked(inp["norm2_g"][l])
        cw = inp["m_conv_w"][l]
        pv[:, PV_CONV[l]:PV_CONV[l] + 12] = np.stack([_chunked(cw[j]) for j in range(3)], axis=-1).reshape(128, 12)
        gb = inp["m_gate_b"][l]
        pv[0:4, PV_GB[l]] = gb[0:4]
        pv[32:36, PV_GB[l]] = gb[8:12]
        pv[0:4, PV_GB[l] + 1] = gb[4:8]
        pv[32:36, PV_GB[l] + 1] = gb[12:16]
        sm = inp["state_m"][core, l]
        pv[0:4, PV_M0[l]] = sm[0]
        pv[32:36, PV_M0[l]] = sm[1]
        pv[:, PV_M0R[l]:PV_M0R[l] + 8] = sm.reshape(1, 8)
        pv[:, PV_GQK[l]:PV_GQK[l] + 64] = inp["q_norm_g"][l][None, :]
        pv[:, PV_GQK[l] + 64:PV_GQK[l] + 128] = inp["k_norm_g"][l][None, :]
        pv[:, PV_GM[l]:PV_GM[l] + 256] = inp["m_norm_g"][l][None, :]
    pos = np.arange(1024)
    row = (pos // 64).astype(np.float64)
    col = (pos % 64).astype(np.float64)
    inv = 1.0 / (10000.0 ** (np.arange(16, dtype=np.float64) / 16.0))
    ang = np.concatenate([row[:, None] * inv[None, :], col[:, None] * inv[None, :]], axis=1)
    cos = np.cos(ang).reshape(8, 128, 32).transpose(1, 0, 2).reshape(128, 256)
    sin = np.sin(ang).reshape(8, 128, 32).transpose(1, 0, 2).reshape(128, 256)
    pv[:, PV_COS:PV_COS + 256] = cos
    pv[:, PV_SIN:PV_SIN + 256] = sin
    pv[:, PV_ONE] = 1.0
    pv[:, PV_EPS] = EPS
    return pv


def make_in_maps(inp):
    inp = {k: np.asarray(v) for k, v in inp.items()}
    consts = _consts()
    shared = dict(w_ada=inp["w_ada"], w_in=inp["w_in"], w_pa=inp["w_proj_attn"], w_pm=inp["w_proj_mlstm"],
                  w_pf=inp["w_proj_fourier"], w_out=inp["w_out"], w_f1=inp["w_ffn_in"], w_f2=inp["w_ffn_out"])
    shared = {k: np.ascontiguousarray(v, dtype=np.float32) for k, v in shared.items()}
    maps = []
    for c in range(NCORES):
        xin = np.concatenate([inp["x_prompt"][2 * c], inp["x_prompt"][2 * c + 1], inp["x_sample"][c]], axis=0)
        stC = np.concatenate([inp["state_C"][c], inp["state_n"][c][..., None]], axis=-1)
        m = dict(xin=np.ascontiguousarray(xin, dtype=np.float32), pvec=_pvec(c, inp),
                 ck=np.ascontiguousarray(inp["cache_k"][c].reshape(DEPTH, 256, 256)),
                 cv=np.ascontiguousarray(inp["cache_v"][c].reshape(DEPTH, 256, 256)),
                 stC=np.ascontiguousarray(stC, dtype=np.float32))
        m.update(shared)
        m.update(consts)
        maps.append(m)
    return maps


_CACHE = {}


def kernel(**inputs):
    if "nc" not in _CACHE:
        _CACHE["nc"] = Builder().build()
    nc = _CACHE["nc"]
    maps = make_in_maps(inputs)
    res = run_bass_kernel_spmd(nc, maps, core_ids=list(range(NCORES)))
    R = res.results
    y_prompt = np.zeros((16, 256, D), np.float32)
    y_sample = np.zeros((8, 1024, D), np.float32)
    new_k = np.zeros((16, DEPTH, 256, 4, 64), np.float32)
    new_v = np.zeros((16, DEPTH, 256, 4, 64), np.float32)
    new_C = np.zeros((16, DEPTH, 2, 4, 64, 64), np.float32)
    new_n = np.zeros((16, DEPTH, 2, 4, 64), np.float32)
    new_m = np.zeros((16, DEPTH, 2, 4), np.float32)
    for c in range(NCORES):
        r = R[c]
        y_prompt[2 * c] = r["y"][0:256]
        y_prompt[2 * c + 1] = r["y"][256:512]
        y_sample[c] = r["y"][512:]
        for s in range(2):
            new_k[2 * c + s] = r["newk"][s].reshape(DEPTH, 256, 4, 64)
            new_v[2 * c + s] = r["newv"][s].reshape(DEPTH, 256, 4, 64)
            new_C[2 * c + s] = r["newC"][s][..., 0:64]
            new_n[2 * c + s] = r["newC"][s][..., 64]
            new_m[2 * c + s] = r["newm"][s]
    return (y_prompt, y_sample, new_k, new_v, new_C, new_n, new_m)
```
